# Optimizing a Trainium2 kernel written in Bass

```python
import math
import jax, jax.numpy as jnp
from jax import lax
import numpy as np

D_MODEL = 2048
BATCH = 1
SEQ = 8192
DEPTH = 4

GRID_W = 64
Q_BLOCK = 128
HEAD_DIM = 128
N_BRANCH = 4
BRANCH_W = D_MODEL // 4
DA_QK = HEAD_DIM // 2
DA_V = HEAD_DIM
HA = BRANCH_W // DA_V
HB = BRANCH_W // HEAD_DIM
NB_ROWS_MAX = 8
NB_COLS = 16
HC = BRANCH_W // HEAD_DIM
KVC = HC // 2
AXIAL_THETA = 10000.0
DIL_WINDOWS = (128, 512, 2048)
DIL_RATES = (1, 4, 16)
N_DIL = 3
HD_PER = BRANCH_W // HEAD_DIM
HD = N_DIL * HD_PER
DIL_SIDE = 64
ROPE_THETA = 500000.0
ROPE_FRAC = 4
DN_ALPHA = (2 * DEPTH) ** 0.25
DN_BETA = (8 * DEPTH) ** -0.25
LN_EPS = 1e-5
RMS_EPS = 1e-6
NEG_INF = -1e30

IN_SIZES = (
    HA * 2 * DA_QK, HA * 2 * DA_QK, HA * DA_V, BRANCH_W,
    HB * HEAD_DIM, HB * HEAD_DIM, HB * HEAD_DIM, BRANCH_W,
    HC * HEAD_DIM, KVC * HEAD_DIM, KVC * HEAD_DIM, BRANCH_W,
    HD * HEAD_DIM, HD * HEAD_DIM, HD * HEAD_DIM, BRANCH_W,
    N_BRANCH * D_MODEL,
)
D_IN = sum(IN_SIZES)

kernel_name = 'hybrid_gated_multi_mixer_encoder'


def _split(h, sizes):
    outs, start = [], 0
    for n in sizes:
        outs.append(h[..., start:start + n])
        start += n
    return outs


def _layer_norm(x, g, b):
    xf = x.astype(jnp.float32)
    mu = jnp.mean(xf, -1, keepdims=True)
    var = jnp.mean(jnp.square(xf - mu), -1, keepdims=True)
    y = (xf - mu) * lax.rsqrt(var + LN_EPS)
    return (y * g.astype(jnp.float32) + b.astype(jnp.float32)).astype(x.dtype)


def _rms_norm(x, g):
    xf = x.astype(jnp.float32)
    y = xf * lax.rsqrt(jnp.mean(xf * xf, -1, keepdims=True) + RMS_EPS)
    return (y * g.astype(jnp.float32)).astype(x.dtype)


def _rope(x, pos, theta):
    dim = x.shape[-1]
    half = dim // 2
    inv = jnp.power(jnp.float32(theta), -jnp.arange(half, dtype=jnp.float32) * 2.0 / dim)
    ang = pos.astype(jnp.float32)[:, None] * inv[None, :]
    cos = jnp.cos(ang)[None, :, None, :].astype(x.dtype)
    sin = jnp.sin(ang)[None, :, None, :].astype(x.dtype)
    x1, x2 = x[..., :half], x[..., half:]
    return jnp.concatenate([x1 * cos - x2 * sin, x2 * cos + x1 * sin], -1)


def _partial_rope(x, pos):
    r = x.shape[-1] // ROPE_FRAC
    return jnp.concatenate([_rope(x[..., :r], pos, ROPE_THETA), x[..., r:]], -1)


def _to_blocks(t):
    b, s = t.shape[:2]
    t = t.reshape((b, s // Q_BLOCK, Q_BLOCK) + t.shape[2:])
    return jnp.moveaxis(t, 1, 0)


def _from_blocks(t):
    t = jnp.moveaxis(t, 0, 1)
    return t.reshape((t.shape[0], t.shape[1] * t.shape[2]) + t.shape[3:])


def _diff_attention(q, k, v, lam_params, subln_g, lam_init, pos):
    b, s = q.shape[:2]
    q = _partial_rope(q.reshape(b, s, HA * 2, DA_QK), pos).reshape(b, s, HA, 2, DA_QK) * (DA_QK ** -0.5)
    k = _partial_rope(k.reshape(b, s, HA * 2, DA_QK), pos).reshape(b, s, HA, 2, DA_QK)
    lp = lam_params.astype(jnp.float32)
    lam = jnp.exp(jnp.sum(lp[0] * lp[1])) - jnp.exp(jnp.sum(lp[2] * lp[3])) + lam_init

    def block(qb):
        sc = jnp.einsum('bqhcd,bshcd->bhcqs', qb, k).astype(jnp.float32)
        p = jax.nn.softmax(sc, axis=-1)
        a = p[:, :, 0] - lam * p[:, :, 1]
        return jnp.einsum('bhqs,bshd->bqhd', a.astype(v.dtype), v)

    o = _from_blocks(lax.map(block, _to_blocks(q)))
    o = _rms_norm(o, subln_g) * (1.0 - lam_init)
    return o.reshape(b, s, HA * DA_V)


def _neighbourhood_attention(q, k, v, rpb):
    b, s = q.shape[:2]
    rows = s // GRID_W
    kr = min(NB_ROWS_MAX, rows)
    kc = NB_COLS
    qg = jnp.moveaxis((q * HEAD_DIM ** -0.5).reshape(b, rows, GRID_W, HB, HEAD_DIM), 1, 0)
    kg = k.reshape(b, rows, GRID_W, HB, HEAD_DIM)
    vg = v.reshape(b, rows, GRID_W, HB, HEAD_DIM)
    cols = jnp.arange(GRID_W)
    col_idx = jnp.clip(cols - kc // 2, 0, GRID_W - kc)[:, None] + jnp.arange(kc)[None, :]
    dc = col_idx - cols[:, None] + (NB_COLS - 1)

    def row_block(args):
        r, qr = args
        r0 = jnp.clip(r - kr // 2, 0, rows - kr)
        k_nb = lax.dynamic_slice_in_dim(kg, r0, kr, axis=1)[:, :, col_idx]
        v_nb = lax.dynamic_slice_in_dim(vg, r0, kr, axis=1)[:, :, col_idx]
        dr = r0 + jnp.arange(kr) - r + (NB_ROWS_MAX - 1)
        bias = rpb[:, dr[None, :, None], dc[:, None, :]]
        sc = jnp.einsum('bchd,brcjhd->bhcrj', qr, k_nb).astype(jnp.float32) + bias.astype(jnp.float32)[None]
        p = jax.nn.softmax(sc.reshape(b, HB, GRID_W, kr * kc), axis=-1).reshape(sc.shape)
        return jnp.einsum('bhcrj,brcjhd->bchd', p.astype(v.dtype), v_nb)

    o = lax.map(row_block, (jnp.arange(rows), qg))
    return jnp.moveaxis(o, 0, 1).reshape(b, s, HB * HEAD_DIM)


def _axial_gqa(q, k, v, qn_g, kn_g):
    b, s = q.shape[:2]
    t = jnp.arange(s)
    row, col = t // GRID_W, t % GRID_W
    half = HEAD_DIM // 2

    def axial(z):
        return jnp.concatenate([_rope(z[..., :half], row, AXIAL_THETA), _rope(z[..., half:], col, AXIAL_THETA)], -1)

    q = (axial(_rms_norm(q, qn_g)) * HEAD_DIM ** -0.5).reshape(b, s, KVC, HC // KVC, HEAD_DIM)
    k = axial(_rms_norm(k, kn_g))

    def block(qb):
        sc = jnp.einsum('bqkgd,bskd->bkgqs', qb, k).astype(jnp.float32)
        p = jax.nn.softmax(sc, axis=-1)
        return jnp.einsum('bkgqs,bskd->bqkgd', p.astype(v.dtype), v)

    o = _from_blocks(lax.map(block, _to_blocks(q)))
    return o.reshape(b, s, HC * HEAD_DIM)


def _dilated_attention(q, k, v, pos):
    b, s = q.shape[:2]
    shp = (b, s, N_DIL, HD_PER, HEAD_DIM)
    q = (_partial_rope(q, pos) * HEAD_DIM ** -0.5).reshape(shp)
    k = _partial_rope(k, pos).reshape(shp)
    v = v.reshape(shp)
    offsets = jnp.array(DIL_RATES, dtype=jnp.int32)[:, None] * jnp.arange(-DIL_SIDE, DIL_SIDE + 1, dtype=jnp.int32)[None, :]
    g_idx = jnp.arange(N_DIL)[None, :, None]

    def block(args):
        i, qb = args
        t = i * Q_BLOCK + jnp.arange(Q_BLOCK, dtype=jnp.int32)
        idx = t[:, None, None] + offsets[None]
        valid = (idx >= 0) & (idx < s)
        idx = jnp.clip(idx, 0, s - 1)
        k_nb = k[:, idx, g_idx]
        v_nb = v[:, idx, g_idx]
        sc = jnp.einsum('bqghd,bqgjhd->bghqj', qb, k_nb).astype(jnp.float32)
        sc = jnp.where(jnp.moveaxis(valid, 0, 1)[None, :, None], sc, NEG_INF)
        m = jnp.max(sc, -1, keepdims=True)
        e = jnp.exp(sc - m)
        den = jnp.sum(e, -1, keepdims=True)
        o = jnp.einsum('bghqj,bqgjhd->bqghd', (e / den).astype(v.dtype), v_nb)
        lse = (m + jnp.log(den))[..., 0]
        w = jnp.transpose(jax.nn.softmax(lse, axis=1), (0, 3, 1, 2))[..., None]
        return jnp.sum(o * w.astype(o.dtype), axis=2)

    o = _from_blocks(lax.map(block, (jnp.arange(s // Q_BLOCK, dtype=jnp.int32), _to_blocks(q))))
    return o.reshape(b, s, HD_PER * HEAD_DIM)


def _hybrid_layer(x, w_in, b_gate, lam_params, subln_g, rpb, qn_g, kn_g, w_branch, w_out, ln_g, ln_b, lam_init):
    b, s, _ = x.shape
    pos = jnp.arange(s, dtype=jnp.int32)
    h = jnp.einsum('bsd,de->bse', x, w_in)
    (aq, ak, av, az, bq, bk, bv, bz, cq, ck, cv, cz, dq, dk, dv, dz, gl) = _split(h, IN_SIZES)
    ya = _diff_attention(aq.reshape(b, s, HA, 2, DA_QK), ak.reshape(b, s, HA, 2, DA_QK),
                         av.reshape(b, s, HA, DA_V), lam_params, subln_g, lam_init, pos)
    yb = _neighbourhood_attention(bq.reshape(b, s, HB, HEAD_DIM), bk.reshape(b, s, HB, HEAD_DIM),
                                  bv.reshape(b, s, HB, HEAD_DIM), rpb)
    yc = _axial_gqa(cq.reshape(b, s, HC, HEAD_DIM), ck.reshape(b, s, KVC, HEAD_DIM),
                    cv.reshape(b, s, KVC, HEAD_DIM), qn_g, kn_g)
    yd = _dilated_attention(dq.reshape(b, s, HD, HEAD_DIM), dk.reshape(b, s, HD, HEAD_DIM),
                            dv.reshape(b, s, HD, HEAD_DIM), pos)
    ys = jnp.stack([ya * jax.nn.silu(az), yb * jax.nn.silu(bz),
                    yc * jax.nn.silu(cz), yd * jax.nn.silu(dz)], axis=2)
    proj = jnp.einsum('bsnw,nwd->bsnd', ys, w_branch)
    gates = jax.nn.sigmoid((gl + b_gate).astype(jnp.float32)).astype(x.dtype).reshape(b, s, N_BRANCH, D_MODEL)
    merged = jnp.sum(proj * gates, axis=2)
    out = jnp.einsum('bsd,de->bse', merged, w_out)
    return _layer_norm(DN_ALPHA * x + out, ln_g, ln_b)


def setup_inputs(seed: int = 0) -> dict:
    key = jax.random.key(seed)
    ks = jax.random.split(key, 16)
    L, D = DEPTH, D_MODEL
    nrm = jax.random.normal
    return {
        'x': nrm(ks[0], (BATCH, SEQ, D), jnp.float32),
        'emb_ln_g': 1.0 + 0.01 * nrm(ks[1], (D,), jnp.float32),
        'emb_ln_b': 0.01 * nrm(ks[2], (D,), jnp.float32),
        'w_in': nrm(ks[3], (L, D, D_IN), jnp.float32) * D ** -0.5,
        'b_gate': 0.01 * nrm(ks[4], (L, N_BRANCH * D), jnp.float32),
        'diff_lambda': 0.1 * nrm(ks[5], (L, 4, DA_QK), jnp.float32),
        'diff_subln_g': 1.0 + 0.01 * nrm(ks[6], (L, DA_V), jnp.float32),
        'nat_rpb': 0.1 * nrm(ks[7], (L, HB, 2 * NB_ROWS_MAX - 1, 2 * NB_COLS - 1), jnp.float32),
        'gqa_q_norm_g': 1.0 + 0.01 * nrm(ks[8], (L, HEAD_DIM), jnp.float32),
        'gqa_k_norm_g': 1.0 + 0.01 * nrm(ks[9], (L, HEAD_DIM), jnp.float32),
        'w_branch': nrm(ks[10], (L, N_BRANCH, BRANCH_W, D), jnp.float32) * (BRANCH_W ** -0.5) * DN_BETA,
        'w_out': nrm(ks[11], (L, D, D), jnp.float32) * (D ** -0.5) * DN_BETA,
        'ln_g': 1.0 + 0.01 * nrm(ks[12], (L, D), jnp.float32),
        'ln_b': 0.01 * nrm(ks[13], (L, D), jnp.float32),
    }


def reference(x, emb_ln_g, emb_ln_b, w_in, b_gate, diff_lambda, diff_subln_g, nat_rpb,
              gqa_q_norm_g, gqa_k_norm_g, w_branch, w_out, ln_g, ln_b):
    x = _layer_norm(x, emb_ln_g, emb_ln_b)
    for l in range(DEPTH):
        lam_init = 0.8 - 0.6 * math.exp(-0.3 * l)
        x = _hybrid_layer(x, w_in[l], b_gate[l], diff_lambda[l], diff_subln_g[l], nat_rpb[l],
                          gqa_q_norm_g[l], gqa_k_norm_g[l], w_branch[l], w_out[l], ln_g[l], ln_b[l], lam_init)
    return x
```

```python
import math
from contextlib import ExitStack
import numpy as np
import ml_dtypes
import concourse.bass as bass
import concourse.mybir as mybir
from concourse.bass_utils import run_bass_kernel_spmd

F32 = mybir.dt.float32
BF16 = mybir.dt.bfloat16
AF = mybir.ActivationFunctionType
ALU = mybir.AluOpType
NPBF = ml_dtypes.bfloat16

NCORE = 8
S = 8192
D = 2048
TPC = 1024
NT = 8
KC = 16
DEPTH = 4
NEG = -30000.0
LN_EPS = 1e-5
RMS_EPS = 1e-6
DN_ALPHA = (2 * DEPTH) ** 0.25
DIL_RATES = (1, 4, 16)


def ss(start, n, step=1):
    return slice(start, start + step * (n - 1) + 1, step)


class Res:
    __slots__ = ("name", "ws", "rs", "dsem")

    def __init__(self, name):
        self.name = name
        self.ws = []
        self.rs = []
        self.dsem = None


class Prog:
    ENGS = ("pe", "act", "dve", "pool", "sp")

    def __init__(self, nc, es):
        self.nc = nc
        self.es = es
        self.eng = {"pe": nc.tensor, "act": nc.scalar, "dve": nc.vector, "pool": nc.gpsimd, "sp": nc.sync}
        self.streams = {e: [] for e in self.ENGS}
        self.cnt = {e: 0 for e in self.ENGS}
        self.sem = {}
        self.semtot = {}
        for e in self.ENGS:
            self.sem[e] = es.enter_context(nc.semaphore("s_" + e))
        self.waited = {e: {} for e in self.ENGS}
        self.ndsem = 0
        self.pending_noinc = {e: False for e in self.ENGS}
        self.nres = 0

    def res(self, name="r"):
        self.nres += 1
        return Res(name + str(self.nres))

    def sb(self, es, name, shape, dt):
        self.nres += 1
        t = es.enter_context(self.nc.sbuf_tensor("sb_%s_%d" % (name, self.nres), shape, dt))
        return t, self.res(name)

    def dsem_of(self, r):
        if r.dsem is None:
            key = "d%d" % self.ndsem
            self.ndsem += 1
            self.sem[key] = self.es.enter_context(self.nc.semaphore("sd%d" % self.ndsem))
            self.semtot[key] = 0
            r.dsem = key
        return r.dsem

    def _wait(self, e, deps):
        need = {}
        for (k, v) in deps:
            if k == e and e == "pe":
                continue
            if self.waited[e].get(k, 0) >= v:
                continue
            if need.get(k, 0) < v:
                need[k] = v
        for k, v in need.items():
            self.waited[e][k] = v
            sem = self.sem[k]
            eng = self.eng[e]
            self.streams[e].append(lambda eng=eng, sem=sem, v=v: eng.wait_ge(sem, v))

    @staticmethod
    def _deps(reads, writes):
        deps = []
        for r in reads:
            deps.extend(r.ws)
        for r in writes:
            deps.extend(r.ws)
            deps.extend(r.rs)
        return deps

    def op(self, e, meth, reads=(), writes=(), inc=True, **kw):
        self._wait(e, self._deps(reads, writes))
        sem = self.sem[e]
        if inc:
            self.streams[e].append(lambda meth=meth, kw=kw, sem=sem: meth(**kw).then_inc(sem, 1))
            self.cnt[e] += 1
            tok = (e, self.cnt[e])
            self.pending_noinc[e] = False
        else:
            assert e == "pe"
            self.streams[e].append(lambda meth=meth, kw=kw: meth(**kw))
            tok = (e, self.cnt[e] + 1)
            self.pending_noinc[e] = True
        for r in reads:
            r.rs.append(tok)
            if len(r.rs) > 64:
                r.rs = self._compact(r.rs)
        for r in writes:
            r.ws = self._compact(r.ws + [tok])
            r.rs = []
        return tok

    @staticmethod
    def _compact(toks):
        best = {}
        for k, v in toks:
            if best.get(k, 0) < v:
                best[k] = v
        return list(best.items())

    def dma(self, q, out, in_, reads=(), writes=(), semres=None, multi=False, **kw):
        self._wait(q, self._deps(reads, writes))
        if semres is None:
            semres = writes[0] if writes else reads[0]
        key = self.dsem_of(semres)
        self.semtot[key] += 16
        tok = (key, self.semtot[key])
        sem = self.sem[key]
        eng = self.eng[q]
        self.streams[q].append(lambda eng=eng, out=out, in_=in_, kw=kw, sem=sem: eng.dma_start(
            out=(out() if callable(out) else out), in_=(in_() if callable(in_) else in_), **kw).then_inc(sem, 16))
        for r in reads:
            r.rs.append(tok)
            if len(r.rs) > 64:
                r.rs = self._compact(r.rs)
        for r in writes:
            r.ws = self._compact(r.ws + [tok])
            r.rs = []
        return tok

    def raw(self, e, fn):
        self.streams[e].append(fn)

    def coll(self, kind, ins, outs, reads=(), writes=()):
        self._wait("pool", self._deps(reads, writes))
        key = self.dsem_of(writes[0])
        self.semtot[key] += 16
        tok = (key, self.semtot[key])
        sem = self.sem[key]
        nc = self.nc
        self.streams["pool"].append(lambda: nc.gpsimd.collective_compute(kind, ALU.bypass, replica_groups=[list(range(NCORE))], ins=ins, outs=outs).then_inc(sem, 16))
        for r in reads:
            r.rs.append(tok)
        for r in writes:
            r.ws = self._compact(r.ws + [tok])
            r.rs = []
        return tok

    def barrier(self):
        assert not any(self.pending_noinc.values())
        deps = [(e, self.cnt[e]) for e in ("pe", "act", "dve", "pool") if self.cnt[e] > 0]
        deps += [(k, v) for k, v in self.semtot.items() if v > 0]
        for e in self.ENGS:
            self._wait(e, [d for d in deps if d[0] != e])

    def flush(self):
        assert not any(self.pending_noinc.values())
        nc = self.nc
        st = self.streams
        with nc.Block() as block:
            @block.tensor
            def _(e):
                for f in st["pe"]:
                    f()

            @block.scalar
            def _(e):
                for f in st["act"]:
                    f()

            @block.vector
            def _(e):
                for f in st["dve"]:
                    f()

            @block.gpsimd
            def _(e):
                for f in st["pool"]:
                    f()

            @block.sync
            def _(e):
                for f in st["sp"]:
                    f()
        self.streams = {e: [] for e in self.ENGS}


class Ctx:
    pass


def setup_common(P, es):
    nc = P.nc
    C = Ctx()
    C.ps = []
    C.rps = []
    for i in range(8):
        t = es.enter_context(nc.psum_tensor("psb%d" % i, [128, 512], F32))
        C.ps.append(t)
        C.rps.append(P.res("ps"))
    C.ident, C.r_ident = P.sb(es, "ident", [128, 128], BF16)
    C.onesb, C.r_onesb = P.sb(es, "onesb", [128, 128], BF16)
    C.onesf, C.r_onesf = P.sb(es, "onesf", [128, 128], F32)
    C.eps_ln, C.r_eps_ln = P.sb(es, "epsln", [128, 1], F32)
    C.eps_rms, C.r_eps_rms = P.sb(es, "epsrms", [128, 1], F32)
    P.op("pool", nc.gpsimd.memset, [], [C.r_ident], ap=C.ident[:], constant=0.0)
    P.op("pool", nc.gpsimd.affine_select, [C.r_ident], [C.r_ident], out=C.ident[:], in_=C.ident[:], compare_op=ALU.not_equal, fill=1.0,
         base=0, pattern=[[-1, 128]], channel_multiplier=1)
    P.op("pool", nc.gpsimd.memset, [], [C.r_onesb], ap=C.onesb[:], constant=1.0)
    P.op("pool", nc.gpsimd.memset, [], [C.r_onesf], ap=C.onesf[:], constant=1.0)
    P.op("pool", nc.gpsimd.memset, [], [C.r_eps_ln], ap=C.eps_ln[:], constant=LN_EPS)
    P.op("pool", nc.gpsimd.memset, [], [C.r_eps_rms], ap=C.eps_rms[:], constant=RMS_EPS)
    return C


def build_xT(P, C, es, x_dram, r_x, xT, r_xT):
    nc = P.nc
    xb = [P.sb(es, "xb%d" % i, [128, D], BF16) for i in range(2)]
    for tt in range(NT):
        t, r = xb[tt % 2]
        P.dma("pool", t[:], x_dram[tt * 128:(tt + 1) * 128, :], reads=[r_x], writes=[r])
        for half in range(2):
            bank = 6 + half
            pst = C.ps[bank][:].bitcast(BF16)
            for k8 in range(8):
                kc = half * 8 + k8
                P.op("pe", nc.tensor.transpose, [r, C.r_ident], [C.rps[bank]], inc=(k8 == 7),
                     out=pst[:, k8 * 128:(k8 + 1) * 128], in_=t[:, kc * 128:(kc + 1) * 128], identity=C.ident[:])
            dst = xT[:, half * 8:(half + 1) * 8, tt * 128:(tt + 1) * 128]
            src = pst.rearrange("p (k t) -> p k t", k=8)
            if half == 0:
                P.op("dve", nc.vector.tensor_copy, [C.rps[bank]], [r_xT], out=dst, in_=src)
            else:
                P.op("act", nc.scalar.copy, [C.rps[bank]], [r_xT], out=dst, in_=src)


def load_wtile(P, wt, r_wt, w_dram, r_w, idx):
    P.dma("pool", wt[:, 0:8, :], w_dram[idx, :, 0:8, :], reads=[r_w], writes=[r_wt])
    P.dma("pool", wt[:, 8:16, :], w_dram[idx, :, 8:16, :], reads=[r_w], writes=[r_wt], multi=True)


def proj_tok(P, C, xT, r_xT, wt, r_wt, tt, bank, ncols=512):
    nc = P.nc
    for kc in range(KC):
        P.op("pe", nc.tensor.matmul, [r_xT, r_wt], [C.rps[bank]], inc=(kc == KC - 1),
             out=C.ps[bank][:, 0:ncols], lhsT=xT[:, kc, tt * 128:(tt + 1) * 128], rhs=wt[:, kc, 0:ncols], start=(kc == 0), stop=(kc == KC - 1))


def proj_feat(P, C, xT, r_xT, wt, r_wt, j, half, bank):
    nc = P.nc
    for kc in range(KC):
        P.op("pe", nc.tensor.matmul, [r_xT, r_wt], [C.rps[bank]], inc=(kc == KC - 1),
             out=C.ps[bank][:], lhsT=wt[:, kc, j * 128:(j + 1) * 128], rhs=xT[:, kc, half * 512:(half + 1) * 512], start=(kc == 0), stop=(kc == KC - 1))


def rope_tok(P, src, r_src, dst, r_dst, H, a, m, cos, sin, r_tab, t1, t2, r_t1, r_t2):
    nc = P.nc
    s3 = src.rearrange("p (h d) -> p h d", h=H)
    d3 = dst.rearrange("p (h d) -> p h d", h=H)
    x1 = s3[:, :, a:a + m]
    x2 = s3[:, :, a + m:a + 2 * m]
    cb = cos.unsqueeze(1).broadcast_to([128, H, m])
    sb_ = sin.unsqueeze(1).broadcast_to([128, H, m])
    u1 = t1[:, 0:H * m].rearrange("p (h d) -> p h d", h=H)
    u2 = t2[:, 0:H * m].rearrange("p (h d) -> p h d", h=H)
    TT = nc.vector.tensor_tensor
    P.op("dve", TT, [r_src, r_tab], [r_t1], out=u1, in0=x1, in1=cb, op=ALU.mult)
    P.op("dve", TT, [r_src, r_tab], [r_t2], out=u2, in0=x2, in1=sb_, op=ALU.mult)
    P.op("dve", TT, [r_t1, r_t2], [r_dst], out=d3[:, :, a:a + m], in0=u1, in1=u2, op=ALU.subtract)
    P.op("dve", TT, [r_src, r_tab], [r_t1], out=u1, in0=x2, in1=cb, op=ALU.mult)
    P.op("dve", TT, [r_src, r_tab], [r_t2], out=u2, in0=x1, in1=sb_, op=ALU.mult)
    P.op("dve", TT, [r_t1, r_t2], [r_dst], out=d3[:, :, a + m:a + 2 * m], in0=u1, in1=u2, op=ALU.add)


T_CA, T_SA, T_CD, T_SD, T_CR, T_SR, T_CC, T_SC = 0, 8, 16, 32, 48, 80, 112, 144
NTAB = 176


class TokQK:
    def __init__(self, P, C, es, tab, r_tab):
        self.P, self.C, self.tab, self.r_tab = P, C, tab, r_tab
        self.hsb = [P.sb(es, "hsb%d" % i, [128, 512], F32) for i in range(2)]
        self.hn = [P.sb(es, "hn%d" % i, [128, 512], F32) for i in range(2)]
        self.qb = [P.sb(es, "qb%d" % i, [128, 512], BF16) for i in range(2)]
        self.t1 = [P.sb(es, "rt1%d" % i, [128, 256], F32) for i in range(2)]
        self.t2 = [P.sb(es, "rt2%d" % i, [128, 256], F32) for i in range(2)]
        self.ssq, self.r_ssq = P.sb(es, "ssq", [128, 8], F32)
        self.junk, self.r_junk = P.sb(es, "junk", [128, 128], F32)
        self.n = 0

    def run(self, xT, r_xT, wt, r_wt, tt, kind, nheads, dst_fn, g_bc=None, r_g=None, extra_v=None):
        P, C, nc = self.P, self.C, self.P.nc
        i = self.n % 2
        self.n += 1
        bank = 4 + i
        proj_tok(P, C, xT, r_xT, wt, r_wt, tt, bank)
        hsb, r_hsb = self.hsb[i]
        hn, r_hn = self.hn[i]
        qb, r_qb = self.qb[i]
        t1, r_t1 = self.t1[i]
        t2, r_t2 = self.t2[i]
        tab, r_tab = self.tab, self.r_tab
        ps = C.ps[bank]
        nq = nheads * 128 if kind == "C" else 512
        P.op("act", nc.scalar.copy, [C.rps[bank]], [r_hsb], out=hsb[:], in_=ps[:])
        if extra_v is not None:
            extra_v(hsb, r_hsb)
        if kind in ("A", "D"):
            P.op("act", nc.scalar.copy, [r_hsb], [r_qb], out=qb[:], in_=hsb[:])
            if kind == "A":
                rope_tok(P, hsb[:], r_hsb, qb[:], r_qb, 8, 0, 8, tab[:, tt, T_CA:T_CA + 8], tab[:, tt, T_SA:T_SA + 8], r_tab, t1, t2, r_t1, r_t2)
            else:
                rope_tok(P, hsb[:], r_hsb, qb[:], r_qb, 4, 0, 16, tab[:, tt, T_CD:T_CD + 16], tab[:, tt, T_SD:T_SD + 16], r_tab, t1, t2, r_t1, r_t2)
        else:
            ssq, r_ssq = self.ssq, self.r_ssq
            P.op("dve", nc.vector.tensor_tensor, [r_hsb], [r_hn], out=hn[:, 0:nq], in0=hsb[:, 0:nq], in1=hsb[:, 0:nq], op=ALU.mult)
            P.op("dve", nc.vector.tensor_reduce, [r_hn], [r_ssq], out=ssq[:, 0:nheads], in_=hn[:, 0:nq].rearrange("p (h d) -> p h d", h=nheads),
                 axis=mybir.AxisListType.X, op=ALU.add)
            P.op("act", nc.scalar.activation, [r_ssq, C.r_eps_rms], [r_ssq], out=ssq[:, 0:nheads], in_=ssq[:, 0:nheads], func=AF.Sqrt,
                 scale=1.0 / 128, bias=C.eps_rms[:])
            P.op("dve", nc.vector.reciprocal, [r_ssq], [r_ssq], out=ssq[:, 0:nheads], in_=ssq[:, 0:nheads])
            for h in range(nheads):
                P.op("dve", nc.vector.scalar_tensor_tensor, [r_hsb, r_ssq, r_g], [r_hn], out=hn[:, h * 128:(h + 1) * 128],
                     in0=hsb[:, h * 128:(h + 1) * 128], scalar=ssq[:, h:h + 1], in1=g_bc[:], op0=ALU.mult, op1=ALU.mult)
            rope_tok(P, hn[:, 0:nq], r_hn, qb[:, 0:nq], r_qb, nheads, 0, 32, tab[:, tt, T_CR:T_CR + 32], tab[:, tt, T_SR:T_SR + 32], r_tab, t1, t2, r_t1, r_t2)
            rope_tok(P, hn[:, 0:nq], r_hn, qb[:, 0:nq], r_qb, nheads, 64, 32, tab[:, tt, T_CC:T_CC + 32], tab[:, tt, T_SC:T_SC + 32], r_tab, t1, t2, r_t1, r_t2)
        nj = nq // 128
        tb = 6 + i
        pst = C.ps[tb][:].bitcast(BF16)
        for j in range(nj):
            P.op("pe", nc.tensor.transpose, [r_qb, C.r_ident], [C.rps[tb]], inc=(j == nj - 1),
                 out=pst[:, j * 128:(j + 1) * 128], in_=qb[:, j * 128:(j + 1) * 128], identity=C.ident[:])
        ncp = 0
        for j in range(nj):
            dl = dst_fn(j)
            if isinstance(dl, tuple):
                dl = [(dl[0], dl[1], 0, 128)]
            for (dst, r_dst, p0, p1) in dl:
                if ncp % 2 == 0:
                    P.op("dve", nc.vector.tensor_copy, [C.rps[tb]], [r_dst], out=dst, in_=pst[p0:p1, j * 128:(j + 1) * 128])
                else:
                    P.op("act", nc.scalar.copy, [C.rps[tb]], [r_dst], out=dst, in_=pst[p0:p1, j * 128:(j + 1) * 128])
                ncp += 1


def layer_norm_tile(P, C, xt, r_xt, g_bc, b_bc, r_gb, stats, r_stats, mv, r_mv):
    nc = P.nc
    for c in range(4):
        P.op("dve", nc.vector.bn_stats, [r_xt], [r_stats], out=stats[:, c * 6:(c + 1) * 6], in_=xt[:, c * 512:(c + 1) * 512])
    P.op("dve", nc.vector.bn_aggr, [r_stats], [r_mv], out=mv[:, 0:2], in_=stats[:, 0:24])
    P.op("act", nc.scalar.activation, [r_mv, C.r_eps_ln], [r_mv], out=mv[:, 2:3], in_=mv[:, 1:2], func=AF.Sqrt, scale=1.0, bias=C.eps_ln[:])
    P.op("dve", nc.vector.reciprocal, [r_mv], [r_mv], out=mv[:, 2:3], in_=mv[:, 2:3])
    P.op("dve", nc.vector.tensor_scalar, [r_xt, r_mv], [r_xt], out=xt[:], in0=xt[:], scalar1=mv[:, 0:1], scalar2=mv[:, 2:3],
         op0=ALU.subtract, op1=ALU.mult)
    P.op("pool", nc.gpsimd.tensor_tensor, [r_xt, r_gb], [r_xt], out=xt[:], in0=xt[:], in1=g_bc[:], op=ALU.mult)
    P.op("dve", nc.vector.tensor_tensor, [r_xt, r_gb], [r_xt], out=xt[:], in0=xt[:], in1=b_bc[:], op=ALU.add)


def emit_E(P, C, es, x, r_in, g, b, xo, r_out):
    gb, r_gb = P.sb(es, "gb", [128, D], F32)
    bb, _ = P.sb(es, "bb", [128, D], F32)
    P.dma("sp", gb[:], g, writes=[r_gb])
    P.dma("sp", bb[:], b, writes=[r_gb], multi=True)
    xts = [P.sb(es, "xt%d" % i, [128, D], F32) for i in range(2)]
    stats, r_stats = P.sb(es, "stats", [128, 24], F32)
    mv, r_mv = P.sb(es, "mv", [128, 4], F32)
    for tt in range(NT):
        xt, r_xt = xts[tt % 2]
        P.dma("sp", xt[:], x[tt * 128:(tt + 1) * 128, :], reads=[r_in], writes=[r_xt])
        layer_norm_tile(P, C, xt, r_xt, gb, bb, r_gb, stats, r_stats, mv, r_mv)
        P.dma("sp", xo[tt * 128:(tt + 1) * 128, :], xt[:], reads=[r_xt], writes=[r_out], semres=r_xt, multi=True)


def build_E():
    nc = bass.Bass("TRN2", target_bir_lowering=False)
    x = nc.dram_tensor("x", [TPC, D], F32, kind="ExternalInput").ap()
    g = nc.dram_tensor("g", [128, D], F32, kind="ExternalInput").ap()
    b = nc.dram_tensor("b", [128, D], F32, kind="ExternalInput").ap()
    xo = nc.dram_tensor("xo", [TPC, D], F32, kind="ExternalOutput").ap()
    with ExitStack() as es:
        P = Prog(nc, es)
        C = setup_common(P, es)
        emit_E(P, C, es, x, P.res("in"), g, b, xo, P.res("out"))
        P.barrier()
        P.flush()
    return nc


NKT = 22
NV = 2816


def emit_P(P, C, es, x, r_x, wP, r_wP, tabd, gkd, kT, r_kT, v, r_v):
    nc = P.nc
    xT, r_xT = P.sb(es, "xT", [128, KC, TPC], BF16)
    tab, r_tab = P.sb(es, "tab", [128, NT, NTAB], F32)
    gk, r_gk = P.sb(es, "gk", [128, 128], F32)
    P.dma("sp", tab[:], tabd, writes=[r_tab])
    P.dma("sp", gk[:], gkd, writes=[r_gk])
    build_xT(P, C, es, x, r_x, xT, r_xT)
    wts = [P.sb(es, "wt%d" % i, [128, KC, 512], BF16) for i in range(2)]
    kst = [P.sb(es, "kst%d" % i, [128, 4, TPC], BF16) for i in range(2)]
    vst = [P.sb(es, "vst%d" % i, [128, 512], BF16) for i in range(3)]
    tq = TokQK(P, C, es, tab, r_tab)
    order = [0, 1, 2, 3, 5, 4, 6, 7, 8, 9, 10]
    load_wtile(P, wts[0][0], wts[0][1], wP, r_wP, order[0])
    nks = 0
    nvs = 0
    for oi, ti in enumerate(order):
        wt, r_wt = wts[oi % 2]
        if oi + 1 < len(order):
            load_wtile(P, wts[(oi + 1) % 2][0], wts[(oi + 1) % 2][1], wP, r_wP, order[oi + 1])
        if ti in (0, 1, 2, 3, 5):
            ks, r_ks = kst[nks % 2]
            nks += 1
            if ti == 5:
                nch, ch0 = 2, 4
            else:
                nch, ch0 = 4, (0 if ti == 0 else 6 + (ti - 1) * 4)
            for tt in range(NT):
                dst_fn = (lambda j, ks=ks, r_ks=r_ks, tt=tt: (ks[:, j, tt * 128:(tt + 1) * 128], r_ks))
                if ti == 5:
                    vs, r_vs = vst[nvs % 3]
                    nvs += 1

                    def extra(hsb, r_hsb, vs=vs, r_vs=r_vs, tt=tt):
                        P.op("act", nc.scalar.copy, [r_hsb], [r_vs], out=vs[:, 0:256], in_=hsb[:, 256:512])
                        P.dma("sp", v[tt * 128:(tt + 1) * 128, 1024:1280], vs[:, 0:256], reads=[r_vs], writes=[r_v], semres=r_vs, multi=True)
                    tq.run(xT, r_xT, wt, r_wt, tt, "C", 2, dst_fn, g_bc=gk, r_g=r_gk, extra_v=extra)
                else:
                    tq.run(xT, r_xT, wt, r_wt, tt, "A" if ti == 0 else "D", 4, dst_fn)
            for j in range(nch):
                P.dma("sp", kT[ch0 + j], ks[:, j, :], reads=[r_ks], writes=[r_kT], semres=r_ks, multi=True)
        elif ti == 4:
            ks, r_ks = kst[nks % 2]
            nks += 1
            n = 0
            for j in range(4):
                for half in range(2):
                    bank = 4 + n % 2
                    n += 1
                    proj_feat(P, C, xT, r_xT, wt, r_wt, j, half, bank)
                    if n % 2 == 0:
                        P.op("act", nc.scalar.copy, [C.rps[bank]], [r_ks], out=ks[:, j, half * 512:(half + 1) * 512], in_=C.ps[bank][:])
                    else:
                        P.op("dve", nc.vector.tensor_copy, [C.rps[bank]], [r_ks], out=ks[:, j, half * 512:(half + 1) * 512], in_=C.ps[bank][:])
            for j in range(4):
                P.dma("sp", kT[18 + j], ks[:, j, :], reads=[r_ks], writes=[r_kT], semres=r_ks, multi=True)
        else:
            voff = {6: 0, 7: 512, 8: 1280, 9: 1792, 10: 2304}[ti]
            for tt in range(NT):
                bank = 4 + tt % 2
                proj_tok(P, C, xT, r_xT, wt, r_wt, tt, bank)
                vs, r_vs = vst[nvs % 3]
                nvs += 1
                if tt % 2 == 0:
                    P.op("act", nc.scalar.copy, [C.rps[bank]], [r_vs], out=vs[:], in_=C.ps[bank][:])
                else:
                    P.op("dve", nc.vector.tensor_copy, [C.rps[bank]], [r_vs], out=vs[:], in_=C.ps[bank][:])
                P.dma("sp", v[tt * 128:(tt + 1) * 128, voff:voff + 512], vs[:], reads=[r_vs], writes=[r_v], semres=r_vs, multi=True)


def build_P():
    nc = bass.Bass("TRN2", target_bir_lowering=False)
    x = nc.dram_tensor("x", [TPC, D], F32, kind="ExternalInput").ap()
    wP = nc.dram_tensor("wP", [11, 128, KC, 512], F32, kind="ExternalInput").ap()
    tabd = nc.dram_tensor("tab", [128, NT, NTAB], F32, kind="ExternalInput").ap()
    gkd = nc.dram_tensor("gk", [128, 128], F32, kind="ExternalInput").ap()
    kT = nc.dram_tensor("kT", [NKT, 128, TPC], BF16, kind="ExternalOutput").ap()
    v = nc.dram_tensor("v", [TPC, NV], BF16, kind="ExternalOutput").ap()
    with ExitStack() as es:
        P = Prog(nc, es)
        C = setup_common(P, es)
        emit_P(P, C, es, x, P.res("x"), wP, P.res("wP"), tabd, gkd, kT, P.res("kT"), v, P.res("v"))
        P.barrier()
        P.flush()
    return nc


BW = 1792
DW = 3072
NMASK = 22


class WStream:
    def __init__(self, P, wts, w_dram, r_w, order):
        self.P, self.wts, self.w, self.r_w, self.order = P, wts, w_dram, r_w, order
        self.i = 0
        load_wtile(P, wts[0][0], wts[0][1], w_dram, r_w, order[0])

    def get(self, expect):
        assert self.order[self.i] == expect, (self.order[self.i], expect)
        cur = self.wts[self.i % 2]
        self.i += 1
        if self.i < len(self.order):
            nxt = self.wts[self.i % 2]
            load_wtile(self.P, nxt[0], nxt[1], self.w, self.r_w, self.order[self.i])
        return cur


def emit_silu_z(P, C, xT, r_xT, wt, r_wt, siluz, r_siluz):
    nc = P.nc
    n = 0
    for j in range(4):
        for half in range(2):
            bank = 4 + n % 2
            n += 1
            proj_feat(P, C, xT, r_xT, wt, r_wt, j, half, bank)
            P.op("act", nc.scalar.activation, [C.rps[bank]], [r_siluz], out=siluz[:, j, half * 512:(half + 1) * 512], in_=C.ps[bank][:], func=AF.Silu)


class Grp:
    pass


def dense_attention(P, C, groups, pT, scale):
    nc = P.nc
    items = [(g, kc) for g in groups for kc in range(g.nk)]
    n = len(items)

    def qk(idx):
        g, kc = items[idx]
        sbk = idx % 3
        P.op("pe", nc.tensor.matmul, [g.r_q, g.r_kfn(kc)], [C.rps[sbk]], out=C.ps[sbk][:], lhsT=g.kfn(kc), rhs=g.q, start=True, stop=True)

    qk(0)
    if n > 1:
        qk(1)
    for idx in range(n):
        g, kc = items[idx]
        sbk = idx % 3
        pt, r_pt = pT[idx % len(pT)]
        P.op("act", nc.scalar.activation, [C.rps[sbk]], [r_pt], out=pt[:], in_=C.ps[sbk][:], func=AF.Exp, scale=scale)
        last = (kc == g.nk - 1)
        P.op("pe", nc.tensor.matmul, [r_pt, g.r_vfn(kc)], [C.rps[g.ob]], inc=False, out=C.ps[g.ob][:], lhsT=g.vfn(kc), rhs=pt[:], start=(kc == 0), stop=last)
        P.op("pe", nc.tensor.matmul, [r_pt, C.r_onesb], [C.rps[g.db]], out=C.ps[g.db][:], lhsT=C.onesb[:], rhs=pt[:], start=(kc == 0), stop=last)
        if idx + 2 < n:
            qk(idx + 2)
        if last:
            g.fin()


def emit_T(P, C, es, D_, lam_consts=None):
    nc = P.nc
    TT = nc.vector.tensor_tensor
    x, r_x = D_["x"], P.res("x")
    r_w = P.res("wT")
    r_kv = P.res("kvdram")
    r_xo = D_.get("r_xo") or P.res("xo")
    wts = [P.sb(es, "wt%d" % i, [128, KC, 512], BF16) for i in range(2)]
    sm, r_sm = P.sb(es, "sm", [128, 16], F32)
    bgT, r_bgT = P.sb(es, "bgT", [128, 64], F32)
    P.dma("sp", bgT[:], D_["bgT"], writes=[r_bgT])
    order = [6, 0, 8, 1, 7, 5, 9, 2, 3, 4] + list(range(10, 26))
    ws = WStream(P, wts, D_["wT"], r_w, order)

    with ExitStack() as em:
        xT, r_xT = P.sb(em, "xT", [128, KC, TPC], BF16)
        ysT, r_ysT = P.sb(em, "ysT", [128, KC, TPC], BF16)
        tab, r_tab = P.sb(em, "tab", [128, NT, NTAB], F32)
        P.dma("sp", tab[:], D_["tab"], writes=[r_tab])
        siluz, r_siluz = P.sb(em, "siluz", [128, 4, TPC], BF16)
        fin = []
        big, r_big = P.sb(em, "big", [128, 16384], BF16)
        mergedT = big[:].rearrange("p (k t) -> p k t", k=KC)
        r_mergedT = r_big

        with ExitStack() as ep:
            lp, r_lp = P.sb(ep, "lp", [128, 256], F32)
            cst, r_cst = P.sb(ep, "cst", [128, 4], F32)
            subg, r_subg = P.sb(ep, "subg", [128, 1], F32)
            pr, r_pr = P.sb(ep, "pr", [128, 128], F32)
            P.dma("sp", lp[:], D_["lp"], writes=[r_lp])
            P.dma("sp", cst[:], D_["cst"], writes=[r_cst])
            P.dma("sp", subg[:], D_["subg"], writes=[r_subg])
            P.op("dve", TT, [r_lp], [r_pr], out=pr[:].rearrange("p (a d) -> p a d", a=2), in0=lp[:].rearrange("p (a b d) -> p a b d", a=2, b=2)[:, :, 0, :],
                 in1=lp[:].rearrange("p (a b d) -> p a b d", a=2, b=2)[:, :, 1, :], op=ALU.mult)
            P.op("dve", nc.vector.tensor_reduce, [r_pr], [r_sm], out=sm[:, 3:5], in_=pr[:].rearrange("p (a d) -> p a d", a=2), axis=mybir.AxisListType.X, op=ALU.add)
            P.op("act", nc.scalar.activation, [r_sm], [r_sm], out=sm[:, 5:7], in_=sm[:, 3:5], func=AF.Exp)
            P.op("dve", TT, [r_sm], [r_sm], out=sm[:, 7:8], in0=sm[:, 5:6], in1=sm[:, 6:7], op=ALU.subtract)
            P.op("dve", TT, [r_sm, r_cst], [r_sm], out=sm[:, 0:1], in0=sm[:, 7:8], in1=cst[:, 0:1], op=ALU.add)
            P.op("dve", nc.vector.tensor_scalar, [r_sm], [r_sm], out=sm[:, 1:2], in0=sm[:, 0:1], scalar1=-1.0, scalar2=None, op0=ALU.mult)
            P.op("dve", TT, [r_subg, r_cst], [r_sm], out=sm[:, 2:3], in0=subg[:], in1=cst[:, 1:2], op=ALU.mult)
            build_xT(P, C, ep, x, r_x, xT, r_xT)
            P.barrier()
            P.flush()

        def fin_simple(ob, db, ci, j):
            rden, r_rden = fin[-2]
            on, r_on = fin[-1]
            P.op("dve", nc.vector.reciprocal, [C.rps[db]], [r_rden], out=rden[:], in_=C.ps[db][:])
            P.op("dve", TT, [C.rps[ob], r_rden], [r_on], out=on[:], in0=C.ps[ob][:], in1=rden[:], op=ALU.mult)
            P.op("dve", TT, [r_on, r_siluz], [r_ysT], out=ysT[:, ci, j * 512:(j + 1) * 512], in0=on[:], in1=siluz[:, ci % 4, j * 512:(j + 1) * 512], op=ALU.mult)

        with ExitStack() as ep:
            tq = TokQK(P, C, ep, tab, r_tab)
            pT = [P.sb(ep, "pT%d" % i, [128, 512], BF16) for i in range(4)]
            fin = [P.sb(ep, "fin%d" % i, [128, 512], F32) for i in range(6)]
            gq, r_gq = P.sb(ep, "gq", [128, 128], F32)
            P.dma("sp", gq[:], D_["gq"], writes=[r_gq])
            qT, r_qT = P.sb(ep, "qT", [128, 4, TPC], BF16)
            qT2, r_qT2 = P.sb(ep, "qT2", [128, 4, TPC], BF16)
            P.op("pool", nc.gpsimd.memset, [], [r_qT], ap=qT[64:128, :, :], constant=0.0)
            P.op("pool", nc.gpsimd.memset, [], [r_qT2], ap=qT2[0:64, :, :], constant=0.0)
            ksb = big[:, 0:S]
            vsb = big[:, S:2 * S].rearrange("p (k d) -> p k d", k=64)
            r_kq = [P.res("kq") for _ in range(4)]
            r_vq = [P.res("vq") for _ in range(4)]

            def load_kv(kidx, vcol):
                for i in range(4):
                    P.dma("sp", ksb[:, i * 2048:(i + 1) * 2048], D_["kT_AC"][kidx, :, i * 2048:(i + 1) * 2048], reads=[r_kv], writes=[r_kq[i]])
                    P.dma("sp", vsb[:, i * 16:(i + 1) * 16, :],
                          D_["v_AC"][i * 2048:(i + 1) * 2048, vcol:vcol + 128].rearrange("(kc p) d -> p kc d", p=128), reads=[r_kv], writes=[r_vq[i]])

            wt, r_wt = ws.get(6)
            emit_silu_z(P, C, xT, r_xT, wt, r_wt, siluz, r_siluz)
            wt, r_wt = ws.get(0)
            for tt in range(NT):
                tq.run(xT, r_xT, wt, r_wt, tt, "A", 4, lambda j, tt=tt: [(qT[0:64, j, tt * 128:(tt + 1) * 128], r_qT, 0, 64),
                                                                        (qT2[64:128, j, tt * 128:(tt + 1) * 128], r_qT2, 64, 128)])
            ng = 0
            for h in range(4):
                load_kv(h, h * 128)
                groups = []
                for j in range(2):
                    for c in range(2):
                        g = Grp()
                        g.q = (qT if c == 0 else qT2)[:, h, j * 512:(j + 1) * 512]
                        g.r_q = (r_qT if c == 0 else r_qT2)
                        g.kfn = lambda kc: ksb[:, kc * 128:(kc + 1) * 128]
                        g.r_kfn = lambda kc: r_kq[kc // 16]
                        g.vfn = lambda kc: vsb[:, kc, :]
                        g.r_vfn = lambda kc: r_vq[kc // 16]
                        g.nk = 64
                        g.ob, g.db = (3, 4) if ng % 2 == 0 else (5, 6)
                        ng += 1

                        def fin_a(g=g, c=c, j=j, h=h):
                            rden, r_rden = fin[0]
                            P.op("dve", nc.vector.reciprocal, [C.rps[g.db]], [r_rden], out=rden[:], in_=C.ps[g.db][:])
                            if c == 0:
                                o1, r_o1 = fin[1]
                                P.op("dve", TT, [C.rps[g.ob], r_rden], [r_o1], out=o1[:], in0=C.ps[g.ob][:], in1=rden[:], op=ALU.mult)
                                return
                            o1, r_o1 = fin[1]
                            o2, r_o2 = fin[2]
                            dd, r_dd = fin[3]
                            sq, r_sq = fin[4]
                            rs, r_rs = fin[5]
                            P.op("dve", TT, [C.rps[g.ob], r_rden], [r_o2], out=o2[:], in0=C.ps[g.ob][:], in1=rden[:], op=ALU.mult)
                            P.op("dve", nc.vector.scalar_tensor_tensor, [r_o2, r_o1, r_sm], [r_dd], out=dd[:], in0=o2[:], scalar=sm[:, 1:2], in1=o1[:],
                                 op0=ALU.mult, op1=ALU.add)
                            P.op("dve", TT, [r_dd], [r_sq], out=sq[:], in0=dd[:], in1=dd[:], op=ALU.mult)
                            P.op("pe", nc.tensor.matmul, [r_sq, C.r_onesf], [C.rps[7]], out=C.ps[7][:], lhsT=C.onesf[:], rhs=sq[:], start=True, stop=True)
                            P.op("act", nc.scalar.activation, [C.rps[7], C.r_eps_rms], [r_rs], out=rs[:], in_=C.ps[7][:], func=AF.Sqrt, scale=1.0 / 128, bias=C.eps_rms[:])
                            P.op("dve", nc.vector.reciprocal, [r_rs], [r_rs], out=rs[:], in_=rs[:])
                            P.op("dve", nc.vector.scalar_tensor_tensor, [r_dd, r_rs, r_sm], [r_sq], out=sq[:], in0=dd[:], scalar=sm[:, 2:3], in1=rs[:],
                                 op0=ALU.mult, op1=ALU.mult)
                            P.op("dve", TT, [r_sq, r_siluz], [r_ysT], out=ysT[:, h, j * 512:(j + 1) * 512], in0=sq[:], in1=siluz[:, h, j * 512:(j + 1) * 512], op=ALU.mult)
                        g.fin = fin_a
                        groups.append(g)
                dense_attention(P, C, groups, pT, 0.125)
            wt, r_wt = ws.get(8)
            emit_silu_z(P, C, xT, r_xT, wt, r_wt, siluz, r_siluz)
            wt, r_wt = ws.get(1)
            for tt in range(NT):
                tq.run(xT, r_xT, wt, r_wt, tt, "C", 4, lambda j, tt=tt: (qT[:, j, tt * 128:(tt + 1) * 128], r_qT), g_bc=gq, r_g=r_gq)
            for kv in range(2):
                load_kv(4 + kv, 512 + kv * 128)
                groups = []
                for gg in range(2):
                    hq = kv * 2 + gg
                    for j in range(2):
                        g = Grp()
                        g.q = qT[:, hq, j * 512:(j + 1) * 512]
                        g.r_q = r_qT
                        g.kfn = lambda kc: ksb[:, kc * 128:(kc + 1) * 128]
                        g.r_kfn = lambda kc: r_kq[kc // 16]
                        g.vfn = lambda kc: vsb[:, kc, :]
                        g.r_vfn = lambda kc: r_vq[kc // 16]
                        g.nk = 64
                        g.ob, g.db = (3, 4) if ng % 2 == 0 else (5, 6)
                        ng += 1
                        g.fin = (lambda g=g, hq=hq, j=j: fin_simple(g.ob, g.db, 8 + hq, j))
                        groups.append(g)
                dense_attention(P, C, groups, pT, 128 ** -0.5)
            P.barrier()
            P.flush()

        sc128 = 128 ** -0.5
        with ExitStack() as ep:
            ksbs = [P.sb(ep, "ksbB%d" % i, [128, BW], BF16) for i in range(2)]
            vsbs = [P.sb(ep, "vsbB%d" % i, [128, 14, 128], BF16) for i in range(2)]
            bias = [P.sb(ep, "biasB%d" % i, [128, 7, 128], F32) for i in range(2)]
            tmp = [P.sb(ep, "tmpB%d" % i, [128, 7, 128], F32) for i in range(2)]
            pB = [P.sb(ep, "pB%d" % i, [128, 7, 128], BF16) for i in range(2)]
            fin = [P.sb(ep, "finB%d" % i, [128, 512], F32) for i in range(2)]
            qT, r_qT = P.sb(ep, "qTB", [128, 4, TPC], BF16)
            wt, r_wt = ws.get(7)
            emit_silu_z(P, C, xT, r_xT, wt, r_wt, siluz, r_siluz)
            wt, r_wt = ws.get(5)
            n = 0
            for j in range(4):
                for half in range(2):
                    bank = 4 + n % 2
                    n += 1
                    proj_feat(P, C, xT, r_xT, wt, r_wt, j, half, bank)
                    P.op("dve", nc.vector.tensor_copy, [C.rps[bank]], [r_qT], out=qT[:, j, half * 512:(half + 1) * 512], in_=C.ps[bank][:])
            itemsB = [(h, qt) for h in range(4) for qt in range(NT)]

            def b_stage1(i):
                h, qt = itemsB[i]
                ksb, r_ksb = ksbs[h % 2]
                vsb, r_vsb = vsbs[h % 2]
                if qt == 0:
                    P.dma("sp", ksb[:], D_["kT_Bw"][h], reads=[r_kv], writes=[r_ksb])
                    P.dma("sp", vsb[:], D_["v_Bw"][:, h * 128:(h + 1) * 128].rearrange("(ch p) d -> p ch d", p=128), reads=[r_kv], writes=[r_vsb])
                bt, r_bt = bias[i % 2]
                tp, r_tp = tmp[i % 2]
                pb, r_pb = pB[i % 2]
                P.dma("sp", bt[:], D_["biasB"][qt, h], reads=[r_kv], writes=[r_bt])
                sb0, sb1 = (0, 1) if i % 2 == 0 else (2, 7)
                qap = qT[:, h, qt * 128:(qt + 1) * 128]
                for kc in range(7):
                    bk = sb0 if kc < 4 else sb1
                    co = (kc % 4) * 128
                    P.op("pe", nc.tensor.matmul, [r_qT, r_ksb], [C.rps[bk]], inc=(kc in (3, 6)), out=C.ps[bk][:, co:co + 128],
                         lhsT=ksb[:, (qt + kc) * 128:(qt + kc + 1) * 128], rhs=qap, start=True, stop=True)
                P.op("dve", nc.vector.scalar_tensor_tensor, [C.rps[sb0], r_bt], [r_tp], out=tp[:, 0:4, :], in0=C.ps[sb0][:].rearrange("p (k q) -> p k q", k=4),
                     scalar=sc128, in1=bt[:, 0:4, :], op0=ALU.mult, op1=ALU.add)
                P.op("dve", nc.vector.scalar_tensor_tensor, [C.rps[sb1], r_bt], [r_tp], out=tp[:, 4:7, :], in0=C.ps[sb1][:, 0:384].rearrange("p (k q) -> p k q", k=3),
                     scalar=sc128, in1=bt[:, 4:7, :], op0=ALU.mult, op1=ALU.add)
                P.op("act", nc.scalar.activation, [r_tp], [r_pb], out=pb[:], in_=tp[:], func=AF.Exp)

            def b_stage2(i):
                h, qt = itemsB[i]
                vsb, r_vsb = vsbs[h % 2]
                pb, r_pb = pB[i % 2]
                jq = qt // 4
                ob, db = (3, 4) if (h * 2 + jq) % 2 == 0 else (5, 6)
                co = (qt % 4) * 128
                for kc in range(7):
                    P.op("pe", nc.tensor.matmul, [r_pb, r_vsb], [C.rps[ob]], inc=False, out=C.ps[ob][:, co:co + 128], lhsT=vsb[:, qt + kc, :], rhs=pb[:, kc, :],
                         start=(kc == 0), stop=(kc == 6))
                    P.op("pe", nc.tensor.matmul, [r_pb, C.r_onesb], [C.rps[db]], inc=(kc == 6), out=C.ps[db][:, co:co + 128], lhsT=C.onesb[:], rhs=pb[:, kc, :],
                         start=(kc == 0), stop=(kc == 6))
                if qt % 4 == 3:
                    fin_simple(ob, db, 4 + h, jq)

            b_stage1(0)
            for i in range(len(itemsB)):
                if i + 1 < len(itemsB):
                    b_stage1(i + 1)
                b_stage2(i)
            P.barrier()
            P.flush()

        with ExitStack() as ep:
            tq = TokQK(P, C, ep, tab, r_tab)
            qTD = big[:, 0:12 * TPC].rearrange("p (h t) -> p h t", h=12)
            r_qTD = r_big
            ksbs = [P.sb(ep, "ksbD%d" % i, [128, DW], BF16) for i in range(2)]
            vsbs = [P.sb(ep, "vsbD%d" % i, [128, 32, 128], BF16) for i in range(2)]
            mk, r_mk = P.sb(ep, "maskD", [128, NMASK, 128], F32)
            oacc, r_oacc = P.sb(ep, "oacc", [128, TPC], F32)
            dacc, r_dacc = P.sb(ep, "dacc", [128, TPC], F32)
            tmp = [P.sb(ep, "tmpD%d" % i, [128, 128], F32) for i in range(3)]
            pD = [P.sb(ep, "pD%d" % i, [128, 128], BF16) for i in range(3)]
            P.dma("sp", mk[:], D_["maskD"], writes=[r_mk])
            wt, r_wt = ws.get(9)
            emit_silu_z(P, C, xT, r_xT, wt, r_wt, siluz, r_siluz)
            for g in range(3):
                wt, r_wt = ws.get(2 + g)
                for tt in range(NT):
                    tq.run(xT, r_xT, wt, r_wt, tt, "D", 4, lambda j, tt=tt, g=g: (qTD[:, g * 4 + j, tt * 128:(tt + 1) * 128], r_qTD))
            mbase = [0, 16, 20]
            blocks = [(hh, g) for hh in range(4) for g in range(3)]

            def d_load(bi):
                hh, g = blocks[bi]
                ksb, r_ksb = ksbs[bi % 2]
                vsb, r_vsb = vsbs[bi % 2]
                r = DIL_RATES[g]
                M = {1: 9, 4: 3, 16: 2}[r]
                W = TPC + 128 * r
                P.dma("sp", ksb[:, 0:W], D_["kT_Dw"][g * 4 + hh, :, 1024 - 64 * r:1024 - 64 * r + W], reads=[r_kv], writes=[r_ksb])
                for rho in range(r):
                    for m in range(M):
                        u0 = -64 + 128 * m
                        nk = 64 if (r == 16 and m == 1) else 128
                        wl0 = 1024 + r * u0 + rho
                        P.dma("sp", vsb[0:nk, rho * M + m, :], D_["v_Dw"][ss(wl0, nk, r), g * 512 + hh * 128:g * 512 + (hh + 1) * 128],
                              reads=[r_kv], writes=[r_vsb])

            itc = [0]

            def d_compute(bi):
                hh, g = blocks[bi]
                ksb, r_ksb = ksbs[bi % 2]
                vsb, r_vsb = vsbs[bi % 2]
                r = DIL_RATES[g]
                n_ = TPC // r
                nq = min(128, n_)
                nqt = n_ // nq
                M = {1: 9, 4: 3, 16: 2}[r]
                ob0, db0 = 3, 5
                its = [(rho, qti, ch) for rho in range(r) for qti in range(nqt) for ch in range(2)]

                def s1(k):
                    rho, qti, ch = its[k]
                    U = qti * nq
                    idx = itc[0] + k
                    sbk = idx % 3
                    tp, r_tp = tmp[idx % 3]
                    pd, r_pd = pD[idx % 3]
                    u0 = U - 64 + 128 * ch
                    nk = 128 if (ch == 0 or nq == 128) else 64
                    qap = qTD[:, g * 4 + hh, ss(rho + r * U, nq, r)]
                    kap = ksb[:, ss(r * (u0 + 64) + rho, nk, r)]
                    P.op("pe", nc.tensor.matmul, [r_qTD, r_ksb], [C.rps[sbk]], out=C.ps[sbk][0:nk, 0:nq], lhsT=kap, rhs=qap, start=True, stop=True)
                    mi = mbase[g] + qti * 2 + ch
                    P.op("dve", nc.vector.scalar_tensor_tensor, [C.rps[sbk], r_mk], [r_tp], out=tp[0:nk, 0:nq], in0=C.ps[sbk][0:nk, 0:nq],
                         scalar=sc128, in1=mk[0:nk, mi, 0:nq], op0=ALU.mult, op1=ALU.add)
                    P.op("act", nc.scalar.activation, [r_tp], [r_pd], out=pd[0:nk, 0:nq], in_=tp[0:nk, 0:nq], func=AF.Exp)

                def s2(k):
                    rho, qti, ch = its[k]
                    U = qti * nq
                    idx = itc[0] + k
                    pd, r_pd = pD[idx % 3]
                    nk = 128 if (ch == 0 or nq == 128) else 64
                    col0 = rho * n_ + U
                    ob = ob0 + col0 // 512
                    db = db0 + col0 // 512
                    cofs = col0 % 512
                    vidx = rho * M + (qti + ch)
                    P.op("pe", nc.tensor.matmul, [r_pd, r_vsb], [C.rps[ob]], inc=False, out=C.ps[ob][:, cofs:cofs + nq], lhsT=vsb[0:nk, vidx, :],
                         rhs=pd[0:nk, 0:nq], start=(ch == 0), stop=(ch == 1))
                    P.op("pe", nc.tensor.matmul, [r_pd, C.r_onesb], [C.rps[db]], out=C.ps[db][:, cofs:cofs + nq], lhsT=C.onesb[0:nk, :],
                         rhs=pd[0:nk, 0:nq], start=(ch == 0), stop=(ch == 1))

                nI = len(its)
                s1(0)
                if nI > 1:
                    s1(1)
                for k in range(nI):
                    s2(k)
                    if k + 2 < nI:
                        s1(k + 2)
                itc[0] += nI
                for (acc, r_acc, b0) in ((oacc, r_oacc, ob0), (dacc, r_dacc, db0)):
                    for b in range(2):
                        if r == 1:
                            dst = acc[:, b * 512:(b + 1) * 512]
                            src = C.ps[b0 + b][:]
                        else:
                            dst = acc[:].rearrange("p (u r) -> p r u", r=r)[:, b * r // 2:(b + 1) * r // 2, :]
                            src = C.ps[b0 + b][:].rearrange("p (r u) -> p r u", u=n_)
                        if g == 0:
                            P.op("dve", nc.vector.tensor_copy, [C.rps[b0 + b]], [r_acc], out=dst, in_=src)
                        else:
                            P.op("dve", TT, [C.rps[b0 + b], r_acc], [r_acc], out=dst, in0=dst, in1=src, op=ALU.add)
                if g == 2:
                    P.op("dve", nc.vector.reciprocal, [r_dacc], [r_dacc], out=dacc[:], in_=dacc[:])
                    P.op("dve", TT, [r_oacc, r_dacc], [r_oacc], out=oacc[:], in0=oacc[:], in1=dacc[:], op=ALU.mult)
                    P.op("dve", TT, [r_oacc, r_siluz], [r_ysT], out=ysT[:, 12 + hh, :], in0=oacc[:], in1=siluz[:, hh, :], op=ALU.mult)

            d_load(0)
            for bi in range(len(blocks)):
                if bi + 1 < len(blocks):
                    d_load(bi + 1)
                d_compute(bi)
            P.barrier()
            P.flush()

        with ExitStack() as ep:
            wbt = [P.sb(ep, "wbt%d" % i, [128, 16, 128], BF16) for i in range(2)]
            gs = [P.sb(ep, "gs%d" % i, [128, 512], F32) for i in range(3)]
            macc = [P.sb(ep, "macc%d" % i, [128, 512], F32) for i in range(2)]
            mt = [P.sb(ep, "mt%d" % i, [128, 512], F32) for i in range(2)]
            r_wbr = P.res("wbr")
            P.dma("pool", wbt[0][0][:], D_["wbr"][0], reads=[r_wbr], writes=[wbt[0][1]])
            it = 0
            for dc in range(KC):
                wt, r_wt = ws.get(10 + dc)
                wb, r_wb = wbt[dc % 2]
                if dc + 1 < KC:
                    P.dma("pool", wbt[(dc + 1) % 2][0][:], D_["wbr"][dc + 1], reads=[r_wbr], writes=[wbt[(dc + 1) % 2][1]])
                for n in range(4):
                    for half in range(2):
                        gb = it % 4
                        pb = 4 + it % 4
                        g_, r_g_ = gs[it % 3]
                        it += 1
                        for kc in range(KC):
                            P.op("pe", nc.tensor.matmul, [r_xT, r_wt], [C.rps[gb]], inc=(kc == KC - 1), out=C.ps[gb][:], lhsT=wt[:, kc, n * 128:(n + 1) * 128],
                                 rhs=xT[:, kc, half * 512:(half + 1) * 512], start=(kc == 0), stop=(kc == KC - 1))
                        P.op("act", nc.scalar.activation, [C.rps[gb], r_bgT], [r_g_], out=g_[:], in_=C.ps[gb][:], func=AF.Sigmoid, bias=bgT[:, n * 16 + dc:n * 16 + dc + 1], scale=1.0)
                        for wc in range(4):
                            P.op("pe", nc.tensor.matmul, [r_ysT, r_wb], [C.rps[pb]], inc=(wc == 3), out=C.ps[pb][:], lhsT=wb[:, n * 4 + wc, :],
                                 rhs=ysT[:, n * 4 + wc, half * 512:(half + 1) * 512], start=(wc == 0), stop=(wc == 3))
                        ma, r_ma = macc[half]
                        if n == 0:
                            P.op("dve", TT, [C.rps[pb], r_g_], [r_ma], out=ma[:], in0=C.ps[pb][:], in1=g_[:], op=ALU.mult)
                        else:
                            t_, r_t_ = mt[half]
                            P.op("dve", TT, [C.rps[pb], r_g_], [r_t_], out=t_[:], in0=C.ps[pb][:], in1=g_[:], op=ALU.mult)
                            if n < 3:
                                P.op("dve", TT, [r_ma, r_t_], [r_ma], out=ma[:], in0=ma[:], in1=t_[:], op=ALU.add)
                            else:
                                P.op("dve", TT, [r_ma, r_t_], [r_mergedT], out=mergedT[:, dc, half * 512:(half + 1) * 512], in0=ma[:], in1=t_[:], op=ALU.add)
            P.barrier()
            P.flush()

        with ExitStack() as ep:
            wos = [(xT[:, :, 0:512], r_xT), (xT[:, :, 512:1024], r_xT), (ysT[:, :, 0:512], r_ysT), (ysT[:, :, 512:1024], r_ysT)]
            r_wod = P.res("wod")
            for cc in range(4):
                load_wtile(P, wos[cc][0], wos[cc][1], D_["wo"], r_wod, cc)
            gb, r_gb = P.sb(ep, "lng", [128, D], F32)
            bb, _ = P.sb(ep, "lnb", [128, D], F32)
            P.dma("sp", gb[:], D_["lng"], writes=[r_gb])
            P.dma("sp", bb[:], D_["lnb"], writes=[r_gb])
            xts = [P.sb(ep, "xt%d" % i, [128, D], F32) for i in range(2)]
            stats, r_stats = P.sb(ep, "stats", [128, 24], F32)
            mv, r_mv = P.sb(ep, "mv", [128, 4], F32)
            for tt in range(NT):
                xt, r_xt = xts[tt % 2]
                P.dma("sp", xt[:], x[tt * 128:(tt + 1) * 128, :], reads=[r_x], writes=[r_xt])
                for cc in range(4):
                    bank = (tt % 2) * 4 + cc
                    wo_, r_wo_ = wos[cc]
                    for dc in range(KC):
                        P.op("pe", nc.tensor.matmul, [r_mergedT, r_wo_], [C.rps[bank]], inc=(dc == KC - 1), out=C.ps[bank][:], lhsT=mergedT[:, dc, tt * 128:(tt + 1) * 128],
                             rhs=wo_[:, dc, :], start=(dc == 0), stop=(dc == KC - 1))
                    P.op("dve", nc.vector.scalar_tensor_tensor, [r_xt, C.rps[bank]], [r_xt], out=xt[:, cc * 512:(cc + 1) * 512], in0=xt[:, cc * 512:(cc + 1) * 512],
                         scalar=float(DN_ALPHA), in1=C.ps[bank][:], op0=ALU.mult, op1=ALU.add)
                layer_norm_tile(P, C, xt, r_xt, gb, bb, r_gb, stats, r_stats, mv, r_mv)
                P.dma("sp", D_["xo"][tt * 128:(tt + 1) * 128, :], xt[:], reads=[r_xt], writes=[r_xo], semres=r_xt)
            P.barrier()
            P.flush()


def build_T():
    nc = bass.Bass("TRN2", target_bir_lowering=False)

    def din(name, shape, dt=F32):
        return nc.dram_tensor(name, shape, dt, kind="ExternalInput").ap()
    D_ = {
        "x": din("x", [TPC, D]), "wT": din("wT", [26, 128, KC, 512]), "tab": din("tab", [128, NT, NTAB]), "gq": din("gq", [128, 128]),
        "lp": din("lp", [128, 256]), "cst": din("cst", [128, 4]), "subg": din("subg", [128, 1]), "bgT": din("bgT", [128, 64]),
        "wbr": din("wbr", [KC, 128, 16, 128]), "wo": din("wo", [4, 128, KC, 512]), "lng": din("lng", [128, D]), "lnb": din("lnb", [128, D]),
        "kT_AC": din("kT_AC", [6, 128, S], BF16), "v_AC": din("v_AC", [S, 768], BF16),
        "kT_Bw": din("kT_Bw", [4, 128, BW], BF16), "v_Bw": din("v_Bw", [BW, 512], BF16), "biasB": din("biasB", [NT, 4, 128, 7, 128]),
        "kT_Dw": din("kT_Dw", [12, 128, DW], BF16), "v_Dw": din("v_Dw", [DW, 1536], BF16), "maskD": din("maskD", [128, NMASK, 128]),
    }
    D_["xo"] = nc.dram_tensor("xo", [TPC, D], F32, kind="ExternalOutput").ap()
    with ExitStack() as es:
        P = Prog(nc, es)
        C = setup_common(P, es)
        emit_T(P, C, es, D_)
    return nc


O_AQ, O_AK, O_AV, O_AZ = 0, 512, 1024, 1536
O_BQ, O_BK, O_BV, O_BZ = 2048, 2560, 3072, 3584
O_CQ, O_CK, O_CV, O_CZ = 4096, 4608, 4864, 5120
O_DQ, O_DK, O_DV, O_DZ = 5632, 7168, 8704, 10240
O_GL = 10752


def tileize(wcols):
    n = wcols.shape[1] // 512
    return np.ascontiguousarray(wcols.reshape(KC, 128, n, 512).transpose(2, 1, 0, 3))


def rep128(vec):
    return np.ascontiguousarray(np.broadcast_to(np.asarray(vec, np.float32).reshape(1, -1), (128, vec.size)))


def make_tabs():
    t = np.arange(S, dtype=np.int64)

    def cs(pos, half, dim, theta):
        inv = np.power(np.float32(theta), -np.arange(half, dtype=np.float32) * np.float32(2.0) / np.float32(dim)).astype(np.float32)
        ang = pos.astype(np.float32)[:, None] * inv[None, :]
        return np.cos(ang).astype(np.float32), np.sin(ang).astype(np.float32)
    cA, sA = cs(t, 8, 16, 500000.0)
    cD, sD = cs(t, 16, 32, 500000.0)
    cR, sR = cs(t // 64, 32, 64, 10000.0)
    cC, sC = cs(t % 64, 32, 64, 10000.0)
    full = np.concatenate([cA, sA, cD, sD, cR, sR, cC, sC], axis=1)
    out = []
    for c in range(NCORE):
        blk = full[c * TPC:(c + 1) * TPC].reshape(NT, 128, NTAB).transpose(1, 0, 2)
        out.append(np.ascontiguousarray(blk))
    return out


def wP_of(w):
    cols = np.concatenate([w[:, O_AK:O_AK + 512], w[:, O_DK:O_DK + 1536], w[:, O_BK:O_BK + 512],
                           w[:, O_CK:O_CK + 256], w[:, O_CV:O_CV + 256],
                           w[:, O_AV:O_AV + 512], w[:, O_BV:O_BV + 512], w[:, O_DV:O_DV + 1536]], axis=1)
    return tileize(cols)


_NC_CACHE = {}


def get_nc(name):
    if name not in _NC_CACHE:
        _NC_CACHE[name] = {"E": build_E, "P": build_P, "T": build_T}[name]()
    return _NC_CACHE[name]


def run_E(x2d, g, b):
    nc = get_nc("E")
    gb, bb = rep128(g), rep128(b)
    maps = [{"x": np.ascontiguousarray(x2d[c * TPC:(c + 1) * TPC]), "g": gb, "b": bb} for c in range(NCORE)]
    res = run_bass_kernel_spmd(nc, maps, core_ids=list(range(NCORE)))
    return np.concatenate([r["xo"] for r in res.results], axis=0)


def run_P(xcur, w, kn_g, tabs):
    nc = get_nc("P")
    wP = wP_of(w)
    gk = rep128(kn_g)
    maps = [{"x": np.ascontiguousarray(xcur[c * TPC:(c + 1) * TPC]), "wP": wP, "tab": tabs[c], "gk": gk} for c in range(NCORE)]
    res = run_bass_kernel_spmd(nc, maps, core_ids=list(range(NCORE)))
    kT = np.concatenate([r["kT"] for r in res.results], axis=2)
    v = np.concatenate([r["v"] for r in res.results], axis=0)
    return kT, v


def wT_of(w):
    gl = w[:, O_GL:O_GL + 8192].reshape(D, 4, KC, 128).transpose(0, 2, 1, 3).reshape(D, 8192)
    cols = np.concatenate([w[:, O_AQ:O_AQ + 512], w[:, O_CQ:O_CQ + 512], w[:, O_DQ:O_DQ + 1536], w[:, O_BQ:O_BQ + 512],
                           w[:, O_AZ:O_AZ + 512], w[:, O_BZ:O_BZ + 512], w[:, O_CZ:O_CZ + 512], w[:, O_DZ:O_DZ + 512], gl], axis=1)
    return tileize(cols)


def window(arr, axis, start, length):
    n = arr.shape[axis]
    lo, hi = max(start, 0), min(start + length, n)
    shp = list(arr.shape)
    shp[axis] = length
    out = np.zeros(shp, arr.dtype)
    sl_src = [slice(None)] * arr.ndim
    sl_dst = [slice(None)] * arr.ndim
    sl_src[axis] = slice(lo, hi)
    sl_dst[axis] = slice(lo - start, hi - start)
    out[tuple(sl_dst)] = arr[tuple(sl_src)]
    return out


def make_biasB_index():
    out = []
    for c in range(NCORE):
        qt = np.arange(NT)[:, None, None, None]
        a = np.arange(128)[None, :, None, None]
        kc = np.arange(7)[None, None, :, None]
        b = np.arange(128)[None, None, None, :]
        tk = TPC * c - 384 + 128 * (qt + kc) + a
        tq = TPC * c + 128 * qt + b
        tk, tq = np.broadcast_arrays(tk, tq)
        inr = (tk >= 0) & (tk < S)
        kr, kcol = tk // 64, tk % 64
        qr, qc = tq // 64, tq % 64
        r0 = np.clip(qr - 4, 0, 120)
        c0 = np.clip(qc - 8, 0, 48)
        valid = inr & (kr >= r0) & (kr < r0 + 8) & (kcol >= c0) & (kcol < c0 + 16)
        dr = np.clip(kr - qr + 7, 0, 14)
        dc = np.clip(kcol - qc + 15, 0, 30)
        out.append((np.where(valid, dr * 31 + dc, 0).astype(np.int64), valid))
    return out


def make_biasB(rpb_l, bidx):
    idx, valid = bidx
    flat = rpb_l.reshape(4, 15 * 31)
    g = flat[:, idx]
    g = np.where(valid[None], g, np.float32(NEG)).astype(np.float32)
    return np.ascontiguousarray(g.transpose(1, 0, 2, 3, 4))


def make_maskD():
    out = []
    for c in range(NCORE):
        m = np.full((128, NMASK, 128), NEG, np.float32)
        a = np.arange(128)[:, None]
        b = np.arange(128)[None, :]
        base = [0, 16, 20]
        for g, r in enumerate(DIL_RATES):
            n = TPC // r
            nq = min(128, n)
            for qti in range(n // nq):
                U = qti * nq
                for ch in range(2):
                    u = U - 64 + 128 * ch + a
                    uq = U + b
                    ug = n * c + u
                    valid = (np.abs(u - uq) <= 64) & (ug >= 0) & (ug < S // r)
                    m[:, base[g] + qti * 2 + ch, :] = np.where(valid, 0.0, NEG)
        out.append(m)
    return out


def run_T(xcur, l, inp, kT, v, tabs, bidx, maskD):
    nc = get_nc("T")
    w = inp["w_in"][l]
    lam_init = 0.8 - 0.6 * math.exp(-0.3 * l)
    common = {
        "wT": wT_of(w), "gq": rep128(inp["gqa_q_norm_g"][l]), "lp": rep128(inp["diff_lambda"][l].reshape(-1)),
        "cst": rep128(np.array([lam_init, 1.0 - lam_init, 0.0, 0.0], np.float32)),
        "subg": np.ascontiguousarray(inp["diff_subln_g"][l].reshape(128, 1)),
        "bgT": np.ascontiguousarray(inp["b_gate"][l].reshape(4, KC, 128).transpose(2, 0, 1).reshape(128, 64)),
        "wbr": np.ascontiguousarray(inp["w_branch"][l].reshape(4, 4, 128, KC, 128).transpose(3, 2, 0, 1, 4).reshape(KC, 128, 16, 128)),
        "wo": tileize(inp["w_out"][l]), "lng": rep128(inp["ln_g"][l]), "lnb": rep128(inp["ln_b"][l]),
        "kT_AC": np.ascontiguousarray(kT[0:6]), "v_AC": np.ascontiguousarray(np.concatenate([v[:, 0:512], v[:, 1024:1280]], axis=1)),
    }
    kT_B, v_B = kT[18:22], v[:, 512:1024]
    kT_D, v_D = kT[6:18], v[:, 1280:2816]
    maps = []
    for c in range(NCORE):
        m = dict(common)
        m["x"] = np.ascontiguousarray(xcur[c * TPC:(c + 1) * TPC])
        m["tab"] = tabs[c]
        m["kT_Bw"] = window(kT_B, 2, TPC * c - 384, BW)
        m["v_Bw"] = window(v_B, 0, TPC * c - 384, BW)
        m["biasB"] = make_biasB(inp["nat_rpb"][l], bidx[c])
        m["kT_Dw"] = window(kT_D, 2, TPC * c - 1024, DW)
        m["v_Dw"] = window(v_D, 0, TPC * c - 1024, DW)
        m["maskD"] = maskD[c]
        maps.append(m)
    res = run_bass_kernel_spmd(nc, maps, core_ids=list(range(NCORE)))
    return np.concatenate([r["xo"] for r in res.results], axis=0)


def kernel(**inputs):
    inp = {k: np.asarray(v) for k, v in inputs.items()}
    x = np.ascontiguousarray(inp["x"].reshape(S, D).astype(np.float32, copy=False))
    tabs = make_tabs()
    bidx = make_biasB_index()
    maskD = make_maskD()
    xcur = run_E(x, inp["emb_ln_g"], inp["emb_ln_b"])
    for l in range(DEPTH):
        kT, v = run_P(xcur, inp["w_in"][l], inp["gqa_k_norm_g"][l], tabs)
        xcur = run_T(xcur, l, inp, kT, v, tabs, bidx, maskD)
    return xcur.reshape(1, S, D).astype(np.float32)
```

```python
import math
from contextlib import ExitStack
import numpy as np
import ml_dtypes
import concourse.bass as bass
import concourse.mybir as mybir
from concourse.bass_utils import run_bass_kernel_spmd

F32 = mybir.dt.float32
BF16 = mybir.dt.bfloat16
AF = mybir.ActivationFunctionType
ALU = mybir.AluOpType
NPBF = ml_dtypes.bfloat16

NCORE = 8
S = 8192
D = 2048
TPC = 1024
NT = 8
KC = 16
DEPTH = 4
NEG = -30000.0
LN_EPS = 1e-5
RMS_EPS = 1e-6
DN_ALPHA = (2 * DEPTH) ** 0.25
DIL_RATES = (1, 4, 16)


def ss(start, n, step=1):
    return slice(start, start + step * (n - 1) + 1, step)


class Res:
    __slots__ = ("name", "ws", "rs", "dsem")

    def __init__(self, name):
        self.name = name
        self.ws = []
        self.rs = []
        self.dsem = None


class Prog:
    ENGS = ("pe", "act", "dve", "pool", "sp")

    def __init__(self, nc, es):
        self.nc = nc
        self.es = es
        self.eng = {"pe": nc.tensor, "act": nc.scalar, "dve": nc.vector, "pool": nc.gpsimd, "sp": nc.sync}
        self.streams = {e: [] for e in self.ENGS}
        self.cnt = {e: 0 for e in self.ENGS}
        self.sem = {}
        self.semtot = {}
        for e in self.ENGS:
            self.sem[e] = es.enter_context(nc.semaphore("s_" + e))
        self.waited = {e: {} for e in self.ENGS}
        self.ndsem = 0
        self.pending_noinc = {e: False for e in self.ENGS}
        self.nres = 0

    def res(self, name="r"):
        self.nres += 1
        return Res(name + str(self.nres))

    def sb(self, es, name, shape, dt):
        self.nres += 1
        t = es.enter_context(self.nc.sbuf_tensor("sb_%s_%d" % (name, self.nres), shape, dt))
        return t, self.res(name)

    def dsem_of(self, r):
        if r.dsem is None:
            key = "d%d" % self.ndsem
            self.ndsem += 1
            self.sem[key] = self.es.enter_context(self.nc.semaphore("sd%d" % self.ndsem))
            self.semtot[key] = 0
            r.dsem = key
        return r.dsem

    def _wait(self, e, deps):
        need = {}
        for (k, v) in deps:
            if k == e and e == "pe":
                continue
            if self.waited[e].get(k, 0) >= v:
                continue
            if need.get(k, 0) < v:
                need[k] = v
        for k, v in need.items():
            self.waited[e][k] = v
            sem = self.sem[k]
            eng = self.eng[e]
            self.streams[e].append(lambda eng=eng, sem=sem, v=v: eng.wait_ge(sem, v))

    @staticmethod
    def _deps(reads, writes):
        deps = []
        for r in reads:
            deps.extend(r.ws)
        for r in writes:
            deps.extend(r.ws)
            deps.extend(r.rs)
        return deps

    def op(self, e, meth, reads=(), writes=(), inc=True, **kw):
        self._wait(e, self._deps(reads, writes))
        sem = self.sem[e]
        if inc:
            self.streams[e].append(lambda meth=meth, kw=kw, sem=sem: meth(**kw).then_inc(sem, 1))
            self.cnt[e] += 1
            tok = (e, self.cnt[e])
            self.pending_noinc[e] = False
        else:
            assert e == "pe"
            self.streams[e].append(lambda meth=meth, kw=kw: meth(**kw))
            tok = (e, self.cnt[e] + 1)
            self.pending_noinc[e] = True
        for r in reads:
            r.rs.append(tok)
            if len(r.rs) > 64:
                r.rs = self._compact(r.rs)
        for r in writes:
            r.ws = self._compact(r.ws + [tok])
            r.rs = []
        return tok

    @staticmethod
    def _compact(toks):
        best = {}
        for k, v in toks:
            if best.get(k, 0) < v:
                best[k] = v
        return list(best.items())

    def dma(self, q, out, in_, reads=(), writes=(), semres=None, multi=False, **kw):
        self._wait(q, self._deps(reads, writes))
        if semres is None:
            semres = writes[0] if writes else reads[0]
        key = self.dsem_of(semres)
        self.semtot[key] += 16
        tok = (key, self.semtot[key])
        sem = self.sem[key]
        eng = self.eng[q]
        self.streams[q].append(lambda eng=eng, out=out, in_=in_, kw=kw, sem=sem: eng.dma_start(
            out=(out() if callable(out) else out), in_=(in_() if callable(in_) else in_), **kw).then_inc(sem, 16))
        for r in reads:
            r.rs.append(tok)
            if len(r.rs) > 64:
                r.rs = self._compact(r.rs)
        for r in writes:
            r.ws = self._compact(r.ws + [tok])
            r.rs = []
        return tok

    def raw(self, e, fn):
        self.streams[e].append(fn)

    def coll(self, kind, ins, outs, reads=(), writes=()):
        self._wait("pool", self._deps(reads, writes))
        key = self.dsem_of(writes[0])
        self.semtot[key] += 16
        tok = (key, self.semtot[key])
        sem = self.sem[key]
        nc = self.nc
        self.streams["pool"].append(lambda: nc.gpsimd.collective_compute(kind, ALU.bypass, replica_groups=[list(range(NCORE))], ins=ins, outs=outs).then_inc(sem, 16))
        for r in reads:
            r.rs.append(tok)
        for r in writes:
            r.ws = self._compact(r.ws + [tok])
            r.rs = []
        return tok

    def barrier(self):
        assert not any(self.pending_noinc.values())
        deps = [(e, self.cnt[e]) for e in ("pe", "act", "dve", "pool") if self.cnt[e] > 0]
        deps += [(k, v) for k, v in self.semtot.items() if v > 0]
        for e in self.ENGS:
            self._wait(e, [d for d in deps if d[0] != e])

    def flush(self):
        assert not any(self.pending_noinc.values())
        nc = self.nc
        st = self.streams
        with nc.Block() as block:
            @block.tensor
            def _(e):
                for f in st["pe"]:
                    f()

            @block.scalar
            def _(e):
                for f in st["act"]:
                    f()

            @block.vector
            def _(e):
                for f in st["dve"]:
                    f()

            @block.gpsimd
            def _(e):
                for f in st["pool"]:
                    f()

            @block.sync
            def _(e):
                for f in st["sp"]:
                    f()
        self.streams = {e: [] for e in self.ENGS}


class Ctx:
    pass


def setup_common(P, es):
    nc = P.nc
    C = Ctx()
    C.ps = []
    C.rps = []
    for i in range(8):
        t = es.enter_context(nc.psum_tensor("psb%d" % i, [128, 512], F32))
        C.ps.append(t)
        C.rps.append(P.res("ps"))
    C.ident, C.r_ident = P.sb(es, "ident", [128, 128], BF16)
    C.onesb, C.r_onesb = P.sb(es, "onesb", [128, 128], BF16)
    C.onesf, C.r_onesf = P.sb(es, "onesf", [128, 128], F32)
    C.eps_ln, C.r_eps_ln = P.sb(es, "epsln", [128, 1], F32)
    C.eps_rms, C.r_eps_rms = P.sb(es, "epsrms", [128, 1], F32)
    P.op("pool", nc.gpsimd.memset, [], [C.r_ident], ap=C.ident[:], constant=0.0)
    P.op("pool", nc.gpsimd.affine_select, [C.r_ident], [C.r_ident], out=C.ident[:], in_=C.ident[:], compare_op=ALU.not_equal, fill=1.0,
         base=0, pattern=[[-1, 128]], channel_multiplier=1)
    P.op("pool", nc.gpsimd.memset, [], [C.r_onesb], ap=C.onesb[:], constant=1.0)
    P.op("pool", nc.gpsimd.memset, [], [C.r_onesf], ap=C.onesf[:], constant=1.0)
    P.op("pool", nc.gpsimd.memset, [], [C.r_eps_ln], ap=C.eps_ln[:], constant=LN_EPS)
    P.op("pool", nc.gpsimd.memset, [], [C.r_eps_rms], ap=C.eps_rms[:], constant=RMS_EPS)
    return C


def build_xT(P, C, es, x_dram, r_x, xT, r_xT):
    nc = P.nc
    xb = [P.sb(es, "xb%d" % i, [128, D], BF16) for i in range(2)]
    for tt in range(NT):
        t, r = xb[tt % 2]
        P.dma("pool", t[:], x_dram[tt * 128:(tt + 1) * 128, :], reads=[r_x], writes=[r])
        for half in range(2):
            bank = 6 + half
            pst = C.ps[bank][:].bitcast(BF16)
            for k8 in range(8):
                kc = half * 8 + k8
                P.op("pe", nc.tensor.transpose, [r, C.r_ident], [C.rps[bank]], inc=(k8 == 7),
                     out=pst[:, k8 * 128:(k8 + 1) * 128], in_=t[:, kc * 128:(kc + 1) * 128], identity=C.ident[:])
            dst = xT[:, half * 8:(half + 1) * 8, tt * 128:(tt + 1) * 128]
            src = pst.rearrange("p (k t) -> p k t", k=8)
            if half == 0:
                P.op("dve", nc.vector.tensor_copy, [C.rps[bank]], [r_xT], out=dst, in_=src)
            else:
                P.op("act", nc.scalar.copy, [C.rps[bank]], [r_xT], out=dst, in_=src)


def load_wtile(P, wt, r_wt, w_dram, r_w, idx):
    P.dma("pool", wt[:, 0:8, :], w_dram[idx, :, 0:8, :], reads=[r_w], writes=[r_wt])
    P.dma("pool", wt[:, 8:16, :], w_dram[idx, :, 8:16, :], reads=[r_w], writes=[r_wt], multi=True)


def proj_tok(P, C, xT, r_xT, wt, r_wt, tt, bank, ncols=512):
    nc = P.nc
    for kc in range(KC):
        P.op("pe", nc.tensor.matmul, [r_xT, r_wt], [C.rps[bank]], inc=(kc == KC - 1),
             out=C.ps[bank][:, 0:ncols], lhsT=xT[:, kc, tt * 128:(tt + 1) * 128], rhs=wt[:, kc, 0:ncols], start=(kc == 0), stop=(kc == KC - 1))


def proj_feat(P, C, xT, r_xT, wt, r_wt, j, half, bank):
    nc = P.nc
    for kc in range(KC):
        P.op("pe", nc.tensor.matmul, [r_xT, r_wt], [C.rps[bank]], inc=(kc == KC - 1),
             out=C.ps[bank][:], lhsT=wt[:, kc, j * 128:(j + 1) * 128], rhs=xT[:, kc, half * 512:(half + 1) * 512], start=(kc == 0), stop=(kc == KC - 1))


def rope_tok(P, src, r_src, dst, r_dst, H, a, m, cos, sin, r_tab, t1, t2, r_t1, r_t2):
    nc = P.nc
    s3 = src.rearrange("p (h d) -> p h d", h=H)
    d3 = dst.rearrange("p (h d) -> p h d", h=H)
    x1 = s3[:, :, a:a + m]
    x2 = s3[:, :, a + m:a + 2 * m]
    cb = cos.unsqueeze(1).broadcast_to([128, H, m])
    sb_ = sin.unsqueeze(1).broadcast_to([128, H, m])
    u1 = t1[:, 0:H * m].rearrange("p (h d) -> p h d", h=H)
    u2 = t2[:, 0:H * m].rearrange("p (h d) -> p h d", h=H)
    TT = nc.vector.tensor_tensor
    P.op("dve", TT, [r_src, r_tab], [r_t1], out=u1, in0=x1, in1=cb, op=ALU.mult)
    P.op("dve", TT, [r_src, r_tab], [r_t2], out=u2, in0=x2, in1=sb_, op=ALU.mult)
    P.op("dve", TT, [r_t1, r_t2], [r_dst], out=d3[:, :, a:a + m], in0=u1, in1=u2, op=ALU.subtract)
    P.op("dve", TT, [r_src, r_tab], [r_t1], out=u1, in0=x2, in1=cb, op=ALU.mult)
    P.op("dve", TT, [r_src, r_tab], [r_t2], out=u2, in0=x1, in1=sb_, op=ALU.mult)
    P.op("dve", TT, [r_t1, r_t2], [r_dst], out=d3[:, :, a + m:a + 2 * m], in0=u1, in1=u2, op=ALU.add)


T_CA, T_SA, T_CD, T_SD, T_CR, T_SR, T_CC, T_SC = 0, 8, 16, 32, 48, 80, 112, 144
NTAB = 176


class TokQK:
    def __init__(self, P, C, es, tab, r_tab):
        self.P, self.C, self.tab, self.r_tab = P, C, tab, r_tab
        self.hsb = [P.sb(es, "hsb%d" % i, [128, 512], F32) for i in range(2)]
        self.hn = [P.sb(es, "hn%d" % i, [128, 512], F32) for i in range(2)]
        self.qb = [P.sb(es, "qb%d" % i, [128, 512], BF16) for i in range(2)]
        self.t1 = [P.sb(es, "rt1%d" % i, [128, 256], F32) for i in range(2)]
        self.t2 = [P.sb(es, "rt2%d" % i, [128, 256], F32) for i in range(2)]
        self.ssq, self.r_ssq = P.sb(es, "ssq", [128, 8], F32)
        self.junk, self.r_junk = P.sb(es, "junk", [128, 128], F32)
        self.n = 0

    def run(self, xT, r_xT, wt, r_wt, tt, kind, nheads, dst_fn, g_bc=None, r_g=None, extra_v=None):
        P, C, nc = self.P, self.C, self.P.nc
        i = self.n % 2
        self.n += 1
        bank = 4 + i
        proj_tok(P, C, xT, r_xT, wt, r_wt, tt, bank)
        hsb, r_hsb = self.hsb[i]
        hn, r_hn = self.hn[i]
        qb, r_qb = self.qb[i]
        t1, r_t1 = self.t1[i]
        t2, r_t2 = self.t2[i]
        tab, r_tab = self.tab, self.r_tab
        ps = C.ps[bank]
        nq = nheads * 128 if kind == "C" else 512
        P.op("act", nc.scalar.copy, [C.rps[bank]], [r_hsb], out=hsb[:], in_=ps[:])
        if extra_v is not None:
            extra_v(hsb, r_hsb)
        if kind in ("A", "D"):
            P.op("act", nc.scalar.copy, [r_hsb], [r_qb], out=qb[:], in_=hsb[:])
            if kind == "A":
                rope_tok(P, hsb[:], r_hsb, qb[:], r_qb, 8, 0, 8, tab[:, tt, T_CA:T_CA + 8], tab[:, tt, T_SA:T_SA + 8], r_tab, t1, t2, r_t1, r_t2)
            else:
                rope_tok(P, hsb[:], r_hsb, qb[:], r_qb, 4, 0, 16, tab[:, tt, T_CD:T_CD + 16], tab[:, tt, T_SD:T_SD + 16], r_tab, t1, t2, r_t1, r_t2)
        else:
            ssq, r_ssq = self.ssq, self.r_ssq
            P.op("dve", nc.vector.tensor_tensor, [r_hsb], [r_hn], out=hn[:, 0:nq], in0=hsb[:, 0:nq], in1=hsb[:, 0:nq], op=ALU.mult)
            P.op("dve", nc.vector.tensor_reduce, [r_hn], [r_ssq], out=ssq[:, 0:nheads], in_=hn[:, 0:nq].rearrange("p (h d) -> p h d", h=nheads),
                 axis=mybir.AxisListType.X, op=ALU.add)
            P.op("act", nc.scalar.activation, [r_ssq, C.r_eps_rms], [r_ssq], out=ssq[:, 0:nheads], in_=ssq[:, 0:nheads], func=AF.Sqrt,
                 scale=1.0 / 128, bias=C.eps_rms[:])
            P.op("dve", nc.vector.reciprocal, [r_ssq], [r_ssq], out=ssq[:, 0:nheads], in_=ssq[:, 0:nheads])
            for h in range(nheads):
                P.op("dve", nc.vector.scalar_tensor_tensor, [r_hsb, r_ssq, r_g], [r_hn], out=hn[:, h * 128:(h + 1) * 128],
                     in0=hsb[:, h * 128:(h + 1) * 128], scalar=ssq[:, h:h + 1], in1=g_bc[:], op0=ALU.mult, op1=ALU.mult)
            rope_tok(P, hn[:, 0:nq], r_hn, qb[:, 0:nq], r_qb, nheads, 0, 32, tab[:, tt, T_CR:T_CR + 32], tab[:, tt, T_SR:T_SR + 32], r_tab, t1, t2, r_t1, r_t2)
            rope_tok(P, hn[:, 0:nq], r_hn, qb[:, 0:nq], r_qb, nheads, 64, 32, tab[:, tt, T_CC:T_CC + 32], tab[:, tt, T_SC:T_SC + 32], r_tab, t1, t2, r_t1, r_t2)
        nj = nq // 128
        tb = 6 + i
        pst = C.ps[tb][:].bitcast(BF16)
        for j in range(nj):
            P.op("pe", nc.tensor.transpose, [r_qb, C.r_ident], [C.rps[tb]], inc=(j == nj - 1),
                 out=pst[:, j * 128:(j + 1) * 128], in_=qb[:, j * 128:(j + 1) * 128], identity=C.ident[:])
        ncp = 0
        for j in range(nj):
            dl = dst_fn(j)
            if isinstance(dl, tuple):
                dl = [(dl[0], dl[1], 0, 128)]
            for (dst, r_dst, p0, p1) in dl:
                if ncp % 2 == 0:
                    P.op("dve", nc.vector.tensor_copy, [C.rps[tb]], [r_dst], out=dst, in_=pst[p0:p1, j * 128:(j + 1) * 128])
                else:
                    P.op("act", nc.scalar.copy, [C.rps[tb]], [r_dst], out=dst, in_=pst[p0:p1, j * 128:(j + 1) * 128])
                ncp += 1


def layer_norm_tile(P, C, xt, r_xt, g_bc, b_bc, r_gb, stats, r_stats, mv, r_mv):
    nc = P.nc
    for c in range(4):
        P.op("dve", nc.vector.bn_stats, [r_xt], [r_stats], out=stats[:, c * 6:(c + 1) * 6], in_=xt[:, c * 512:(c + 1) * 512])
    P.op("dve", nc.vector.bn_aggr, [r_stats], [r_mv], out=mv[:, 0:2], in_=stats[:, 0:24])
    P.op("act", nc.scalar.activation, [r_mv, C.r_eps_ln], [r_mv], out=mv[:, 2:3], in_=mv[:, 1:2], func=AF.Sqrt, scale=1.0, bias=C.eps_ln[:])
    P.op("dve", nc.vector.reciprocal, [r_mv], [r_mv], out=mv[:, 2:3], in_=mv[:, 2:3])
    P.op("dve", nc.vector.tensor_scalar, [r_xt, r_mv], [r_xt], out=xt[:], in0=xt[:], scalar1=mv[:, 0:1], scalar2=mv[:, 2:3],
         op0=ALU.subtract, op1=ALU.mult)
    P.op("pool", nc.gpsimd.tensor_tensor, [r_xt, r_gb], [r_xt], out=xt[:], in0=xt[:], in1=g_bc[:], op=ALU.mult)
    P.op("dve", nc.vector.tensor_tensor, [r_xt, r_gb], [r_xt], out=xt[:], in0=xt[:], in1=b_bc[:], op=ALU.add)


def emit_E(P, C, es, x, r_in, g, b, xo, r_out):
    gb, r_gb = P.sb(es, "gb", [128, D], F32)
    bb, _ = P.sb(es, "bb", [128, D], F32)
    P.dma("sp", gb[:], g, writes=[r_gb])
    P.dma("sp", bb[:], b, writes=[r_gb], multi=True)
    xts = [P.sb(es, "xt%d" % i, [128, D], F32) for i in range(2)]
    stats, r_stats = P.sb(es, "stats", [128, 24], F32)
    mv, r_mv = P.sb(es, "mv", [128, 4], F32)
    for tt in range(NT):
        xt, r_xt = xts[tt % 2]
        P.dma("sp", xt[:], x[tt * 128:(tt + 1) * 128, :], reads=[r_in], writes=[r_xt])
        layer_norm_tile(P, C, xt, r_xt, gb, bb, r_gb, stats, r_stats, mv, r_mv)
        P.dma("sp", xo[tt * 128:(tt + 1) * 128, :], xt[:], reads=[r_xt], writes=[r_out], semres=r_xt, multi=True)


def build_E():
    nc = bass.Bass("TRN2", target_bir_lowering=False)
    x = nc.dram_tensor("x", [TPC, D], F32, kind="ExternalInput").ap()
    g = nc.dram_tensor("g", [128, D], F32, kind="ExternalInput").ap()
    b = nc.dram_tensor("b", [128, D], F32, kind="ExternalInput").ap()
    xo = nc.dram_tensor("xo", [TPC, D], F32, kind="ExternalOutput").ap()
    with ExitStack() as es:
        P = Prog(nc, es)
        C = setup_common(P, es)
        emit_E(P, C, es, x, P.res("in"), g, b, xo, P.res("out"))
        P.barrier()
        P.flush()
    return nc


NKT = 22
NV = 2816


def emit_P(P, C, es, x, r_x, wP, r_wP, tabd, gkd, kT, r_kT, v, r_v):
    nc = P.nc
    xT, r_xT = P.sb(es, "xT", [128, KC, TPC], BF16)
    tab, r_tab = P.sb(es, "tab", [128, NT, NTAB], F32)
    gk, r_gk = P.sb(es, "gk", [128, 128], F32)
    P.dma("sp", tab[:], tabd, writes=[r_tab])
    P.dma("sp", gk[:], gkd, writes=[r_gk])
    build_xT(P, C, es, x, r_x, xT, r_xT)
    wts = [P.sb(es, "wt%d" % i, [128, KC, 512], BF16) for i in range(2)]
    kst = [P.sb(es, "kst%d" % i, [128, 4, TPC], BF16) for i in range(2)]
    vst = [P.sb(es, "vst%d" % i, [128, 512], BF16) for i in range(3)]
    tq = TokQK(P, C, es, tab, r_tab)
    order = [0, 1, 2, 3, 5, 4, 6, 7, 8, 9, 10]
    load_wtile(P, wts[0][0], wts[0][1], wP, r_wP, order[0])
    nks = 0
    nvs = 0
    for oi, ti in enumerate(order):
        wt, r_wt = wts[oi % 2]
        if oi + 1 < len(order):
            load_wtile(P, wts[(oi + 1) % 2][0], wts[(oi + 1) % 2][1], wP, r_wP, order[oi + 1])
        if ti in (0, 1, 2, 3, 5):
            ks, r_ks = kst[nks % 2]
            nks += 1
            if ti == 5:
                nch, ch0 = 2, 4
            else:
                nch, ch0 = 4, (0 if ti == 0 else 6 + (ti - 1) * 4)
            for tt in range(NT):
                dst_fn = (lambda j, ks=ks, r_ks=r_ks, tt=tt: (ks[:, j, tt * 128:(tt + 1) * 128], r_ks))
                if ti == 5:
                    vs, r_vs = vst[nvs % 3]
                    nvs += 1

                    def extra(hsb, r_hsb, vs=vs, r_vs=r_vs, tt=tt):
                        P.op("act", nc.scalar.copy, [r_hsb], [r_vs], out=vs[:, 0:256], in_=hsb[:, 256:512])
                        P.dma("sp", v[tt * 128:(tt + 1) * 128, 1024:1280], vs[:, 0:256], reads=[r_vs], writes=[r_v], semres=r_vs, multi=True)
                    tq.run(xT, r_xT, wt, r_wt, tt, "C", 2, dst_fn, g_bc=gk, r_g=r_gk, extra_v=extra)
                else:
                    tq.run(xT, r_xT, wt, r_wt, tt, "A" if ti == 0 else "D", 4, dst_fn)
            for j in range(nch):
                P.dma("sp", kT[ch0 + j], ks[:, j, :], reads=[r_ks], writes=[r_kT], semres=r_ks, multi=True)
        elif ti == 4:
            ks, r_ks = kst[nks % 2]
            nks += 1
            n = 0
            for j in range(4):
                for half in range(2):
                    bank = 4 + n % 2
                    n += 1
                    proj_feat(P, C, xT, r_xT, wt, r_wt, j, half, bank)
                    if n % 2 == 0:
                        P.op("act", nc.scalar.copy, [C.rps[bank]], [r_ks], out=ks[:, j, half * 512:(half + 1) * 512], in_=C.ps[bank][:])
                    else:
                        P.op("dve", nc.vector.tensor_copy, [C.rps[bank]], [r_ks], out=ks[:, j, half * 512:(half + 1) * 512], in_=C.ps[bank][:])
            for j in range(4):
                P.dma("sp", kT[18 + j], ks[:, j, :], reads=[r_ks], writes=[r_kT], semres=r_ks, multi=True)
        else:
            voff = {6: 0, 7: 512, 8: 1280, 9: 1792, 10: 2304}[ti]
            for tt in range(NT):
                bank = 4 + tt % 2
                proj_tok(P, C, xT, r_xT, wt, r_wt, tt, bank)
                vs, r_vs = vst[nvs % 3]
                nvs += 1
                if tt % 2 == 0:
                    P.op("act", nc.scalar.copy, [C.rps[bank]], [r_vs], out=vs[:], in_=C.ps[bank][:])
                else:
                    P.op("dve", nc.vector.tensor_copy, [C.rps[bank]], [r_vs], out=vs[:], in_=C.ps[bank][:])
                P.dma("sp", v[tt * 128:(tt + 1) * 128, voff:voff + 512], vs[:], reads=[r_vs], writes=[r_v], semres=r_vs, multi=True)


def build_P():
    nc = bass.Bass("TRN2", target_bir_lowering=False)
    x = nc.dram_tensor("x", [TPC, D], F32, kind="ExternalInput").ap()
    wP = nc.dram_tensor("wP", [11, 128, KC, 512], F32, kind="ExternalInput").ap()
    tabd = nc.dram_tensor("tab", [128, NT, NTAB], F32, kind="ExternalInput").ap()
    gkd = nc.dram_tensor("gk", [128, 128], F32, kind="ExternalInput").ap()
    kT = nc.dram_tensor("kT", [NKT, 128, TPC], BF16, kind="ExternalOutput").ap()
    v = nc.dram_tensor("v", [TPC, NV], BF16, kind="ExternalOutput").ap()
    with ExitStack() as es:
        P = Prog(nc, es)
        C = setup_common(P, es)
        emit_P(P, C, es, x, P.res("x"), wP, P.res("wP"), tabd, gkd, kT, P.res("kT"), v, P.res("v"))
        P.barrier()
        P.flush()
    return nc


BW = 1792
DW = 3072
NMASK = 22


class WStream:
    def __init__(self, P, wts, w_dram, r_w, order):
        self.P, self.wts, self.w, self.r_w, self.order = P, wts, w_dram, r_w, order
        self.i = 0
        load_wtile(P, wts[0][0], wts[0][1], w_dram, r_w, order[0])

    def get(self, expect):
        assert self.order[self.i] == expect, (self.order[self.i], expect)
        cur = self.wts[self.i % 2]
        self.i += 1
        if self.i < len(self.order):
            nxt = self.wts[self.i % 2]
            load_wtile(self.P, nxt[0], nxt[1], self.w, self.r_w, self.order[self.i])
        return cur


def emit_silu_z(P, C, xT, r_xT, wt, r_wt, siluz, r_siluz):
    nc = P.nc
    n = 0
    for j in range(4):
        for half in range(2):
            bank = 4 + n % 2
            n += 1
            proj_feat(P, C, xT, r_xT, wt, r_wt, j, half, bank)
            P.op("act", nc.scalar.activation, [C.rps[bank]], [r_siluz], out=siluz[:, j, half * 512:(half + 1) * 512], in_=C.ps[bank][:], func=AF.Silu)


class Grp:
    pass


def dense_attention(P, C, groups, pT, scale, accs):
    nc = P.nc
    SB = (0, 1, 2, 7)
    items = [(gi, kc) for gi in range(len(groups)) for kc in range(groups[gi].nk)]
    n = len(items)

    def qk(idx):
        gi, kc = items[idx]
        g = groups[gi]
        sbk = SB[idx % 4]
        P.op("pe", nc.tensor.matmul, [g.r_q, g.r_kfn(kc)], [C.rps[sbk]], out=C.ps[sbk][:], lhsT=g.kfn(kc), rhs=g.q, start=True, stop=True)

    for i0 in range(min(3, n)):
        qk(i0)
    for idx in range(n):
        gi, kc = items[idx]
        g = groups[gi]
        sbk = SB[idx % 4]
        pt, r_pt = pT[idx % len(pT)]
        P.op("act", nc.scalar.activation, [C.rps[sbk]], [r_pt], out=pt[:], in_=C.ps[sbk][:], func=AF.Exp, scale=scale)
        last = (kc == g.nk - 1)
        P.op("pe", nc.tensor.matmul, [r_pt, g.r_vfn(kc)], [C.rps[g.ob]], out=C.ps[g.ob][:], lhsT=g.vfn(kc), rhs=pt[:], start=(kc == 0), stop=last)
        ac, r_ac = accs[(gi % 2) * 2 + kc % 2]
        if kc < 2:
            P.op("dve", nc.vector.tensor_copy, [r_pt], [r_ac], out=ac[:], in_=pt[:])
        else:
            P.op("dve", nc.vector.tensor_tensor, [r_pt, r_ac], [r_ac], out=ac[:], in0=ac[:], in1=pt[:], op=ALU.add)
        if idx + 3 < n:
            qk(idx + 3)
        if last:
            a0, r_a0 = accs[(gi % 2) * 2]
            a1, r_a1 = accs[(gi % 2) * 2 + 1]
            P.op("pe", nc.tensor.matmul, [r_a0, C.r_onesf], [C.rps[g.db]], inc=False, out=C.ps[g.db][:], lhsT=C.onesf[:], rhs=a0[:], start=True, stop=False)
            P.op("pe", nc.tensor.matmul, [r_a1, C.r_onesf], [C.rps[g.db]], out=C.ps[g.db][:], lhsT=C.onesf[:], rhs=a1[:], start=False, stop=True)
            g.fin()


def emit_T(P, C, es, D_, lam_consts=None):
    nc = P.nc
    TT = nc.vector.tensor_tensor
    x, r_x = D_["x"], P.res("x")
    r_w = P.res("wT")
    r_kv = P.res("kvdram")
    r_xo = D_.get("r_xo") or P.res("xo")
    wts = [P.sb(es, "wt%d" % i, [128, KC, 512], BF16) for i in range(2)]
    sm, r_sm = P.sb(es, "sm", [128, 16], F32)
    bgT, r_bgT = P.sb(es, "bgT", [128, 64], F32)
    P.dma("sp", bgT[:], D_["bgT"], writes=[r_bgT])
    order = [6, 0, 8, 1, 7, 5, 9, 2, 3, 4] + list(range(10, 26))
    ws = WStream(P, wts, D_["wT"], r_w, order)

    with ExitStack() as em:
        xT, r_xT = P.sb(em, "xT", [128, KC, TPC], BF16)
        ysT, r_ysT = P.sb(em, "ysT", [128, KC, TPC], BF16)
        tab, r_tab = P.sb(em, "tab", [128, NT, NTAB], F32)
        P.dma("sp", tab[:], D_["tab"], writes=[r_tab])
        siluz, r_siluz = P.sb(em, "siluz", [128, 4, TPC], BF16)
        fin = []
        big, r_big = P.sb(em, "big", [128, 16384], BF16)
        mergedT = big[:].rearrange("p (k t) -> p k t", k=KC)
        r_mergedT = r_big

        with ExitStack() as ep:
            lp, r_lp = P.sb(ep, "lp", [128, 256], F32)
            cst, r_cst = P.sb(ep, "cst", [128, 4], F32)
            subg, r_subg = P.sb(ep, "subg", [128, 1], F32)
            pr, r_pr = P.sb(ep, "pr", [128, 128], F32)
            P.dma("sp", lp[:], D_["lp"], writes=[r_lp])
            P.dma("sp", cst[:], D_["cst"], writes=[r_cst])
            P.dma("sp", subg[:], D_["subg"], writes=[r_subg])
            P.op("dve", TT, [r_lp], [r_pr], out=pr[:].rearrange("p (a d) -> p a d", a=2), in0=lp[:].rearrange("p (a b d) -> p a b d", a=2, b=2)[:, :, 0, :],
                 in1=lp[:].rearrange("p (a b d) -> p a b d", a=2, b=2)[:, :, 1, :], op=ALU.mult)
            P.op("dve", nc.vector.tensor_reduce, [r_pr], [r_sm], out=sm[:, 3:5], in_=pr[:].rearrange("p (a d) -> p a d", a=2), axis=mybir.AxisListType.X, op=ALU.add)
            P.op("act", nc.scalar.activation, [r_sm], [r_sm], out=sm[:, 5:7], in_=sm[:, 3:5], func=AF.Exp)
            P.op("dve", TT, [r_sm], [r_sm], out=sm[:, 7:8], in0=sm[:, 5:6], in1=sm[:, 6:7], op=ALU.subtract)
            P.op("dve", TT, [r_sm, r_cst], [r_sm], out=sm[:, 0:1], in0=sm[:, 7:8], in1=cst[:, 0:1], op=ALU.add)
            P.op("dve", nc.vector.tensor_scalar, [r_sm], [r_sm], out=sm[:, 1:2], in0=sm[:, 0:1], scalar1=-1.0, scalar2=None, op0=ALU.mult)
            P.op("dve", TT, [r_subg, r_cst], [r_sm], out=sm[:, 2:3], in0=subg[:], in1=cst[:, 1:2], op=ALU.mult)
            build_xT(P, C, ep, x, r_x, xT, r_xT)
            P.barrier()
            P.flush()

        def fin_simple(ob, db, ci, j):
            rden, r_rden = fin[-2]
            on, r_on = fin[-1]
            P.op("dve", nc.vector.reciprocal, [C.rps[db]], [r_rden], out=rden[:], in_=C.ps[db][:])
            P.op("dve", TT, [C.rps[ob], r_rden], [r_on], out=on[:], in0=C.ps[ob][:], in1=rden[:], op=ALU.mult)
            P.op("dve", TT, [r_on, r_siluz], [r_ysT], out=ysT[:, ci, j * 512:(j + 1) * 512], in0=on[:], in1=siluz[:, ci % 4, j * 512:(j + 1) * 512], op=ALU.mult)

        with ExitStack() as ep:
            tq = TokQK(P, C, ep, tab, r_tab)
            pT = [P.sb(ep, "pT%d" % i, [128, 512], BF16) for i in range(4)]
            accs = [P.sb(ep, "dacc%d" % i, [128, 512], F32) for i in range(4)]
            fin = [P.sb(ep, "fin%d" % i, [128, 512], F32) for i in range(6)]
            gq, r_gq = P.sb(ep, "gq", [128, 128], F32)
            P.dma("sp", gq[:], D_["gq"], writes=[r_gq])
            qT, r_qT = P.sb(ep, "qT", [128, 4, TPC], BF16)
            qT2, r_qT2 = P.sb(ep, "qT2", [128, 4, TPC], BF16)
            P.op("pool", nc.gpsimd.memset, [], [r_qT], ap=qT[64:128, :, :], constant=0.0)
            P.op("pool", nc.gpsimd.memset, [], [r_qT2], ap=qT2[0:64, :, :], constant=0.0)
            ksb = big[:, 0:S]
            vsb = big[:, S:2 * S].rearrange("p (k d) -> p k d", k=64)
            r_kq = [P.res("kq") for _ in range(4)]
            r_vq = [P.res("vq") for _ in range(4)]

            def load_kv(kidx, vcol):
                for i in range(4):
                    P.dma("sp", ksb[:, i * 2048:(i + 1) * 2048], D_["kT_AC"][kidx, :, i * 2048:(i + 1) * 2048], reads=[r_kv], writes=[r_kq[i]])
                    P.dma("sp", vsb[:, i * 16:(i + 1) * 16, :],
                          D_["v_AC"][i * 2048:(i + 1) * 2048, vcol:vcol + 128].rearrange("(kc p) d -> p kc d", p=128), reads=[r_kv], writes=[r_vq[i]])

            wt, r_wt = ws.get(6)
            emit_silu_z(P, C, xT, r_xT, wt, r_wt, siluz, r_siluz)
            wt, r_wt = ws.get(0)
            for tt in range(NT):
                tq.run(xT, r_xT, wt, r_wt, tt, "A", 4, lambda j, tt=tt: [(qT[0:64, j, tt * 128:(tt + 1) * 128], r_qT, 0, 64),
                                                                        (qT2[64:128, j, tt * 128:(tt + 1) * 128], r_qT2, 64, 128)])
            ng = 0
            for h in range(4):
                load_kv(h, h * 128)
                groups = []
                for j in range(2):
                    for c in range(2):
                        g = Grp()
                        g.q = (qT if c == 0 else qT2)[:, h, j * 512:(j + 1) * 512]
                        g.r_q = (r_qT if c == 0 else r_qT2)
                        g.kfn = lambda kc: ksb[:, kc * 128:(kc + 1) * 128]
                        g.r_kfn = lambda kc: r_kq[kc // 16]
                        g.vfn = lambda kc: vsb[:, kc, :]
                        g.r_vfn = lambda kc: r_vq[kc // 16]
                        g.nk = 64
                        g.ob, g.db = (3, 4) if ng % 2 == 0 else (5, 6)
                        ng += 1

                        def fin_a(g=g, c=c, j=j, h=h):
                            rden, r_rden = fin[0]
                            P.op("dve", nc.vector.reciprocal, [C.rps[g.db]], [r_rden], out=rden[:], in_=C.ps[g.db][:])
                            if c == 0:
                                o1, r_o1 = fin[1]
                                P.op("dve", TT, [C.rps[g.ob], r_rden], [r_o1], out=o1[:], in0=C.ps[g.ob][:], in1=rden[:], op=ALU.mult)
                                return
                            o1, r_o1 = fin[1]
                            o2, r_o2 = fin[2]
                            dd, r_dd = fin[3]
                            sq, r_sq = fin[4]
                            rs, r_rs = fin[5]
                            P.op("dve", TT, [C.rps[g.ob], r_rden], [r_o2], out=o2[:], in0=C.ps[g.ob][:], in1=rden[:], op=ALU.mult)
                            P.op("dve", nc.vector.scalar_tensor_tensor, [r_o2, r_o1, r_sm], [r_dd], out=dd[:], in0=o2[:], scalar=sm[:, 1:2], in1=o1[:],
                                 op0=ALU.mult, op1=ALU.add)
                            P.op("dve", TT, [r_dd], [r_sq], out=sq[:], in0=dd[:], in1=dd[:], op=ALU.mult)
                            P.op("pe", nc.tensor.matmul, [r_sq, C.r_onesf], [C.rps[7]], out=C.ps[7][:], lhsT=C.onesf[:], rhs=sq[:], start=True, stop=True)
                            P.op("act", nc.scalar.activation, [C.rps[7], C.r_eps_rms], [r_rs], out=rs[:], in_=C.ps[7][:], func=AF.Sqrt, scale=1.0 / 128, bias=C.eps_rms[:])
                            P.op("dve", nc.vector.reciprocal, [r_rs], [r_rs], out=rs[:], in_=rs[:])
                            P.op("dve", nc.vector.scalar_tensor_tensor, [r_dd, r_rs, r_sm], [r_sq], out=sq[:], in0=dd[:], scalar=sm[:, 2:3], in1=rs[:],
                                 op0=ALU.mult, op1=ALU.mult)
                            P.op("dve", TT, [r_sq, r_siluz], [r_ysT], out=ysT[:, h, j * 512:(j + 1) * 512], in0=sq[:], in1=siluz[:, h, j * 512:(j + 1) * 512], op=ALU.mult)
                        g.fin = fin_a
                        groups.append(g)
                dense_attention(P, C, groups, pT, 0.125, accs)
            wt, r_wt = ws.get(8)
            emit_silu_z(P, C, xT, r_xT, wt, r_wt, siluz, r_siluz)
            wt, r_wt = ws.get(1)
            for tt in range(NT):
                tq.run(xT, r_xT, wt, r_wt, tt, "C", 4, lambda j, tt=tt: (qT[:, j, tt * 128:(tt + 1) * 128], r_qT), g_bc=gq, r_g=r_gq)
            for kv in range(2):
                load_kv(4 + kv, 512 + kv * 128)
                groups = []
                for gg in range(2):
                    hq = kv * 2 + gg
                    for j in range(2):
                        g = Grp()
                        g.q = qT[:, hq, j * 512:(j + 1) * 512]
                        g.r_q = r_qT
                        g.kfn = lambda kc: ksb[:, kc * 128:(kc + 1) * 128]
                        g.r_kfn = lambda kc: r_kq[kc // 16]
                        g.vfn = lambda kc: vsb[:, kc, :]
                        g.r_vfn = lambda kc: r_vq[kc // 16]
                        g.nk = 64
                        g.ob, g.db = (3, 4) if ng % 2 == 0 else (5, 6)
                        ng += 1
                        g.fin = (lambda g=g, hq=hq, j=j: fin_simple(g.ob, g.db, 8 + hq, j))
                        groups.append(g)
                dense_attention(P, C, groups, pT, 128 ** -0.5, accs)
            P.barrier()
            P.flush()

        sc128 = 128 ** -0.5
        with ExitStack() as ep:
            ksbs = [P.sb(ep, "ksbB%d" % i, [128, BW], BF16) for i in range(2)]
            vsbs = [P.sb(ep, "vsbB%d" % i, [128, 14, 128], BF16) for i in range(2)]
            bias = [P.sb(ep, "biasB%d" % i, [128, 7, 128], F32) for i in range(2)]
            tmp = [P.sb(ep, "tmpB%d" % i, [128, 7, 128], F32) for i in range(2)]
            pB = [P.sb(ep, "pB%d" % i, [128, 7, 128], BF16) for i in range(2)]
            fin = [P.sb(ep, "finB%d" % i, [128, 512], F32) for i in range(2)]
            qT, r_qT = P.sb(ep, "qTB", [128, 4, TPC], BF16)
            wt, r_wt = ws.get(7)
            emit_silu_z(P, C, xT, r_xT, wt, r_wt, siluz, r_siluz)
            wt, r_wt = ws.get(5)
            n = 0
            for j in range(4):
                for half in range(2):
                    bank = 4 + n % 2
                    n += 1
                    proj_feat(P, C, xT, r_xT, wt, r_wt, j, half, bank)
                    P.op("dve", nc.vector.tensor_copy, [C.rps[bank]], [r_qT], out=qT[:, j, half * 512:(half + 1) * 512], in_=C.ps[bank][:])
            itemsB = [(h, qt) for h in range(4) for qt in range(NT)]

            def b_stage1(i):
                h, qt = itemsB[i]
                ksb, r_ksb = ksbs[h % 2]
                vsb, r_vsb = vsbs[h % 2]
                if qt == 0:
                    P.dma("sp", ksb[:], D_["kT_Bw"][h], reads=[r_kv], writes=[r_ksb])
                    P.dma("sp", vsb[:], D_["v_Bw"][:, h * 128:(h + 1) * 128].rearrange("(ch p) d -> p ch d", p=128), reads=[r_kv], writes=[r_vsb])
                bt, r_bt = bias[i % 2]
                tp, r_tp = tmp[i % 2]
                pb, r_pb = pB[i % 2]
                P.dma("sp", bt[:], D_["biasB"][qt, h], reads=[r_kv], writes=[r_bt])
                sb0, sb1 = (0, 1) if i % 2 == 0 else (2, 7)
                qap = qT[:, h, qt * 128:(qt + 1) * 128]
                for kc in range(7):
                    bk = sb0 if kc < 4 else sb1
                    co = (kc % 4) * 128
                    P.op("pe", nc.tensor.matmul, [r_qT, r_ksb], [C.rps[bk]], inc=(kc in (3, 6)), out=C.ps[bk][:, co:co + 128],
                         lhsT=ksb[:, (qt + kc) * 128:(qt + kc + 1) * 128], rhs=qap, start=True, stop=True)
                P.op("dve", nc.vector.scalar_tensor_tensor, [C.rps[sb0], r_bt], [r_tp], out=tp[:, 0:4, :], in0=C.ps[sb0][:].rearrange("p (k q) -> p k q", k=4),
                     scalar=sc128, in1=bt[:, 0:4, :], op0=ALU.mult, op1=ALU.add)
                P.op("dve", nc.vector.scalar_tensor_tensor, [C.rps[sb1], r_bt], [r_tp], out=tp[:, 4:7, :], in0=C.ps[sb1][:, 0:384].rearrange("p (k q) -> p k q", k=3),
                     scalar=sc128, in1=bt[:, 4:7, :], op0=ALU.mult, op1=ALU.add)
                P.op("act", nc.scalar.activation, [r_tp], [r_pb], out=pb[:], in_=tp[:], func=AF.Exp)

            def b_stage2(i):
                h, qt = itemsB[i]
                vsb, r_vsb = vsbs[h % 2]
                pb, r_pb = pB[i % 2]
                jq = qt // 4
                ob, db = (3, 4) if (h * 2 + jq) % 2 == 0 else (5, 6)
                co = (qt % 4) * 128
                for kc in range(7):
                    P.op("pe", nc.tensor.matmul, [r_pb, r_vsb], [C.rps[ob]], inc=False, out=C.ps[ob][:, co:co + 128], lhsT=vsb[:, qt + kc, :], rhs=pb[:, kc, :],
                         start=(kc == 0), stop=(kc == 6))
                    P.op("pe", nc.tensor.matmul, [r_pb, C.r_onesb], [C.rps[db]], inc=(kc == 6), out=C.ps[db][:, co:co + 128], lhsT=C.onesb[:], rhs=pb[:, kc, :],
                         start=(kc == 0), stop=(kc == 6))
                if qt % 4 == 3:
                    fin_simple(ob, db, 4 + h, jq)

            b_stage1(0)
            for i in range(len(itemsB)):
                if i + 1 < len(itemsB):
                    b_stage1(i + 1)
                b_stage2(i)
            P.barrier()
            P.flush()

        with ExitStack() as ep:
            tq = TokQK(P, C, ep, tab, r_tab)
            qTD = big[:, 0:12 * TPC].rearrange("p (h t) -> p h t", h=12)
            r_qTD = r_big
            ksbs = [P.sb(ep, "ksbD%d" % i, [128, DW], BF16) for i in range(2)]
            vsbs = [P.sb(ep, "vsbD%d" % i, [128, 32, 128], BF16) for i in range(2)]
            mk, r_mk = P.sb(ep, "maskD", [128, NMASK, 128], F32)
            oacc, r_oacc = P.sb(ep, "oacc", [128, TPC], F32)
            dacc, r_dacc = P.sb(ep, "dacc", [128, TPC], F32)
            tmp = [P.sb(ep, "tmpD%d" % i, [128, 128], F32) for i in range(3)]
            pD = [P.sb(ep, "pD%d" % i, [128, 128], BF16) for i in range(3)]
            P.dma("sp", mk[:], D_["maskD"], writes=[r_mk])
            wt, r_wt = ws.get(9)
            emit_silu_z(P, C, xT, r_xT, wt, r_wt, siluz, r_siluz)
            for g in range(3):
                wt, r_wt = ws.get(2 + g)
                for tt in range(NT):
                    tq.run(xT, r_xT, wt, r_wt, tt, "D", 4, lambda j, tt=tt, g=g: (qTD[:, g * 4 + j, tt * 128:(tt + 1) * 128], r_qTD))
            mbase = [0, 16, 20]
            blocks = [(hh, g) for hh in range(4) for g in range(3)]

            def d_load(bi):
                hh, g = blocks[bi]
                ksb, r_ksb = ksbs[bi % 2]
                vsb, r_vsb = vsbs[bi % 2]
                r = DIL_RATES[g]
                M = {1: 9, 4: 3, 16: 2}[r]
                W = TPC + 128 * r
                P.dma("sp", ksb[:, 0:W], D_["kT_Dw"][g * 4 + hh, :, 1024 - 64 * r:1024 - 64 * r + W], reads=[r_kv], writes=[r_ksb])
                c0 = g * 512 + hh * 128
                vsrc = D_["v_Dw"]
                if r == 1:
                    P.dma("sp", vsb[:, 0:9, :], vsrc[960:960 + 1152, c0:c0 + 128].rearrange("(m a) d -> a m d", a=128), reads=[r_kv], writes=[r_vsb])
                elif r == 4:
                    for m in range(3):
                        P.dma("sp", vsb[:, ss(m, 4, 3), :], vsrc[768 + 512 * m:768 + 512 * m + 512, c0:c0 + 128].rearrange("(a r) d -> a r d", r=4),
                              reads=[r_kv], writes=[r_vsb])
                else:
                    for r0 in (0, 8):
                        P.dma("sp", vsb[:, ss(2 * r0, 8, 2), :], vsrc[0:2048, c0:c0 + 128].rearrange("(a r) d -> a r d", r=16)[:, r0:r0 + 8, :],
                              reads=[r_kv], writes=[r_vsb])
                    P.dma("sp", vsb[0:64, ss(1, 16, 2), :], vsrc[2048:3072, c0:c0 + 128].rearrange("(a r) d -> a r d", r=16), reads=[r_kv], writes=[r_vsb])

            itc = [0]

            def d_compute(bi):
                hh, g = blocks[bi]
                ksb, r_ksb = ksbs[bi % 2]
                vsb, r_vsb = vsbs[bi % 2]
                r = DIL_RATES[g]
                n_ = TPC // r
                nq = min(128, n_)
                nqt = n_ // nq
                M = {1: 9, 4: 3, 16: 2}[r]
                ob0, db0 = 3, 5
                its = [(rho, qti, ch) for rho in range(r) for qti in range(nqt) for ch in range(2)]

                def s1(k):
                    rho, qti, ch = its[k]
                    U = qti * nq
                    idx = itc[0] + k
                    sbk = idx % 3
                    tp, r_tp = tmp[idx % 3]
                    pd, r_pd = pD[idx % 3]
                    u0 = U - 64 + 128 * ch
                    nk = 128 if (ch == 0 or nq == 128) else 64
                    qap = qTD[:, g * 4 + hh, ss(rho + r * U, nq, r)]
                    kap = ksb[:, ss(r * (u0 + 64) + rho, nk, r)]
                    P.op("pe", nc.tensor.matmul, [r_qTD, r_ksb], [C.rps[sbk]], out=C.ps[sbk][0:nk, 0:nq], lhsT=kap, rhs=qap, start=True, stop=True)
                    mi = mbase[g] + qti * 2 + ch
                    P.op("dve", nc.vector.scalar_tensor_tensor, [C.rps[sbk], r_mk], [r_tp], out=tp[0:nk, 0:nq], in0=C.ps[sbk][0:nk, 0:nq],
                         scalar=sc128, in1=mk[0:nk, mi, 0:nq], op0=ALU.mult, op1=ALU.add)
                    P.op("act", nc.scalar.activation, [r_tp], [r_pd], out=pd[0:nk, 0:nq], in_=tp[0:nk, 0:nq], func=AF.Exp)

                def s2(k):
                    rho, qti, ch = its[k]
                    U = qti * nq
                    idx = itc[0] + k
                    pd, r_pd = pD[idx % 3]
                    nk = 128 if (ch == 0 or nq == 128) else 64
                    col0 = rho * n_ + U
                    ob = ob0 + col0 // 512
                    db = db0 + col0 // 512
                    cofs = col0 % 512
                    vidx = rho * M + (qti + ch)
                    P.op("pe", nc.tensor.matmul, [r_pd, r_vsb], [C.rps[ob]], inc=False, out=C.ps[ob][:, cofs:cofs + nq], lhsT=vsb[0:nk, vidx, :],
                         rhs=pd[0:nk, 0:nq], start=(ch == 0), stop=(ch == 1))
                    P.op("pe", nc.tensor.matmul, [r_pd, C.r_onesb], [C.rps[db]], out=C.ps[db][:, cofs:cofs + nq], lhsT=C.onesb[0:nk, :],
                         rhs=pd[0:nk, 0:nq], start=(ch == 0), stop=(ch == 1))

                nI = len(its)
                s1(0)
                if nI > 1:
                    s1(1)
                for k in range(nI):
                    s2(k)
                    if k + 2 < nI:
                        s1(k + 2)
                itc[0] += nI
                for (acc, r_acc, b0) in ((oacc, r_oacc, ob0), (dacc, r_dacc, db0)):
                    for b in range(2):
                        if r == 1:
                            dst = acc[:, b * 512:(b + 1) * 512]
                            src = C.ps[b0 + b][:]
                        else:
                            dst = acc[:].rearrange("p (u r) -> p r u", r=r)[:, b * r // 2:(b + 1) * r // 2, :]
                            src = C.ps[b0 + b][:].rearrange("p (r u) -> p r u", u=n_)
                        if g == 0:
                            P.op("dve", nc.vector.tensor_copy, [C.rps[b0 + b]], [r_acc], out=dst, in_=src)
                        else:
                            P.op("dve", TT, [C.rps[b0 + b], r_acc], [r_acc], out=dst, in0=dst, in1=src, op=ALU.add)
                if g == 2:
                    P.op("dve", nc.vector.reciprocal, [r_dacc], [r_dacc], out=dacc[:], in_=dacc[:])
                    P.op("dve", TT, [r_oacc, r_dacc], [r_oacc], out=oacc[:], in0=oacc[:], in1=dacc[:], op=ALU.mult)
                    P.op("dve", TT, [r_oacc, r_siluz], [r_ysT], out=ysT[:, 12 + hh, :], in0=oacc[:], in1=siluz[:, hh, :], op=ALU.mult)

            d_load(0)
            for bi in range(len(blocks)):
                if bi + 1 < len(blocks):
                    d_load(bi + 1)
                d_compute(bi)
            P.barrier()
            P.flush()

        with ExitStack() as ep:
            wbt = [P.sb(ep, "wbt%d" % i, [128, 16, 128], BF16) for i in range(2)]
            gs = [P.sb(ep, "gs%d" % i, [128, 512], F32) for i in range(3)]
            macc = [P.sb(ep, "macc%d" % i, [128, 512], F32) for i in range(2)]
            mt = [P.sb(ep, "mt%d" % i, [128, 512], F32) for i in range(2)]
            r_wbr = P.res("wbr")
            P.dma("pool", wbt[0][0][:], D_["wbr"][0], reads=[r_wbr], writes=[wbt[0][1]])
            it = 0
            for dc in range(KC):
                wt, r_wt = ws.get(10 + dc)
                wb, r_wb = wbt[dc % 2]
                if dc + 1 < KC:
                    P.dma("pool", wbt[(dc + 1) % 2][0][:], D_["wbr"][dc + 1], reads=[r_wbr], writes=[wbt[(dc + 1) % 2][1]])
                for n in range(4):
                    for half in range(2):
                        gb = it % 4
                        pb = 4 + it % 4
                        g_, r_g_ = gs[it % 3]
                        it += 1
                        for kc in range(KC):
                            P.op("pe", nc.tensor.matmul, [r_xT, r_wt], [C.rps[gb]], inc=(kc == KC - 1), out=C.ps[gb][:], lhsT=wt[:, kc, n * 128:(n + 1) * 128],
                                 rhs=xT[:, kc, half * 512:(half + 1) * 512], start=(kc == 0), stop=(kc == KC - 1))
                        P.op("act", nc.scalar.activation, [C.rps[gb], r_bgT], [r_g_], out=g_[:], in_=C.ps[gb][:], func=AF.Sigmoid, bias=bgT[:, n * 16 + dc:n * 16 + dc + 1], scale=1.0)
                        for wc in range(4):
                            P.op("pe", nc.tensor.matmul, [r_ysT, r_wb], [C.rps[pb]], inc=(wc == 3), out=C.ps[pb][:], lhsT=wb[:, n * 4 + wc, :],
                                 rhs=ysT[:, n * 4 + wc, half * 512:(half + 1) * 512], start=(wc == 0), stop=(wc == 3))
                        ma, r_ma = macc[half]
                        if n == 0:
                            P.op("dve", TT, [C.rps[pb], r_g_], [r_ma], out=ma[:], in0=C.ps[pb][:], in1=g_[:], op=ALU.mult)
                        else:
                            t_, r_t_ = mt[half]
                            P.op("dve", TT, [C.rps[pb], r_g_], [r_t_], out=t_[:], in0=C.ps[pb][:], in1=g_[:], op=ALU.mult)
                            if n < 3:
                                P.op("dve", TT, [r_ma, r_t_], [r_ma], out=ma[:], in0=ma[:], in1=t_[:], op=ALU.add)
                            else:
                                P.op("dve", TT, [r_ma, r_t_], [r_mergedT], out=mergedT[:, dc, half * 512:(half + 1) * 512], in0=ma[:], in1=t_[:], op=ALU.add)
            P.barrier()
            P.flush()

        with ExitStack() as ep:
            wos = [(xT[:, :, 0:512], r_xT), (xT[:, :, 512:1024], r_xT), (ysT[:, :, 0:512], r_ysT), (ysT[:, :, 512:1024], r_ysT)]
            r_wod = P.res("wod")
            for cc in range(4):
                load_wtile(P, wos[cc][0], wos[cc][1], D_["wo"], r_wod, cc)
            gb, r_gb = P.sb(ep, "lng", [128, D], F32)
            bb, _ = P.sb(ep, "lnb", [128, D], F32)
            P.dma("sp", gb[:], D_["lng"], writes=[r_gb])
            P.dma("sp", bb[:], D_["lnb"], writes=[r_gb])
            xts = [P.sb(ep, "xt%d" % i, [128, D], F32) for i in range(2)]
            stats, r_stats = P.sb(ep, "stats", [128, 24], F32)
            mv, r_mv = P.sb(ep, "mv", [128, 4], F32)
            for tt in range(NT):
                xt, r_xt = xts[tt % 2]
                P.dma("sp", xt[:], x[tt * 128:(tt + 1) * 128, :], reads=[r_x], writes=[r_xt])
                for cc in range(4):
                    bank = (tt % 2) * 4 + cc
                    wo_, r_wo_ = wos[cc]
                    for dc in range(KC):
                        P.op("pe", nc.tensor.matmul, [r_mergedT, r_wo_], [C.rps[bank]], inc=(dc == KC - 1), out=C.ps[bank][:], lhsT=mergedT[:, dc, tt * 128:(tt + 1) * 128],
                             rhs=wo_[:, dc, :], start=(dc == 0), stop=(dc == KC - 1))
                    P.op("dve", nc.vector.scalar_tensor_tensor, [r_xt, C.rps[bank]], [r_xt], out=xt[:, cc * 512:(cc + 1) * 512], in0=xt[:, cc * 512:(cc + 1) * 512],
                         scalar=float(DN_ALPHA), in1=C.ps[bank][:], op0=ALU.mult, op1=ALU.add)
                layer_norm_tile(P, C, xt, r_xt, gb, bb, r_gb, stats, r_stats, mv, r_mv)
                P.dma("sp", D_["xo"][tt * 128:(tt + 1) * 128, :], xt[:], reads=[r_xt], writes=[r_xo], semres=r_xt)
            P.barrier()
            P.flush()


def build_T():
    nc = bass.Bass("TRN2", target_bir_lowering=False)

    def din(name, shape, dt=F32):
        return nc.dram_tensor(name, shape, dt, kind="ExternalInput").ap()
    D_ = {
        "x": din("x", [TPC, D]), "wT": din("wT", [26, 128, KC, 512]), "tab": din("tab", [128, NT, NTAB]), "gq": din("gq", [128, 128]),
        "lp": din("lp", [128, 256]), "cst": din("cst", [128, 4]), "subg": din("subg", [128, 1]), "bgT": din("bgT", [128, 64]),
        "wbr": din("wbr", [KC, 128, 16, 128]), "wo": din("wo", [4, 128, KC, 512]), "lng": din("lng", [128, D]), "lnb": din("lnb", [128, D]),
        "kT_AC": din("kT_AC", [6, 128, S], BF16), "v_AC": din("v_AC", [S, 768], BF16),
        "kT_Bw": din("kT_Bw", [4, 128, BW], BF16), "v_Bw": din("v_Bw", [BW, 512], BF16), "biasB": din("biasB", [NT, 4, 128, 7, 128]),
        "kT_Dw": din("kT_Dw", [12, 128, DW], BF16), "v_Dw": din("v_Dw", [DW, 1536], BF16), "maskD": din("maskD", [128, NMASK, 128]),
    }
    D_["xo"] = nc.dram_tensor("xo", [TPC, D], F32, kind="ExternalOutput").ap()
    with ExitStack() as es:
        P = Prog(nc, es)
        C = setup_common(P, es)
        emit_T(P, C, es, D_)
    return nc


O_AQ, O_AK, O_AV, O_AZ = 0, 512, 1024, 1536
O_BQ, O_BK, O_BV, O_BZ = 2048, 2560, 3072, 3584
O_CQ, O_CK, O_CV, O_CZ = 4096, 4608, 4864, 5120
O_DQ, O_DK, O_DV, O_DZ = 5632, 7168, 8704, 10240
O_GL = 10752


def tileize(wcols):
    n = wcols.shape[1] // 512
    return np.ascontiguousarray(wcols.reshape(KC, 128, n, 512).transpose(2, 1, 0, 3))


def rep128(vec):
    return np.ascontiguousarray(np.broadcast_to(np.asarray(vec, np.float32).reshape(1, -1), (128, vec.size)))


def make_tabs():
    t = np.arange(S, dtype=np.int64)

    def cs(pos, half, dim, theta):
        inv = np.power(np.float32(theta), -np.arange(half, dtype=np.float32) * np.float32(2.0) / np.float32(dim)).astype(np.float32)
        ang = pos.astype(np.float32)[:, None] * inv[None, :]
        return np.cos(ang).astype(np.float32), np.sin(ang).astype(np.float32)
    cA, sA = cs(t, 8, 16, 500000.0)
    cD, sD = cs(t, 16, 32, 500000.0)
    cR, sR = cs(t // 64, 32, 64, 10000.0)
    cC, sC = cs(t % 64, 32, 64, 10000.0)
    full = np.concatenate([cA, sA, cD, sD, cR, sR, cC, sC], axis=1)
    out = []
    for c in range(NCORE):
        blk = full[c * TPC:(c + 1) * TPC].reshape(NT, 128, NTAB).transpose(1, 0, 2)
        out.append(np.ascontiguousarray(blk))
    return out


def wP_of(w):
    cols = np.concatenate([w[:, O_AK:O_AK + 512], w[:, O_DK:O_DK + 1536], w[:, O_BK:O_BK + 512],
                           w[:, O_CK:O_CK + 256], w[:, O_CV:O_CV + 256],
                           w[:, O_AV:O_AV + 512], w[:, O_BV:O_BV + 512], w[:, O_DV:O_DV + 1536]], axis=1)
    return tileize(cols)


_NC_CACHE = {}


def get_nc(name):
    if name not in _NC_CACHE:
        _NC_CACHE[name] = {"E": build_E, "P": build_P, "T": build_T}[name]()
    return _NC_CACHE[name]


def run_E(x2d, g, b):
    nc = get_nc("E")
    gb, bb = rep128(g), rep128(b)
    maps = [{"x": np.ascontiguousarray(x2d[c * TPC:(c + 1) * TPC]), "g": gb, "b": bb} for c in range(NCORE)]
    res = run_bass_kernel_spmd(nc, maps, core_ids=list(range(NCORE)))
    return np.concatenate([r["xo"] for r in res.results], axis=0)


def run_P(xcur, w, kn_g, tabs):
    nc = get_nc("P")
    wP = wP_of(w)
    gk = rep128(kn_g)
    maps = [{"x": np.ascontiguousarray(xcur[c * TPC:(c + 1) * TPC]), "wP": wP, "tab": tabs[c], "gk": gk} for c in range(NCORE)]
    res = run_bass_kernel_spmd(nc, maps, core_ids=list(range(NCORE)))
    kT = np.concatenate([r["kT"] for r in res.results], axis=2)
    v = np.concatenate([r["v"] for r in res.results], axis=0)
    return kT, v


def wT_of(w):
    gl = w[:, O_GL:O_GL + 8192].reshape(D, 4, KC, 128).transpose(0, 2, 1, 3).reshape(D, 8192)
    cols = np.concatenate([w[:, O_AQ:O_AQ + 512], w[:, O_CQ:O_CQ + 512], w[:, O_DQ:O_DQ + 1536], w[:, O_BQ:O_BQ + 512],
                           w[:, O_AZ:O_AZ + 512], w[:, O_BZ:O_BZ + 512], w[:, O_CZ:O_CZ + 512], w[:, O_DZ:O_DZ + 512], gl], axis=1)
    return tileize(cols)


def window(arr, axis, start, length):
    n = arr.shape[axis]
    lo, hi = max(start, 0), min(start + length, n)
    shp = list(arr.shape)
    shp[axis] = length
    out = np.zeros(shp, arr.dtype)
    sl_src = [slice(None)] * arr.ndim
    sl_dst = [slice(None)] * arr.ndim
    sl_src[axis] = slice(lo, hi)
    sl_dst[axis] = slice(lo - start, hi - start)
    out[tuple(sl_dst)] = arr[tuple(sl_src)]
    return out


def make_biasB_index():
    out = []
    for c in range(NCORE):
        qt = np.arange(NT)[:, None, None, None]
        a = np.arange(128)[None, :, None, None]
        kc = np.arange(7)[None, None, :, None]
        b = np.arange(128)[None, None, None, :]
        tk = TPC * c - 384 + 128 * (qt + kc) + a
        tq = TPC * c + 128 * qt + b
        tk, tq = np.broadcast_arrays(tk, tq)
        inr = (tk >= 0) & (tk < S)
        kr, kcol = tk // 64, tk % 64
        qr, qc = tq // 64, tq % 64
        r0 = np.clip(qr - 4, 0, 120)
        c0 = np.clip(qc - 8, 0, 48)
        valid = inr & (kr >= r0) & (kr < r0 + 8) & (kcol >= c0) & (kcol < c0 + 16)
        dr = np.clip(kr - qr + 7, 0, 14)
        dc = np.clip(kcol - qc + 15, 0, 30)
        out.append((np.where(valid, dr * 31 + dc, 0).astype(np.int64), valid))
    return out


def make_biasB(rpb_l, bidx):
    idx, valid = bidx
    flat = rpb_l.reshape(4, 15 * 31)
    g = flat[:, idx]
    g = np.where(valid[None], g, np.float32(NEG)).astype(np.float32)
    return np.ascontiguousarray(g.transpose(1, 0, 2, 3, 4))


def make_maskD():
    out = []
    for c in range(NCORE):
        m = np.full((128, NMASK, 128), NEG, np.float32)
        a = np.arange(128)[:, None]
        b = np.arange(128)[None, :]
        base = [0, 16, 20]
        for g, r in enumerate(DIL_RATES):
            n = TPC // r
            nq = min(128, n)
            for qti in range(n // nq):
                U = qti * nq
                for ch in range(2):
                    u = U - 64 + 128 * ch + a
                    uq = U + b
                    ug = n * c + u
                    valid = (np.abs(u - uq) <= 64) & (ug >= 0) & (ug < S // r)
                    m[:, base[g] + qti * 2 + ch, :] = np.where(valid, 0.0, NEG)
        out.append(m)
    return out


def run_T(xcur, l, inp, kT, v, tabs, bidx, maskD):
    nc = get_nc("T")
    w = inp["w_in"][l]
    lam_init = 0.8 - 0.6 * math.exp(-0.3 * l)
    common = {
        "wT": wT_of(w), "gq": rep128(inp["gqa_q_norm_g"][l]), "lp": rep128(inp["diff_lambda"][l].reshape(-1)),
        "cst": rep128(np.array([lam_init, 1.0 - lam_init, 0.0, 0.0], np.float32)),
        "subg": np.ascontiguousarray(inp["diff_subln_g"][l].reshape(128, 1)),
        "bgT": np.ascontiguousarray(inp["b_gate"][l].reshape(4, KC, 128).transpose(2, 0, 1).reshape(128, 64)),
        "wbr": np.ascontiguousarray(inp["w_branch"][l].reshape(4, 4, 128, KC, 128).transpose(3, 2, 0, 1, 4).reshape(KC, 128, 16, 128)),
        "wo": tileize(inp["w_out"][l]), "lng": rep128(inp["ln_g"][l]), "lnb": rep128(inp["ln_b"][l]),
        "kT_AC": np.ascontiguousarray(kT[0:6]), "v_AC": np.ascontiguousarray(np.concatenate([v[:, 0:512], v[:, 1024:1280]], axis=1)),
    }
    kT_B, v_B = kT[18:22], v[:, 512:1024]
    kT_D, v_D = kT[6:18], v[:, 1280:2816]
    maps = []
    for c in range(NCORE):
        m = dict(common)
        m["x"] = np.ascontiguousarray(xcur[c * TPC:(c + 1) * TPC])
        m["tab"] = tabs[c]
        m["kT_Bw"] = window(kT_B, 2, TPC * c - 384, BW)
        m["v_Bw"] = window(v_B, 0, TPC * c - 384, BW)
        m["biasB"] = make_biasB(inp["nat_rpb"][l], bidx[c])
        m["kT_Dw"] = window(kT_D, 2, TPC * c - 1024, DW)
        m["v_Dw"] = window(v_D, 0, TPC * c - 1024, DW)
        m["maskD"] = maskD[c]
        maps.append(m)
    res = run_bass_kernel_spmd(nc, maps, core_ids=list(range(NCORE)))
    return np.concatenate([r["xo"] for r in res.results], axis=0)


def kernel(**inputs):
    inp = {k: np.asarray(v) for k, v in inputs.items()}
    x = np.ascontiguousarray(inp["x"].reshape(S, D).astype(np.float32, copy=False))
    tabs = make_tabs()
    bidx = make_biasB_index()
    maskD = make_maskD()
    xcur = run_E(x, inp["emb_ln_g"], inp["emb_ln_b"])
    for l in range(DEPTH):
        kT, v = run_P(xcur, inp["w_in"][l], inp["gqa_k_norm_g"][l], tabs)
        xcur = run_T(xcur, l, inp, kT, v, tabs, bidx, maskD)
    return xcur.reshape(1, S, D).astype(np.float32)
```

```python
import math
from contextlib import ExitStack
import numpy as np
import ml_dtypes
import concourse.bass as bass
import concourse.mybir as mybir
from concourse.bass_utils import run_bass_kernel_spmd

F32 = mybir.dt.float32
BF16 = mybir.dt.bfloat16
AF = mybir.ActivationFunctionType
ALU = mybir.AluOpType
NPBF = ml_dtypes.bfloat16

NCORE = 8
S = 8192
D = 2048
TPC = 1024
NT = 8
KC = 16
DEPTH = 4
NEG = -30000.0
LN_EPS = 1e-5
RMS_EPS = 1e-6
DN_ALPHA = (2 * DEPTH) ** 0.25
DIL_RATES = (1, 4, 16)


def ss(start, n, step=1):
    return slice(start, start + step * (n - 1) + 1, step)


class Res:
    __slots__ = ("name", "ws", "rs", "dsem")

    def __init__(self, name):
        self.name = name
        self.ws = []
        self.rs = []
        self.dsem = None


class Prog:
    ENGS = ("pe", "act", "dve", "pool", "sp")

    def __init__(self, nc, es):
        self.nc = nc
        self.es = es
        self.eng = {"pe": nc.tensor, "act": nc.scalar, "dve": nc.vector, "pool": nc.gpsimd, "sp": nc.sync}
        self.streams = {e: [] for e in self.ENGS}
        self.cnt = {e: 0 for e in self.ENGS}
        self.sem = {}
        self.semtot = {}
        for e in self.ENGS:
            self.sem[e] = es.enter_context(nc.semaphore("s_" + e))
        self.waited = {e: {} for e in self.ENGS}
        self.ndsem = 0
        self.pending_noinc = {e: False for e in self.ENGS}
        self.nres = 0

    def res(self, name="r"):
        self.nres += 1
        return Res(name + str(self.nres))

    def sb(self, es, name, shape, dt):
        self.nres += 1
        t = es.enter_context(self.nc.sbuf_tensor("sb_%s_%d" % (name, self.nres), shape, dt))
        return t, self.res(name)

    def dsem_of(self, r):
        if r.dsem is None:
            key = "d%d" % self.ndsem
            self.ndsem += 1
            self.sem[key] = self.es.enter_context(self.nc.semaphore("sd%d" % self.ndsem))
            self.semtot[key] = 0
            r.dsem = key
        return r.dsem

    def _wait(self, e, deps):
        need = {}
        for (k, v) in deps:
            if k == e and e == "pe":
                continue
            if self.waited[e].get(k, 0) >= v:
                continue
            if need.get(k, 0) < v:
                need[k] = v
        for k, v in need.items():
            self.waited[e][k] = v
            sem = self.sem[k]
            eng = self.eng[e]
            self.streams[e].append(lambda eng=eng, sem=sem, v=v: eng.wait_ge(sem, v))

    @staticmethod
    def _deps(reads, writes):
        deps = []
        for r in reads:
            deps.extend(r.ws)
        for r in writes:
            deps.extend(r.ws)
            deps.extend(r.rs)
        return deps

    def op(self, e, meth, reads=(), writes=(), inc=True, **kw):
        self._wait(e, self._deps(reads, writes))
        sem = self.sem[e]
        if inc:
            self.streams[e].append(lambda meth=meth, kw=kw, sem=sem: meth(**kw).then_inc(sem, 1))
            self.cnt[e] += 1
            tok = (e, self.cnt[e])
            self.pending_noinc[e] = False
        else:
            assert e == "pe"
            self.streams[e].append(lambda meth=meth, kw=kw: meth(**kw))
            tok = (e, self.cnt[e] + 1)
            self.pending_noinc[e] = True
        for r in reads:
            r.rs.append(tok)
            if len(r.rs) > 64:
                r.rs = self._compact(r.rs)
        for r in writes:
            r.ws = self._compact(r.ws + [tok])
            r.rs = []
        return tok

    @staticmethod
    def _compact(toks):
        best = {}
        for k, v in toks:
            if best.get(k, 0) < v:
                best[k] = v
        return list(best.items())

    def dma(self, q, out, in_, reads=(), writes=(), semres=None, multi=False, **kw):
        self._wait(q, self._deps(reads, writes))
        if semres is None:
            semres = writes[0] if writes else reads[0]
        key = self.dsem_of(semres)
        self.semtot[key] += 16
        tok = (key, self.semtot[key])
        sem = self.sem[key]
        eng = self.eng[q]
        self.streams[q].append(lambda eng=eng, out=out, in_=in_, kw=kw, sem=sem: eng.dma_start(
            out=(out() if callable(out) else out), in_=(in_() if callable(in_) else in_), **kw).then_inc(sem, 16))
        for r in reads:
            r.rs.append(tok)
            if len(r.rs) > 64:
                r.rs = self._compact(r.rs)
        for r in writes:
            r.ws = self._compact(r.ws + [tok])
            r.rs = []
        return tok

    def raw(self, e, fn):
        self.streams[e].append(fn)

    def coll(self, kind, ins, outs, reads=(), writes=()):
        self._wait("pool", self._deps(reads, writes))
        key = self.dsem_of(writes[0])
        self.semtot[key] += 16
        tok = (key, self.semtot[key])
        sem = self.sem[key]
        nc = self.nc
        self.streams["pool"].append(lambda: nc.gpsimd.collective_compute(kind, ALU.bypass, replica_groups=[list(range(NCORE))], ins=ins, outs=outs).then_inc(sem, 16))
        for r in reads:
            r.rs.append(tok)
        for r in writes:
            r.ws = self._compact(r.ws + [tok])
            r.rs = []
        return tok

    def barrier(self):
        assert not any(self.pending_noinc.values())
        deps = [(e, self.cnt[e]) for e in ("pe", "act", "dve", "pool") if self.cnt[e] > 0]
        deps += [(k, v) for k, v in self.semtot.items() if v > 0]
        for e in self.ENGS:
            self._wait(e, [d for d in deps if d[0] != e])

    def flush(self):
        assert not any(self.pending_noinc.values())
        nc = self.nc
        st = self.streams
        with nc.Block() as block:
            @block.tensor
            def _(e):
                for f in st["pe"]:
                    f()

            @block.scalar
            def _(e):
                for f in st["act"]:
                    f()

            @block.vector
            def _(e):
                for f in st["dve"]:
                    f()

            @block.gpsimd
            def _(e):
                for f in st["pool"]:
                    f()

            @block.sync
            def _(e):
                for f in st["sp"]:
                    f()
        self.streams = {e: [] for e in self.ENGS}


class Ctx:
    pass


def setup_common(P, es):
    nc = P.nc
    C = Ctx()
    C.ps = []
    C.rps = []
    for i in range(8):
        t = es.enter_context(nc.psum_tensor("psb%d" % i, [128, 512], F32))
        C.ps.append(t)
        C.rps.append(P.res("ps"))
    C.ident, C.r_ident = P.sb(es, "ident", [128, 128], BF16)
    C.onesb, C.r_onesb = P.sb(es, "onesb", [128, 128], BF16)
    C.onesf, C.r_onesf = P.sb(es, "onesf", [128, 128], F32)
    C.eps_ln, C.r_eps_ln = P.sb(es, "epsln", [128, 1], F32)
    C.eps_rms, C.r_eps_rms = P.sb(es, "epsrms", [128, 1], F32)
    P.op("pool", nc.gpsimd.memset, [], [C.r_ident], ap=C.ident[:], constant=0.0)
    P.op("pool", nc.gpsimd.affine_select, [C.r_ident], [C.r_ident], out=C.ident[:], in_=C.ident[:], compare_op=ALU.not_equal, fill=1.0,
         base=0, pattern=[[-1, 128]], channel_multiplier=1)
    P.op("pool", nc.gpsimd.memset, [], [C.r_onesb], ap=C.onesb[:], constant=1.0)
    P.op("pool", nc.gpsimd.memset, [], [C.r_onesf], ap=C.onesf[:], constant=1.0)
    P.op("pool", nc.gpsimd.memset, [], [C.r_eps_ln], ap=C.eps_ln[:], constant=LN_EPS)
    P.op("pool", nc.gpsimd.memset, [], [C.r_eps_rms], ap=C.eps_rms[:], constant=RMS_EPS)
    return C


def build_xT(P, C, es, x_dram, r_x, xT, r_xT):
    nc = P.nc
    xb = [P.sb(es, "xb%d" % i, [128, D], BF16) for i in range(2)]
    for tt in range(NT):
        t, r = xb[tt % 2]
        P.dma("pool", t[:], x_dram[tt * 128:(tt + 1) * 128, :], reads=[r_x], writes=[r])
        for half in range(2):
            bank = 6 + half
            pst = C.ps[bank][:].bitcast(BF16)
            for k8 in range(8):
                kc = half * 8 + k8
                P.op("pe", nc.tensor.transpose, [r, C.r_ident], [C.rps[bank]], inc=(k8 == 7),
                     out=pst[:, k8 * 128:(k8 + 1) * 128], in_=t[:, kc * 128:(kc + 1) * 128], identity=C.ident[:])
            dst = xT[:, half * 8:(half + 1) * 8, tt * 128:(tt + 1) * 128]
            src = pst.rearrange("p (k t) -> p k t", k=8)
            if half == 0:
                P.op("dve", nc.vector.tensor_copy, [C.rps[bank]], [r_xT], out=dst, in_=src)
            else:
                P.op("act", nc.scalar.copy, [C.rps[bank]], [r_xT], out=dst, in_=src)


def load_wtile(P, wt, r_wt, w_dram, r_w, idx):
    P.dma("pool", wt[:, 0:8, :], w_dram[idx, :, 0:8, :], reads=[r_w], writes=[r_wt])
    P.dma("pool", wt[:, 8:16, :], w_dram[idx, :, 8:16, :], reads=[r_w], writes=[r_wt], multi=True)


def proj_tok(P, C, xT, r_xT, wt, r_wt, tt, bank, ncols=512):
    nc = P.nc
    for kc in range(KC):
        P.op("pe", nc.tensor.matmul, [r_xT, r_wt], [C.rps[bank]], inc=(kc == KC - 1),
             out=C.ps[bank][:, 0:ncols], lhsT=xT[:, kc, tt * 128:(tt + 1) * 128], rhs=wt[:, kc, 0:ncols], start=(kc == 0), stop=(kc == KC - 1))


def proj_feat(P, C, xT, r_xT, wt, r_wt, j, half, bank):
    nc = P.nc
    for kc in range(KC):
        P.op("pe", nc.tensor.matmul, [r_xT, r_wt], [C.rps[bank]], inc=(kc == KC - 1),
             out=C.ps[bank][:], lhsT=wt[:, kc, j * 128:(j + 1) * 128], rhs=xT[:, kc, half * 512:(half + 1) * 512], start=(kc == 0), stop=(kc == KC - 1))


def rope_tok(P, src, r_src, dst, r_dst, H, a, m, cos, sin, r_tab, t1, t2, r_t1, r_t2):
    nc = P.nc
    s3 = src.rearrange("p (h d) -> p h d", h=H)
    d3 = dst.rearrange("p (h d) -> p h d", h=H)
    x1 = s3[:, :, a:a + m]
    x2 = s3[:, :, a + m:a + 2 * m]
    cb = cos.unsqueeze(1).broadcast_to([128, H, m])
    sb_ = sin.unsqueeze(1).broadcast_to([128, H, m])
    u1 = t1[:, 0:H * m].rearrange("p (h d) -> p h d", h=H)
    u2 = t2[:, 0:H * m].rearrange("p (h d) -> p h d", h=H)
    TT = nc.vector.tensor_tensor
    P.op("dve", TT, [r_src, r_tab], [r_t1], out=u1, in0=x1, in1=cb, op=ALU.mult)
    P.op("dve", TT, [r_src, r_tab], [r_t2], out=u2, in0=x2, in1=sb_, op=ALU.mult)
    P.op("dve", TT, [r_t1, r_t2], [r_dst], out=d3[:, :, a:a + m], in0=u1, in1=u2, op=ALU.subtract)
    P.op("dve", TT, [r_src, r_tab], [r_t1], out=u1, in0=x2, in1=cb, op=ALU.mult)
    P.op("dve", TT, [r_src, r_tab], [r_t2], out=u2, in0=x1, in1=sb_, op=ALU.mult)
    P.op("dve", TT, [r_t1, r_t2], [r_dst], out=d3[:, :, a + m:a + 2 * m], in0=u1, in1=u2, op=ALU.add)


T_CA, T_SA, T_CD, T_SD, T_CR, T_SR, T_CC, T_SC = 0, 8, 16, 32, 48, 80, 112, 144
NTAB = 176


class TokQK:
    def __init__(self, P, C, es, tab, r_tab):
        self.P, self.C, self.tab, self.r_tab = P, C, tab, r_tab
        self.hsb = [P.sb(es, "hsb%d" % i, [128, 512], F32) for i in range(2)]
        self.hn = [P.sb(es, "hn%d" % i, [128, 512], F32) for i in range(2)]
        self.qb = [P.sb(es, "qb%d" % i, [128, 512], BF16) for i in range(2)]
        self.t1 = [P.sb(es, "rt1%d" % i, [128, 256], F32) for i in range(2)]
        self.t2 = [P.sb(es, "rt2%d" % i, [128, 256], F32) for i in range(2)]
        self.ssq, self.r_ssq = P.sb(es, "ssq", [128, 8], F32)
        self.junk, self.r_junk = P.sb(es, "junk", [128, 128], F32)
        self.n = 0
        self.pending = None

    def run(self, xT, r_xT, wt, r_wt, tt, kind, nheads, dst_fn, g_bc=None, r_g=None, extra_v=None):
        P, C, nc = self.P, self.C, self.P.nc
        i = self.n % 2
        self.n += 1
        bank = 4 + i
        proj_tok(P, C, xT, r_xT, wt, r_wt, tt, bank)
        self.flush()
        hsb, r_hsb = self.hsb[i]
        hn, r_hn = self.hn[i]
        qb, r_qb = self.qb[i]
        t1, r_t1 = self.t1[i]
        t2, r_t2 = self.t2[i]
        tab, r_tab = self.tab, self.r_tab
        ps = C.ps[bank]
        nq = nheads * 128 if kind == "C" else 512
        P.op("act", nc.scalar.copy, [C.rps[bank]], [r_hsb], out=hsb[:], in_=ps[:])
        if extra_v is not None:
            extra_v(hsb, r_hsb)
        if kind in ("A", "D"):
            P.op("act", nc.scalar.copy, [r_hsb], [r_qb], out=qb[:], in_=hsb[:])
            if kind == "A":
                rope_tok(P, hsb[:], r_hsb, qb[:], r_qb, 8, 0, 8, tab[:, tt, T_CA:T_CA + 8], tab[:, tt, T_SA:T_SA + 8], r_tab, t1, t2, r_t1, r_t2)
            else:
                rope_tok(P, hsb[:], r_hsb, qb[:], r_qb, 4, 0, 16, tab[:, tt, T_CD:T_CD + 16], tab[:, tt, T_SD:T_SD + 16], r_tab, t1, t2, r_t1, r_t2)
        else:
            ssq, r_ssq = self.ssq, self.r_ssq
            P.op("dve", nc.vector.tensor_tensor, [r_hsb], [r_hn], out=hn[:, 0:nq], in0=hsb[:, 0:nq], in1=hsb[:, 0:nq], op=ALU.mult)
            P.op("dve", nc.vector.tensor_reduce, [r_hn], [r_ssq], out=ssq[:, 0:nheads], in_=hn[:, 0:nq].rearrange("p (h d) -> p h d", h=nheads),
                 axis=mybir.AxisListType.X, op=ALU.add)
            P.op("act", nc.scalar.activation, [r_ssq, C.r_eps_rms], [r_ssq], out=ssq[:, 0:nheads], in_=ssq[:, 0:nheads], func=AF.Sqrt,
                 scale=1.0 / 128, bias=C.eps_rms[:])
            P.op("dve", nc.vector.reciprocal, [r_ssq], [r_ssq], out=ssq[:, 0:nheads], in_=ssq[:, 0:nheads])
            for h in range(nheads):
                P.op("dve", nc.vector.scalar_tensor_tensor, [r_hsb, r_ssq, r_g], [r_hn], out=hn[:, h * 128:(h + 1) * 128],
                     in0=hsb[:, h * 128:(h + 1) * 128], scalar=ssq[:, h:h + 1], in1=g_bc[:], op0=ALU.mult, op1=ALU.mult)
            rope_tok(P, hn[:, 0:nq], r_hn, qb[:, 0:nq], r_qb, nheads, 0, 32, tab[:, tt, T_CR:T_CR + 32], tab[:, tt, T_SR:T_SR + 32], r_tab, t1, t2, r_t1, r_t2)
            rope_tok(P, hn[:, 0:nq], r_hn, qb[:, 0:nq], r_qb, nheads, 64, 32, tab[:, tt, T_CC:T_CC + 32], tab[:, tt, T_SC:T_SC + 32], r_tab, t1, t2, r_t1, r_t2)
        nj = nq // 128
        tb = 6 + i

        def tail(qb=qb, r_qb=r_qb, nj=nj, tb=tb, dst_fn=dst_fn):
            pst = C.ps[tb][:].bitcast(BF16)
            for j in range(nj):
                P.op("pe", nc.tensor.transpose, [r_qb, C.r_ident], [C.rps[tb]], inc=(j == nj - 1),
                     out=pst[:, j * 128:(j + 1) * 128], in_=qb[:, j * 128:(j + 1) * 128], identity=C.ident[:])
            ncp = 0
            for j in range(nj):
                dl = dst_fn(j)
                if isinstance(dl, tuple):
                    dl = [(dl[0], dl[1], 0, 128)]
                for (dst, r_dst, p0, p1) in dl:
                    if ncp % 2 == 0:
                        P.op("dve", nc.vector.tensor_copy, [C.rps[tb]], [r_dst], out=dst, in_=pst[p0:p1, j * 128:(j + 1) * 128])
                    else:
                        P.op("act", nc.scalar.copy, [C.rps[tb]], [r_dst], out=dst, in_=pst[p0:p1, j * 128:(j + 1) * 128])
                    ncp += 1
        self.pending = tail

    def flush(self):
        if self.pending is not None:
            self.pending()
            self.pending = None


def layer_norm_tile(P, C, xt, r_xt, g_bc, b_bc, r_gb, stats, r_stats, mv, r_mv):
    nc = P.nc
    for c in range(4):
        P.op("dve", nc.vector.bn_stats, [r_xt], [r_stats], out=stats[:, c * 6:(c + 1) * 6], in_=xt[:, c * 512:(c + 1) * 512])
    P.op("dve", nc.vector.bn_aggr, [r_stats], [r_mv], out=mv[:, 0:2], in_=stats[:, 0:24])
    P.op("act", nc.scalar.activation, [r_mv, C.r_eps_ln], [r_mv], out=mv[:, 2:3], in_=mv[:, 1:2], func=AF.Sqrt, scale=1.0, bias=C.eps_ln[:])
    P.op("dve", nc.vector.reciprocal, [r_mv], [r_mv], out=mv[:, 2:3], in_=mv[:, 2:3])
    P.op("dve", nc.vector.tensor_scalar, [r_xt, r_mv], [r_xt], out=xt[:], in0=xt[:], scalar1=mv[:, 0:1], scalar2=mv[:, 2:3],
         op0=ALU.subtract, op1=ALU.mult)
    P.op("pool", nc.gpsimd.tensor_tensor, [r_xt, r_gb], [r_xt], out=xt[:], in0=xt[:], in1=g_bc[:], op=ALU.mult)
    P.op("dve", nc.vector.tensor_tensor, [r_xt, r_gb], [r_xt], out=xt[:], in0=xt[:], in1=b_bc[:], op=ALU.add)


def emit_E(P, C, es, x, r_in, g, b, xo, r_out):
    gb, r_gb = P.sb(es, "gb", [128, D], F32)
    bb, _ = P.sb(es, "bb", [128, D], F32)
    P.dma("sp", gb[:], g, writes=[r_gb])
    P.dma("sp", bb[:], b, writes=[r_gb], multi=True)
    xts = [P.sb(es, "xt%d" % i, [128, D], F32) for i in range(2)]
    stats, r_stats = P.sb(es, "stats", [128, 24], F32)
    mv, r_mv = P.sb(es, "mv", [128, 4], F32)
    for tt in range(NT):
        xt, r_xt = xts[tt % 2]
        P.dma("sp", xt[:], x[tt * 128:(tt + 1) * 128, :], reads=[r_in], writes=[r_xt])
        layer_norm_tile(P, C, xt, r_xt, gb, bb, r_gb, stats, r_stats, mv, r_mv)
        P.dma("sp", xo[tt * 128:(tt + 1) * 128, :], xt[:], reads=[r_xt], writes=[r_out], semres=r_xt, multi=True)


def build_E():
    nc = bass.Bass("TRN2", target_bir_lowering=False)
    x = nc.dram_tensor("x", [TPC, D], F32, kind="ExternalInput").ap()
    g = nc.dram_tensor("g", [128, D], F32, kind="ExternalInput").ap()
    b = nc.dram_tensor("b", [128, D], F32, kind="ExternalInput").ap()
    xo = nc.dram_tensor("xo", [TPC, D], F32, kind="ExternalOutput").ap()
    with ExitStack() as es:
        P = Prog(nc, es)
        C = setup_common(P, es)
        emit_E(P, C, es, x, P.res("in"), g, b, xo, P.res("out"))
        P.barrier()
        P.flush()
    return nc


NKT = 22
NV = 2816


def emit_P(P, C, es, x, r_x, wP, r_wP, tabd, gkd, kT, r_kT, v, r_v):
    nc = P.nc
    xT, r_xT = P.sb(es, "xT", [128, KC, TPC], BF16)
    tab, r_tab = P.sb(es, "tab", [128, NT, NTAB], F32)
    gk, r_gk = P.sb(es, "gk", [128, 128], F32)
    P.dma("sp", tab[:], tabd, writes=[r_tab])
    P.dma("sp", gk[:], gkd, writes=[r_gk])
    build_xT(P, C, es, x, r_x, xT, r_xT)
    wts = [P.sb(es, "wt%d" % i, [128, KC, 512], BF16) for i in range(2)]
    kst = [P.sb(es, "kst%d" % i, [128, 4, TPC], BF16) for i in range(2)]
    vst = [P.sb(es, "vst%d" % i, [128, 512], BF16) for i in range(3)]
    tq = TokQK(P, C, es, tab, r_tab)
    order = [0, 1, 2, 3, 5, 4, 6, 7, 8, 9, 10]
    load_wtile(P, wts[0][0], wts[0][1], wP, r_wP, order[0])
    nks = 0
    nvs = 0
    for oi, ti in enumerate(order):
        wt, r_wt = wts[oi % 2]
        if oi + 1 < len(order):
            load_wtile(P, wts[(oi + 1) % 2][0], wts[(oi + 1) % 2][1], wP, r_wP, order[oi + 1])
        if ti in (0, 1, 2, 3, 5):
            ks, r_ks = kst[nks % 2]
            nks += 1
            if ti == 5:
                nch, ch0 = 2, 4
            else:
                nch, ch0 = 4, (0 if ti == 0 else 6 + (ti - 1) * 4)
            for tt in range(NT):
                dst_fn = (lambda j, ks=ks, r_ks=r_ks, tt=tt: (ks[:, j, tt * 128:(tt + 1) * 128], r_ks))
                if ti == 5:
                    vs, r_vs = vst[nvs % 3]
                    nvs += 1

                    def extra(hsb, r_hsb, vs=vs, r_vs=r_vs, tt=tt):
                        P.op("act", nc.scalar.copy, [r_hsb], [r_vs], out=vs[:, 0:256], in_=hsb[:, 256:512])
                        P.dma("sp", v[tt * 128:(tt + 1) * 128, 1024:1280], vs[:, 0:256], reads=[r_vs], writes=[r_v], semres=r_vs, multi=True)
                    tq.run(xT, r_xT, wt, r_wt, tt, "C", 2, dst_fn, g_bc=gk, r_g=r_gk, extra_v=extra)
                else:
                    tq.run(xT, r_xT, wt, r_wt, tt, "A" if ti == 0 else "D", 4, dst_fn)
            tq.flush()
            for j in range(nch):
                P.dma("sp", kT[ch0 + j], ks[:, j, :], reads=[r_ks], writes=[r_kT], semres=r_ks, multi=True)
        elif ti == 4:
            ks, r_ks = kst[nks % 2]
            nks += 1
            n = 0
            for j in range(4):
                for half in range(2):
                    bank = 4 + n % 2
                    n += 1
                    proj_feat(P, C, xT, r_xT, wt, r_wt, j, half, bank)
                    if n % 2 == 0:
                        P.op("act", nc.scalar.copy, [C.rps[bank]], [r_ks], out=ks[:, j, half * 512:(half + 1) * 512], in_=C.ps[bank][:])
                    else:
                        P.op("dve", nc.vector.tensor_copy, [C.rps[bank]], [r_ks], out=ks[:, j, half * 512:(half + 1) * 512], in_=C.ps[bank][:])
            for j in range(4):
                P.dma("sp", kT[18 + j], ks[:, j, :], reads=[r_ks], writes=[r_kT], semres=r_ks, multi=True)
        else:
            voff = {6: 0, 7: 512, 8: 1280, 9: 1792, 10: 2304}[ti]
            for tt in range(NT):
                bank = 4 + tt % 2
                proj_tok(P, C, xT, r_xT, wt, r_wt, tt, bank)
                vs, r_vs = vst[nvs % 3]
                nvs += 1
                if tt % 2 == 0:
                    P.op("act", nc.scalar.copy, [C.rps[bank]], [r_vs], out=vs[:], in_=C.ps[bank][:])
                else:
                    P.op("dve", nc.vector.tensor_copy, [C.rps[bank]], [r_vs], out=vs[:], in_=C.ps[bank][:])
                P.dma("sp", v[tt * 128:(tt + 1) * 128, voff:voff + 512], vs[:], reads=[r_vs], writes=[r_v], semres=r_vs, multi=True)


def build_P():
    nc = bass.Bass("TRN2", target_bir_lowering=False)
    x = nc.dram_tensor("x", [TPC, D], F32, kind="ExternalInput").ap()
    wP = nc.dram_tensor("wP", [11, 128, KC, 512], F32, kind="ExternalInput").ap()
    tabd = nc.dram_tensor("tab", [128, NT, NTAB], F32, kind="ExternalInput").ap()
    gkd = nc.dram_tensor("gk", [128, 128], F32, kind="ExternalInput").ap()
    kT = nc.dram_tensor("kT", [NKT, 128, TPC], BF16, kind="ExternalOutput").ap()
    v = nc.dram_tensor("v", [TPC, NV], BF16, kind="ExternalOutput").ap()
    with ExitStack() as es:
        P = Prog(nc, es)
        C = setup_common(P, es)
        emit_P(P, C, es, x, P.res("x"), wP, P.res("wP"), tabd, gkd, kT, P.res("kT"), v, P.res("v"))
        P.barrier()
        P.flush()
    return nc


BW = 1792
DW = 3072
NMASK = 22


class WStream:
    def __init__(self, P, wts, w_dram, r_w, order):
        self.P, self.wts, self.w, self.r_w, self.order = P, wts, w_dram, r_w, order
        self.i = 0
        load_wtile(P, wts[0][0], wts[0][1], w_dram, r_w, order[0])

    def get(self, expect):
        assert self.order[self.i] == expect, (self.order[self.i], expect)
        cur = self.wts[self.i % 2]
        self.i += 1
        if self.i < len(self.order):
            nxt = self.wts[self.i % 2]
            load_wtile(self.P, nxt[0], nxt[1], self.w, self.r_w, self.order[self.i])
        return cur


def emit_silu_z(P, C, xT, r_xT, wt, r_wt, siluz, r_siluz):
    nc = P.nc
    n = 0
    for j in range(4):
        for half in range(2):
            bank = 4 + n % 2
            n += 1
            proj_feat(P, C, xT, r_xT, wt, r_wt, j, half, bank)
            P.op("act", nc.scalar.activation, [C.rps[bank]], [r_siluz], out=siluz[:, j, half * 512:(half + 1) * 512], in_=C.ps[bank][:], func=AF.Silu)


class Grp:
    pass


def dense_attention(P, C, groups, pT, scale, accs):
    nc = P.nc
    SB = (0, 1, 2, 7)
    items = [(gi, kc) for gi in range(len(groups)) for kc in range(groups[gi].nk)]
    n = len(items)

    def qk(idx):
        gi, kc = items[idx]
        g = groups[gi]
        sbk = SB[idx % 4]
        P.op("pe", nc.tensor.matmul, [g.r_q, g.r_kfn(kc)], [C.rps[sbk]], out=C.ps[sbk][:], lhsT=g.kfn(kc), rhs=g.q, start=True, stop=True)

    for i0 in range(min(3, n)):
        qk(i0)
    for idx in range(n):
        gi, kc = items[idx]
        g = groups[gi]
        sbk = SB[idx % 4]
        pt, r_pt = pT[idx % len(pT)]
        P.op("act", nc.scalar.activation, [C.rps[sbk]], [r_pt], out=pt[:], in_=C.ps[sbk][:], func=AF.Exp, scale=scale)
        last = (kc == g.nk - 1)
        P.op("pe", nc.tensor.matmul, [r_pt, g.r_vfn(kc)], [C.rps[g.ob]], out=C.ps[g.ob][:], lhsT=g.vfn(kc), rhs=pt[:], start=(kc == 0), stop=last)
        ac, r_ac = accs[(gi % 2) * 2 + kc % 2]
        if kc < 2:
            P.op("dve", nc.vector.tensor_copy, [r_pt], [r_ac], out=ac[:], in_=pt[:])
        else:
            P.op("dve", nc.vector.tensor_tensor, [r_pt, r_ac], [r_ac], out=ac[:], in0=ac[:], in1=pt[:], op=ALU.add)
        if idx + 3 < n:
            qk(idx + 3)
        if last:
            for sl in range(2):
                a_, r_a_ = accs[(gi % 2) * 2 + sl]
                P.op("pe", nc.tensor.matmul, [r_a_, C.r_onesf], [C.rps[g.db]], inc=(sl == 1), out=C.ps[g.db][:], lhsT=C.onesf[:], rhs=a_[:],
                     start=(sl == 0), stop=(sl == 1))
            g.fin()


def emit_T(P, C, es, D_, lam_consts=None):
    nc = P.nc
    TT = nc.vector.tensor_tensor
    x, r_x = D_["x"], P.res("x")
    r_w = P.res("wT")
    r_kv = P.res("kvdram")
    r_xo = D_.get("r_xo") or P.res("xo")
    wts = [P.sb(es, "wt%d" % i, [128, KC, 512], BF16) for i in range(2)]
    sm, r_sm = P.sb(es, "sm", [128, 16], F32)
    bgT, r_bgT = P.sb(es, "bgT", [128, 64], F32)
    P.dma("sp", bgT[:], D_["bgT"], writes=[r_bgT])
    order = [6, 0, 8, 1, 7, 5, 9, 2, 3, 4] + list(range(10, 26))
    ws = WStream(P, wts, D_["wT"], r_w, order)

    with ExitStack() as em:
        xT, r_xT = P.sb(em, "xT", [128, KC, TPC], BF16)
        ysT, r_ysT = P.sb(em, "ysT", [128, KC, TPC], BF16)
        tab, r_tab = P.sb(em, "tab", [128, NT, NTAB], F32)
        P.dma("sp", tab[:], D_["tab"], writes=[r_tab])
        siluz, r_siluz = P.sb(em, "siluz", [128, 4, TPC], BF16)
        fin = []
        big, r_big = P.sb(em, "big", [128, 16384], BF16)
        mergedT = big[:].rearrange("p (k t) -> p k t", k=KC)
        r_mergedT = r_big

        with ExitStack() as ep:
            lp, r_lp = P.sb(ep, "lp", [128, 256], F32)
            cst, r_cst = P.sb(ep, "cst", [128, 4], F32)
            subg, r_subg = P.sb(ep, "subg", [128, 1], F32)
            pr, r_pr = P.sb(ep, "pr", [128, 128], F32)
            P.dma("sp", lp[:], D_["lp"], writes=[r_lp])
            P.dma("sp", cst[:], D_["cst"], writes=[r_cst])
            P.dma("sp", subg[:], D_["subg"], writes=[r_subg])
            P.op("dve", TT, [r_lp], [r_pr], out=pr[:].rearrange("p (a d) -> p a d", a=2), in0=lp[:].rearrange("p (a b d) -> p a b d", a=2, b=2)[:, :, 0, :],
                 in1=lp[:].rearrange("p (a b d) -> p a b d", a=2, b=2)[:, :, 1, :], op=ALU.mult)
            P.op("dve", nc.vector.tensor_reduce, [r_pr], [r_sm], out=sm[:, 3:5], in_=pr[:].rearrange("p (a d) -> p a d", a=2), axis=mybir.AxisListType.X, op=ALU.add)
            P.op("act", nc.scalar.activation, [r_sm], [r_sm], out=sm[:, 5:7], in_=sm[:, 3:5], func=AF.Exp)
            P.op("dve", TT, [r_sm], [r_sm], out=sm[:, 7:8], in0=sm[:, 5:6], in1=sm[:, 6:7], op=ALU.subtract)
            P.op("dve", TT, [r_sm, r_cst], [r_sm], out=sm[:, 0:1], in0=sm[:, 7:8], in1=cst[:, 0:1], op=ALU.add)
            P.op("dve", nc.vector.tensor_scalar, [r_sm], [r_sm], out=sm[:, 1:2], in0=sm[:, 0:1], scalar1=-1.0, scalar2=None, op0=ALU.mult)
            P.op("dve", TT, [r_subg, r_cst], [r_sm], out=sm[:, 2:3], in0=subg[:], in1=cst[:, 1:2], op=ALU.mult)
            build_xT(P, C, ep, x, r_x, xT, r_xT)
            P.barrier()
            P.flush()

        def fin_simple(ob, db, ci, j):
            rden, r_rden = fin[-2]
            on, r_on = fin[-1]
            P.op("dve", nc.vector.reciprocal, [C.rps[db]], [r_rden], out=rden[:], in_=C.ps[db][:])
            P.op("dve", TT, [C.rps[ob], r_rden], [r_on], out=on[:], in0=C.ps[ob][:], in1=rden[:], op=ALU.mult)
            P.op("dve", TT, [r_on, r_siluz], [r_ysT], out=ysT[:, ci, j * 512:(j + 1) * 512], in0=on[:], in1=siluz[:, ci % 4, j * 512:(j + 1) * 512], op=ALU.mult)

        with ExitStack() as ep:
            tq = TokQK(P, C, ep, tab, r_tab)
            pT = [P.sb(ep, "pT%d" % i, [128, 512], BF16) for i in range(4)]
            accs = [P.sb(ep, "dacc%d" % i, [128, 512], F32) for i in range(4)]
            fin = [P.sb(ep, "fin%d" % i, [128, 512], F32) for i in range(6)]
            gq, r_gq = P.sb(ep, "gq", [128, 128], F32)
            P.dma("sp", gq[:], D_["gq"], writes=[r_gq])
            qT, r_qT = P.sb(ep, "qT", [128, 4, TPC], BF16)
            qT2, r_qT2 = P.sb(ep, "qT2", [128, 4, TPC], BF16)
            P.op("pool", nc.gpsimd.memset, [], [r_qT], ap=qT[64:128, :, :], constant=0.0)
            P.op("pool", nc.gpsimd.memset, [], [r_qT2], ap=qT2[0:64, :, :], constant=0.0)
            ksb = big[:, 0:S]
            vsb = big[:, S:2 * S].rearrange("p (k d) -> p k d", k=64)
            r_kq = [P.res("kq") for _ in range(4)]
            r_vq = [P.res("vq") for _ in range(4)]

            def load_kv(kidx, vcol):
                for i in range(4):
                    P.dma("sp", ksb[:, i * 2048:(i + 1) * 2048], D_["kT_AC"][kidx, :, i * 2048:(i + 1) * 2048], reads=[r_kv], writes=[r_kq[i]])
                    P.dma("sp", vsb[:, i * 16:(i + 1) * 16, :],
                          D_["v_AC"][i * 2048:(i + 1) * 2048, vcol:vcol + 128].rearrange("(kc p) d -> p kc d", p=128), reads=[r_kv], writes=[r_vq[i]])

            wt, r_wt = ws.get(6)
            emit_silu_z(P, C, xT, r_xT, wt, r_wt, siluz, r_siluz)
            wt, r_wt = ws.get(0)
            for tt in range(NT):
                tq.run(xT, r_xT, wt, r_wt, tt, "A", 4, lambda j, tt=tt: [(qT[0:64, j, tt * 128:(tt + 1) * 128], r_qT, 0, 64),
                                                                        (qT2[64:128, j, tt * 128:(tt + 1) * 128], r_qT2, 64, 128)])
            tq.flush()
            ng = 0
            for h in range(4):
                load_kv(h, h * 128)
                groups = []
                for j in range(2):
                    for c in range(2):
                        g = Grp()
                        g.q = (qT if c == 0 else qT2)[:, h, j * 512:(j + 1) * 512]
                        g.r_q = (r_qT if c == 0 else r_qT2)
                        g.kfn = lambda kc: ksb[:, kc * 128:(kc + 1) * 128]
                        g.r_kfn = lambda kc: r_kq[kc // 16]
                        g.vfn = lambda kc: vsb[:, kc, :]
                        g.r_vfn = lambda kc: r_vq[kc // 16]
                        g.nk = 64
                        g.ob, g.db = (3, 4) if ng % 2 == 0 else (5, 6)
                        ng += 1

                        def fin_a(g=g, c=c, j=j, h=h):
                            rden, r_rden = fin[0]
                            P.op("dve", nc.vector.reciprocal, [C.rps[g.db]], [r_rden], out=rden[:], in_=C.ps[g.db][:])
                            if c == 0:
                                o1, r_o1 = fin[1]
                                P.op("dve", TT, [C.rps[g.ob], r_rden], [r_o1], out=o1[:], in0=C.ps[g.ob][:], in1=rden[:], op=ALU.mult)
                                return
                            o1, r_o1 = fin[1]
                            o2, r_o2 = fin[2]
                            dd, r_dd = fin[3]
                            sq, r_sq = fin[4]
                            rs, r_rs = fin[5]
                            P.op("dve", TT, [C.rps[g.ob], r_rden], [r_o2], out=o2[:], in0=C.ps[g.ob][:], in1=rden[:], op=ALU.mult)
                            P.op("dve", nc.vector.scalar_tensor_tensor, [r_o2, r_o1, r_sm], [r_dd], out=dd[:], in0=o2[:], scalar=sm[:, 1:2], in1=o1[:],
                                 op0=ALU.mult, op1=ALU.add)
                            P.op("dve", TT, [r_dd], [r_sq], out=sq[:], in0=dd[:], in1=dd[:], op=ALU.mult)
                            P.op("pe", nc.tensor.matmul, [r_sq, C.r_onesf], [C.rps[7]], out=C.ps[7][:], lhsT=C.onesf[:], rhs=sq[:], start=True, stop=True)
                            P.op("act", nc.scalar.activation, [C.rps[7], C.r_eps_rms], [r_rs], out=rs[:], in_=C.ps[7][:], func=AF.Sqrt, scale=1.0 / 128, bias=C.eps_rms[:])
                            P.op("dve", nc.vector.reciprocal, [r_rs], [r_rs], out=rs[:], in_=rs[:])
                            P.op("dve", nc.vector.scalar_tensor_tensor, [r_dd, r_rs, r_sm], [r_sq], out=sq[:], in0=dd[:], scalar=sm[:, 2:3], in1=rs[:],
                                 op0=ALU.mult, op1=ALU.mult)
                            P.op("dve", TT, [r_sq, r_siluz], [r_ysT], out=ysT[:, h, j * 512:(j + 1) * 512], in0=sq[:], in1=siluz[:, h, j * 512:(j + 1) * 512], op=ALU.mult)
                        g.fin = fin_a
                        groups.append(g)
                dense_attention(P, C, groups, pT, 0.125, accs)
            wt, r_wt = ws.get(8)
            emit_silu_z(P, C, xT, r_xT, wt, r_wt, siluz, r_siluz)
            wt, r_wt = ws.get(1)
            for tt in range(NT):
                tq.run(xT, r_xT, wt, r_wt, tt, "C", 4, lambda j, tt=tt: (qT[:, j, tt * 128:(tt + 1) * 128], r_qT), g_bc=gq, r_g=r_gq)
            tq.flush()
            for kv in range(2):
                load_kv(4 + kv, 512 + kv * 128)
                groups = []
                for gg in range(2):
                    hq = kv * 2 + gg
                    for j in range(2):
                        g = Grp()
                        g.q = qT[:, hq, j * 512:(j + 1) * 512]
                        g.r_q = r_qT
                        g.kfn = lambda kc: ksb[:, kc * 128:(kc + 1) * 128]
                        g.r_kfn = lambda kc: r_kq[kc // 16]
                        g.vfn = lambda kc: vsb[:, kc, :]
                        g.r_vfn = lambda kc: r_vq[kc // 16]
                        g.nk = 64
                        g.ob, g.db = (3, 4) if ng % 2 == 0 else (5, 6)
                        ng += 1
                        g.fin = (lambda g=g, hq=hq, j=j: fin_simple(g.ob, g.db, 8 + hq, j))
                        groups.append(g)
                dense_attention(P, C, groups, pT, 128 ** -0.5, accs)
            P.barrier()
            P.flush()

        sc128 = 128 ** -0.5
        with ExitStack() as ep:
            ksbs = [P.sb(ep, "ksbB%d" % i, [128, BW], BF16) for i in range(2)]
            vsbs = [P.sb(ep, "vsbB%d" % i, [128, 14, 128], BF16) for i in range(2)]
            bias = [P.sb(ep, "biasB%d" % i, [128, 7, 128], F32) for i in range(2)]
            tmp = [P.sb(ep, "tmpB%d" % i, [128, 7, 128], F32) for i in range(2)]
            pB = [P.sb(ep, "pB%d" % i, [128, 7, 128], BF16) for i in range(2)]
            fin = [P.sb(ep, "finB%d" % i, [128, 512], F32) for i in range(2)]
            qT, r_qT = P.sb(ep, "qTB", [128, 4, TPC], BF16)
            wt, r_wt = ws.get(7)
            emit_silu_z(P, C, xT, r_xT, wt, r_wt, siluz, r_siluz)
            wt, r_wt = ws.get(5)
            n = 0
            for j in range(4):
                for half in range(2):
                    bank = 4 + n % 2
                    n += 1
                    proj_feat(P, C, xT, r_xT, wt, r_wt, j, half, bank)
                    P.op("dve", nc.vector.tensor_copy, [C.rps[bank]], [r_qT], out=qT[:, j, half * 512:(half + 1) * 512], in_=C.ps[bank][:])
            itemsB = [(h, qt) for h in range(4) for qt in range(NT)]

            def b_stage1(i):
                h, qt = itemsB[i]
                ksb, r_ksb = ksbs[h % 2]
                vsb, r_vsb = vsbs[h % 2]
                if qt == 0:
                    P.dma("sp", ksb[:], D_["kT_Bw"][h], reads=[r_kv], writes=[r_ksb])
                    P.dma("sp", vsb[:], D_["v_Bw"][:, h * 128:(h + 1) * 128].rearrange("(ch p) d -> p ch d", p=128), reads=[r_kv], writes=[r_vsb])
                bt, r_bt = bias[i % 2]
                tp, r_tp = tmp[i % 2]
                pb, r_pb = pB[i % 2]
                P.dma("sp", bt[:], D_["biasB"][qt, h], reads=[r_kv], writes=[r_bt])
                sb0, sb1 = (0, 1) if i % 2 == 0 else (2, 7)
                qap = qT[:, h, qt * 128:(qt + 1) * 128]
                for kc in range(7):
                    bk = sb0 if kc < 4 else sb1
                    co = (kc % 4) * 128
                    P.op("pe", nc.tensor.matmul, [r_qT, r_ksb], [C.rps[bk]], inc=(kc in (3, 6)), out=C.ps[bk][:, co:co + 128],
                         lhsT=ksb[:, (qt + kc) * 128:(qt + kc + 1) * 128], rhs=qap, start=True, stop=True)
                P.op("dve", nc.vector.scalar_tensor_tensor, [C.rps[sb0], r_bt], [r_tp], out=tp[:, 0:4, :], in0=C.ps[sb0][:].rearrange("p (k q) -> p k q", k=4),
                     scalar=sc128, in1=bt[:, 0:4, :], op0=ALU.mult, op1=ALU.add)
                P.op("dve", nc.vector.scalar_tensor_tensor, [C.rps[sb1], r_bt], [r_tp], out=tp[:, 4:7, :], in0=C.ps[sb1][:, 0:384].rearrange("p (k q) -> p k q", k=3),
                     scalar=sc128, in1=bt[:, 4:7, :], op0=ALU.mult, op1=ALU.add)
                P.op("act", nc.scalar.activation, [r_tp], [r_pb], out=pb[:], in_=tp[:], func=AF.Exp)

            def b_stage2(i):
                h, qt = itemsB[i]
                vsb, r_vsb = vsbs[h % 2]
                pb, r_pb = pB[i % 2]
                jq = qt // 4
                ob, db = (3, 4) if (h * 2 + jq) % 2 == 0 else (5, 6)
                co = (qt % 4) * 128
                for kc in range(7):
                    P.op("pe", nc.tensor.matmul, [r_pb, r_vsb], [C.rps[ob]], inc=False, out=C.ps[ob][:, co:co + 128], lhsT=vsb[:, qt + kc, :], rhs=pb[:, kc, :],
                         start=(kc == 0), stop=(kc == 6))
                    P.op("pe", nc.tensor.matmul, [r_pb, C.r_onesb], [C.rps[db]], inc=(kc == 6), out=C.ps[db][:, co:co + 128], lhsT=C.onesb[:], rhs=pb[:, kc, :],
                         start=(kc == 0), stop=(kc == 6))
                if qt % 4 == 3:
                    fin_simple(ob, db, 4 + h, jq)

            b_stage1(0)
            for i in range(len(itemsB)):
                if i + 1 < len(itemsB):
                    b_stage1(i + 1)
                b_stage2(i)
            P.barrier()
            P.flush()

        with ExitStack() as ep:
            tq = TokQK(P, C, ep, tab, r_tab)
            qTD = big[:, 0:12 * TPC].rearrange("p (h t) -> p h t", h=12)
            r_qTD = r_big
            ksbs = [P.sb(ep, "ksbD%d" % i, [128, DW], BF16) for i in range(2)]
            vsbs = [P.sb(ep, "vsbD%d" % i, [128, 32, 128], BF16) for i in range(2)]
            mk, r_mk = P.sb(ep, "maskD", [128, NMASK, 128], F32)
            oacc, r_oacc = P.sb(ep, "oacc", [128, TPC], F32)
            dacc, r_dacc = P.sb(ep, "dacc", [128, TPC], F32)
            tmp = [P.sb(ep, "tmpD%d" % i, [128, 128], F32) for i in range(3)]
            pD = [P.sb(ep, "pD%d" % i, [128, 128], BF16) for i in range(3)]
            P.dma("sp", mk[:], D_["maskD"], writes=[r_mk])
            wt, r_wt = ws.get(9)
            emit_silu_z(P, C, xT, r_xT, wt, r_wt, siluz, r_siluz)
            for g in range(3):
                wt, r_wt = ws.get(2 + g)
                for tt in range(NT):
                    tq.run(xT, r_xT, wt, r_wt, tt, "D", 4, lambda j, tt=tt, g=g: (qTD[:, g * 4 + j, tt * 128:(tt + 1) * 128], r_qTD))
            tq.flush()
            mbase = [0, 16, 20]
            blocks = [(hh, g) for hh in range(4) for g in range(3)]

            def d_load(bi):
                hh, g = blocks[bi]
                ksb, r_ksb = ksbs[bi % 2]
                vsb, r_vsb = vsbs[bi % 2]
                r = DIL_RATES[g]
                M = {1: 9, 4: 3, 16: 2}[r]
                W = TPC + 128 * r
                P.dma("sp", ksb[:, 0:W], D_["kT_Dw"][g * 4 + hh, :, 1024 - 64 * r:1024 - 64 * r + W], reads=[r_kv], writes=[r_ksb])
                c0 = g * 512 + hh * 128
                vsrc = D_["v_Dw"]
                if r == 1:
                    P.dma("sp", vsb[:, 0:9, :], vsrc[960:960 + 1152, c0:c0 + 128].rearrange("(m a) d -> a m d", a=128), reads=[r_kv], writes=[r_vsb])
                elif r == 4:
                    for m in range(3):
                        P.dma("sp", vsb[:, ss(m, 4, 3), :], vsrc[768 + 512 * m:768 + 512 * m + 512, c0:c0 + 128].rearrange("(a r) d -> a r d", r=4),
                              reads=[r_kv], writes=[r_vsb])
                else:
                    for r0 in (0, 8):
                        P.dma("sp", vsb[:, ss(2 * r0, 8, 2), :], vsrc[0:2048, c0:c0 + 128].rearrange("(a r) d -> a r d", r=16)[:, r0:r0 + 8, :],
                              reads=[r_kv], writes=[r_vsb])
                    P.dma("sp", vsb[0:64, ss(1, 16, 2), :], vsrc[2048:3072, c0:c0 + 128].rearrange("(a r) d -> a r d", r=16), reads=[r_kv], writes=[r_vsb])

            itc = [0]

            def d_compute(bi):
                hh, g = blocks[bi]
                ksb, r_ksb = ksbs[bi % 2]
                vsb, r_vsb = vsbs[bi % 2]
                r = DIL_RATES[g]
                n_ = TPC // r
                nq = min(128, n_)
                nqt = n_ // nq
                M = {1: 9, 4: 3, 16: 2}[r]
                ob0, db0 = 3, 5
                its = [(rho, qti, ch) for rho in range(r) for qti in range(nqt) for ch in range(2)]

                def s1(k):
                    rho, qti, ch = its[k]
                    U = qti * nq
                    idx = itc[0] + k
                    sbk = idx % 3
                    tp, r_tp = tmp[idx % 3]
                    pd, r_pd = pD[idx % 3]
                    u0 = U - 64 + 128 * ch
                    nk = 128 if (ch == 0 or nq == 128) else 64
                    qap = qTD[:, g * 4 + hh, ss(rho + r * U, nq, r)]
                    kap = ksb[:, ss(r * (u0 + 64) + rho, nk, r)]
                    P.op("pe", nc.tensor.matmul, [r_qTD, r_ksb], [C.rps[sbk]], out=C.ps[sbk][0:nk, 0:nq], lhsT=kap, rhs=qap, start=True, stop=True)
                    mi = mbase[g] + qti * 2 + ch
                    P.op("dve", nc.vector.scalar_tensor_tensor, [C.rps[sbk], r_mk], [r_tp], out=tp[0:nk, 0:nq], in0=C.ps[sbk][0:nk, 0:nq],
                         scalar=sc128, in1=mk[0:nk, mi, 0:nq], op0=ALU.mult, op1=ALU.add)
                    P.op("act", nc.scalar.activation, [r_tp], [r_pd], out=pd[0:nk, 0:nq], in_=tp[0:nk, 0:nq], func=AF.Exp)

                def s2(k):
                    rho, qti, ch = its[k]
                    U = qti * nq
                    idx = itc[0] + k
                    pd, r_pd = pD[idx % 3]
                    nk = 128 if (ch == 0 or nq == 128) else 64
                    col0 = rho * n_ + U
                    ob = ob0 + col0 // 512
                    db = db0 + col0 // 512
                    cofs = col0 % 512
                    vidx = rho * M + (qti + ch)
                    P.op("pe", nc.tensor.matmul, [r_pd, r_vsb], [C.rps[ob]], inc=False, out=C.ps[ob][:, cofs:cofs + nq], lhsT=vsb[0:nk, vidx, :],
                         rhs=pd[0:nk, 0:nq], start=(ch == 0), stop=(ch == 1))
                    P.op("pe", nc.tensor.matmul, [r_pd, C.r_onesb], [C.rps[db]], out=C.ps[db][:, cofs:cofs + nq], lhsT=C.onesb[0:nk, :],
                         rhs=pd[0:nk, 0:nq], start=(ch == 0), stop=(ch == 1))

                nI = len(its)
                s1(0)
                if nI > 1:
                    s1(1)
                for k in range(nI):
                    s2(k)
                    if k + 2 < nI:
                        s1(k + 2)
                itc[0] += nI
                for (acc, r_acc, b0) in ((oacc, r_oacc, ob0), (dacc, r_dacc, db0)):
                    for b in range(2):
                        if r == 1:
                            dst = acc[:, b * 512:(b + 1) * 512]
                            src = C.ps[b0 + b][:]
                        else:
                            dst = acc[:].rearrange("p (u r) -> p r u", r=r)[:, b * r // 2:(b + 1) * r // 2, :]
                            src = C.ps[b0 + b][:].rearrange("p (r u) -> p r u", u=n_)
                        if g == 0:
                            P.op("dve", nc.vector.tensor_copy, [C.rps[b0 + b]], [r_acc], out=dst, in_=src)
                        else:
                            P.op("dve", TT, [C.rps[b0 + b], r_acc], [r_acc], out=dst, in0=dst, in1=src, op=ALU.add)
                if g == 2:
                    P.op("dve", nc.vector.reciprocal, [r_dacc], [r_dacc], out=dacc[:], in_=dacc[:])
                    P.op("dve", TT, [r_oacc, r_dacc], [r_oacc], out=oacc[:], in0=oacc[:], in1=dacc[:], op=ALU.mult)
                    P.op("dve", TT, [r_oacc, r_siluz], [r_ysT], out=ysT[:, 12 + hh, :], in0=oacc[:], in1=siluz[:, hh, :], op=ALU.mult)

            d_load(0)
            for bi in range(len(blocks)):
                if bi + 1 < len(blocks):
                    d_load(bi + 1)
                d_compute(bi)
            P.barrier()
            P.flush()

        wo0, r_wo0 = P.sb(em, "wo0", [128, KC, 512], BF16)
        r_wod = P.res("wod")
        load_wtile(P, wo0, r_wo0, D_["wo"], r_wod, 0)
        with ExitStack() as ep:
            wbt = [P.sb(ep, "wbt%d" % i, [128, 16, 128], BF16) for i in range(2)]
            gs = [P.sb(ep, "gs%d" % i, [128, 512], F32) for i in range(3)]
            macc = [P.sb(ep, "macc%d" % i, [128, 512], F32) for i in range(2)]
            mt = [P.sb(ep, "mt%d" % i, [128, 512], F32) for i in range(2)]
            r_wbr = P.res("wbr")
            P.dma("pool", wbt[0][0][:], D_["wbr"][0], reads=[r_wbr], writes=[wbt[0][1]])
            it = 0
            for dc in range(KC):
                wt, r_wt = ws.get(10 + dc)
                wb, r_wb = wbt[dc % 2]
                if dc + 1 < KC:
                    P.dma("pool", wbt[(dc + 1) % 2][0][:], D_["wbr"][dc + 1], reads=[r_wbr], writes=[wbt[(dc + 1) % 2][1]])
                for n in range(4):
                    for half in range(2):
                        gb = it % 4
                        pb = 4 + it % 4
                        g_, r_g_ = gs[it % 3]
                        it += 1
                        for kc in range(KC):
                            P.op("pe", nc.tensor.matmul, [r_xT, r_wt], [C.rps[gb]], inc=(kc == KC - 1), out=C.ps[gb][:], lhsT=wt[:, kc, n * 128:(n + 1) * 128],
                                 rhs=xT[:, kc, half * 512:(half + 1) * 512], start=(kc == 0), stop=(kc == KC - 1))
                        P.op("act", nc.scalar.activation, [C.rps[gb], r_bgT], [r_g_], out=g_[:], in_=C.ps[gb][:], func=AF.Sigmoid, bias=bgT[:, n * 16 + dc:n * 16 + dc + 1], scale=1.0)
                        for wc in range(4):
                            P.op("pe", nc.tensor.matmul, [r_ysT, r_wb], [C.rps[pb]], inc=(wc == 3), out=C.ps[pb][:], lhsT=wb[:, n * 4 + wc, :],
                                 rhs=ysT[:, n * 4 + wc, half * 512:(half + 1) * 512], start=(wc == 0), stop=(wc == 3))
                        ma, r_ma = macc[half]
                        if n == 0:
                            P.op("dve", TT, [C.rps[pb], r_g_], [r_ma], out=ma[:], in0=C.ps[pb][:], in1=g_[:], op=ALU.mult)
                        else:
                            t_, r_t_ = mt[half]
                            P.op("dve", TT, [C.rps[pb], r_g_], [r_t_], out=t_[:], in0=C.ps[pb][:], in1=g_[:], op=ALU.mult)
                            if n < 3:
                                P.op("dve", TT, [r_ma, r_t_], [r_ma], out=ma[:], in0=ma[:], in1=t_[:], op=ALU.add)
                            else:
                                P.op("dve", TT, [r_ma, r_t_], [r_mergedT], out=mergedT[:, dc, half * 512:(half + 1) * 512], in0=ma[:], in1=t_[:], op=ALU.add)
            P.barrier()
            P.flush()

        with ExitStack() as ep:
            r_wx1, r_wy0, r_wy1 = P.res("wx1"), P.res("wy0"), P.res("wy1")
            wos = [(wo0, r_wo0), (xT[:, :, 512:1024], r_wx1), (ysT[:, :, 0:512], r_wy0), (ysT[:, :, 512:1024], r_wy1)]
            for cc in range(1, 4):
                load_wtile(P, wos[cc][0], wos[cc][1], D_["wo"], r_wod, cc)
            gb, r_gb = P.sb(ep, "lng", [128, D], F32)
            bb, _ = P.sb(ep, "lnb", [128, D], F32)
            P.dma("sp", gb[:], D_["lng"], writes=[r_gb])
            P.dma("sp", bb[:], D_["lnb"], writes=[r_gb])
            xts = [P.sb(ep, "xt%d" % i, [128, D], F32) for i in range(2)]
            stats, r_stats = P.sb(ep, "stats", [128, 24], F32)
            mv, r_mv = P.sb(ep, "mv", [128, 4], F32)
            for tt in range(NT):
                xt, r_xt = xts[tt % 2]
                P.dma("sp", xt[:], x[tt * 128:(tt + 1) * 128, :], reads=[r_x], writes=[r_xt])
                for cc in range(4):
                    bank = (tt % 2) * 4 + cc
                    wo_, r_wo_ = wos[cc]
                    for dc in range(KC):
                        P.op("pe", nc.tensor.matmul, [r_mergedT, r_wo_], [C.rps[bank]], inc=(dc == KC - 1), out=C.ps[bank][:], lhsT=mergedT[:, dc, tt * 128:(tt + 1) * 128],
                             rhs=wo_[:, dc, :], start=(dc == 0), stop=(dc == KC - 1))
                    P.op("dve", nc.vector.scalar_tensor_tensor, [r_xt, C.rps[bank]], [r_xt], out=xt[:, cc * 512:(cc + 1) * 512], in0=xt[:, cc * 512:(cc + 1) * 512],
                         scalar=float(DN_ALPHA), in1=C.ps[bank][:], op0=ALU.mult, op1=ALU.add)
                layer_norm_tile(P, C, xt, r_xt, gb, bb, r_gb, stats, r_stats, mv, r_mv)
                P.dma("sp", D_["xo"][tt * 128:(tt + 1) * 128, :], xt[:], reads=[r_xt], writes=[r_xo], semres=r_xt)
            P.barrier()
            P.flush()


def build_T():
    nc = bass.Bass("TRN2", target_bir_lowering=False)

    def din(name, shape, dt=F32):
        return nc.dram_tensor(name, shape, dt, kind="ExternalInput").ap()
    D_ = {
        "x": din("x", [TPC, D]), "wT": din("wT", [26, 128, KC, 512]), "tab": din("tab", [128, NT, NTAB]), "gq": din("gq", [128, 128]),
        "lp": din("lp", [128, 256]), "cst": din("cst", [128, 4]), "subg": din("subg", [128, 1]), "bgT": din("bgT", [128, 64]),
        "wbr": din("wbr", [KC, 128, 16, 128]), "wo": din("wo", [4, 128, KC, 512]), "lng": din("lng", [128, D]), "lnb": din("lnb", [128, D]),
        "kT_AC": din("kT_AC", [6, 128, S], BF16), "v_AC": din("v_AC", [S, 768], BF16),
        "kT_Bw": din("kT_Bw", [4, 128, BW], BF16), "v_Bw": din("v_Bw", [BW, 512], BF16), "biasB": din("biasB", [NT, 4, 128, 7, 128]),
        "kT_Dw": din("kT_Dw", [12, 128, DW], BF16), "v_Dw": din("v_Dw", [DW, 1536], BF16), "maskD": din("maskD", [128, NMASK, 128]),
    }
    D_["xo"] = nc.dram_tensor("xo", [TPC, D], F32, kind="ExternalOutput").ap()
    with ExitStack() as es:
        P = Prog(nc, es)
        C = setup_common(P, es)
        emit_T(P, C, es, D_)
    return nc


O_AQ, O_AK, O_AV, O_AZ = 0, 512, 1024, 1536
O_BQ, O_BK, O_BV, O_BZ = 2048, 2560, 3072, 3584
O_CQ, O_CK, O_CV, O_CZ = 4096, 4608, 4864, 5120
O_DQ, O_DK, O_DV, O_DZ = 5632, 7168, 8704, 10240
O_GL = 10752


def tileize(wcols):
    n = wcols.shape[1] // 512
    return np.ascontiguousarray(wcols.reshape(KC, 128, n, 512).transpose(2, 1, 0, 3))


def rep128(vec):
    return np.ascontiguousarray(np.broadcast_to(np.asarray(vec, np.float32).reshape(1, -1), (128, vec.size)))


def make_tabs():
    t = np.arange(S, dtype=np.int64)

    def cs(pos, half, dim, theta):
        inv = np.power(np.float32(theta), -np.arange(half, dtype=np.float32) * np.float32(2.0) / np.float32(dim)).astype(np.float32)
        ang = pos.astype(np.float32)[:, None] * inv[None, :]
        return np.cos(ang).astype(np.float32), np.sin(ang).astype(np.float32)
    cA, sA = cs(t, 8, 16, 500000.0)
    cD, sD = cs(t, 16, 32, 500000.0)
    cR, sR = cs(t // 64, 32, 64, 10000.0)
    cC, sC = cs(t % 64, 32, 64, 10000.0)
    full = np.concatenate([cA, sA, cD, sD, cR, sR, cC, sC], axis=1)
    out = []
    for c in range(NCORE):
        blk = full[c * TPC:(c + 1) * TPC].reshape(NT, 128, NTAB).transpose(1, 0, 2)
        out.append(np.ascontiguousarray(blk))
    return out


def wP_of(w):
    cols = np.concatenate([w[:, O_AK:O_AK + 512], w[:, O_DK:O_DK + 1536], w[:, O_BK:O_BK + 512],
                           w[:, O_CK:O_CK + 256], w[:, O_CV:O_CV + 256],
                           w[:, O_AV:O_AV + 512], w[:, O_BV:O_BV + 512], w[:, O_DV:O_DV + 1536]], axis=1)
    return tileize(cols)


_NC_CACHE = {}


def get_nc(name):
    if name not in _NC_CACHE:
        _NC_CACHE[name] = {"E": build_E, "P": build_P, "T": build_T}[name]()
    return _NC_CACHE[name]


def run_E(x2d, g, b):
    nc = get_nc("E")
    gb, bb = rep128(g), rep128(b)
    maps = [{"x": np.ascontiguousarray(x2d[c * TPC:(c + 1) * TPC]), "g": gb, "b": bb} for c in range(NCORE)]
    res = run_bass_kernel_spmd(nc, maps, core_ids=list(range(NCORE)))
    return np.concatenate([r["xo"] for r in res.results], axis=0)


def run_P(xcur, w, kn_g, tabs):
    nc = get_nc("P")
    wP = wP_of(w)
    gk = rep128(kn_g)
    maps = [{"x": np.ascontiguousarray(xcur[c * TPC:(c + 1) * TPC]), "wP": wP, "tab": tabs[c], "gk": gk} for c in range(NCORE)]
    res = run_bass_kernel_spmd(nc, maps, core_ids=list(range(NCORE)))
    kT = np.concatenate([r["kT"] for r in res.results], axis=2)
    v = np.concatenate([r["v"] for r in res.results], axis=0)
    return kT, v


def wT_of(w):
    gl = w[:, O_GL:O_GL + 8192].reshape(D, 4, KC, 128).transpose(0, 2, 1, 3).reshape(D, 8192)
    cols = np.concatenate([w[:, O_AQ:O_AQ + 512], w[:, O_CQ:O_CQ + 512], w[:, O_DQ:O_DQ + 1536], w[:, O_BQ:O_BQ + 512],
                           w[:, O_AZ:O_AZ + 512], w[:, O_BZ:O_BZ + 512], w[:, O_CZ:O_CZ + 512], w[:, O_DZ:O_DZ + 512], gl], axis=1)
    return tileize(cols)


def window(arr, axis, start, length):
    n = arr.shape[axis]
    lo, hi = max(start, 0), min(start + length, n)
    shp = list(arr.shape)
    shp[axis] = length
    out = np.zeros(shp, arr.dtype)
    sl_src = [slice(None)] * arr.ndim
    sl_dst = [slice(None)] * arr.ndim
    sl_src[axis] = slice(lo, hi)
    sl_dst[axis] = slice(lo - start, hi - start)
    out[tuple(sl_dst)] = arr[tuple(sl_src)]
    return out


def make_biasB_index():
    out = []
    for c in range(NCORE):
        qt = np.arange(NT)[:, None, None, None]
        a = np.arange(128)[None, :, None, None]
        kc = np.arange(7)[None, None, :, None]
        b = np.arange(128)[None, None, None, :]
        tk = TPC * c - 384 + 128 * (qt + kc) + a
        tq = TPC * c + 128 * qt + b
        tk, tq = np.broadcast_arrays(tk, tq)
        inr = (tk >= 0) & (tk < S)
        kr, kcol = tk // 64, tk % 64
        qr, qc = tq // 64, tq % 64
        r0 = np.clip(qr - 4, 0, 120)
        c0 = np.clip(qc - 8, 0, 48)
        valid = inr & (kr >= r0) & (kr < r0 + 8) & (kcol >= c0) & (kcol < c0 + 16)
        dr = np.clip(kr - qr + 7, 0, 14)
        dc = np.clip(kcol - qc + 15, 0, 30)
        out.append((np.where(valid, dr * 31 + dc, 0).astype(np.int64), valid))
    return out


def make_biasB(rpb_l, bidx):
    idx, valid = bidx
    flat = rpb_l.reshape(4, 15 * 31)
    g = flat[:, idx]
    g = np.where(valid[None], g, np.float32(NEG)).astype(np.float32)
    return np.ascontiguousarray(g.transpose(1, 0, 2, 3, 4))


def make_maskD():
    out = []
    for c in range(NCORE):
        m = np.full((128, NMASK, 128), NEG, np.float32)
        a = np.arange(128)[:, None]
        b = np.arange(128)[None, :]
        base = [0, 16, 20]
        for g, r in enumerate(DIL_RATES):
            n = TPC // r
            nq = min(128, n)
            for qti in range(n // nq):
                U = qti * nq
                for ch in range(2):
                    u = U - 64 + 128 * ch + a
                    uq = U + b
                    ug = n * c + u
                    valid = (np.abs(u - uq) <= 64) & (ug >= 0) & (ug < S // r)
                    m[:, base[g] + qti * 2 + ch, :] = np.where(valid, 0.0, NEG)
        out.append(m)
    return out


def run_T(xcur, l, inp, kT, v, tabs, bidx, maskD):
    nc = get_nc("T")
    w = inp["w_in"][l]
    lam_init = 0.8 - 0.6 * math.exp(-0.3 * l)
    common = {
        "wT": wT_of(w), "gq": rep128(inp["gqa_q_norm_g"][l]), "lp": rep128(inp["diff_lambda"][l].reshape(-1)),
        "cst": rep128(np.array([lam_init, 1.0 - lam_init, 0.0, 0.0], np.float32)),
        "subg": np.ascontiguousarray(inp["diff_subln_g"][l].reshape(128, 1)),
        "bgT": np.ascontiguousarray(inp["b_gate"][l].reshape(4, KC, 128).transpose(2, 0, 1).reshape(128, 64)),
        "wbr": np.ascontiguousarray(inp["w_branch"][l].reshape(4, 4, 128, KC, 128).transpose(3, 2, 0, 1, 4).reshape(KC, 128, 16, 128)),
        "wo": tileize(inp["w_out"][l]), "lng": rep128(inp["ln_g"][l]), "lnb": rep128(inp["ln_b"][l]),
        "kT_AC": np.ascontiguousarray(kT[0:6]), "v_AC": np.ascontiguousarray(np.concatenate([v[:, 0:512], v[:, 1024:1280]], axis=1)),
    }
    kT_B, v_B = kT[18:22], v[:, 512:1024]
    kT_D, v_D = kT[6:18], v[:, 1280:2816]
    maps = []
    for c in range(NCORE):
        m = dict(common)
        m["x"] = np.ascontiguousarray(xcur[c * TPC:(c + 1) * TPC])
        m["tab"] = tabs[c]
        m["kT_Bw"] = window(kT_B, 2, TPC * c - 384, BW)
        m["v_Bw"] = window(v_B, 0, TPC * c - 384, BW)
        m["biasB"] = make_biasB(inp["nat_rpb"][l], bidx[c])
        m["kT_Dw"] = window(kT_D, 2, TPC * c - 1024, DW)
        m["v_Dw"] = window(v_D, 0, TPC * c - 1024, DW)
        m["maskD"] = maskD[c]
        maps.append(m)
    res = run_bass_kernel_spmd(nc, maps, core_ids=list(range(NCORE)))
    return np.concatenate([r["xo"] for r in res.results], axis=0)


def kernel(**inputs):
    inp = {k: np.asarray(v) for k, v in inputs.items()}
    x = np.ascontiguousarray(inp["x"].reshape(S, D).astype(np.float32, copy=False))
    tabs = make_tabs()
    bidx = make_biasB_index()
    maskD = make_maskD()
    xcur = run_E(x, inp["emb_ln_g"], inp["emb_ln_b"])
    for l in range(DEPTH):
        kT, v = run_P(xcur, inp["w_in"][l], inp["gqa_k_norm_g"][l], tabs)
        xcur = run_T(xcur, l, inp, kT, v, tabs, bidx, maskD)
    return xcur.reshape(1, S, D).astype(np.float32)
```

```python
import math
from contextlib import ExitStack
import numpy as np
import ml_dtypes
import concourse.bass as bass
import concourse.mybir as mybir
from concourse.bass_utils import run_bass_kernel_spmd

F32 = mybir.dt.float32
BF16 = mybir.dt.bfloat16
AF = mybir.ActivationFunctionType
ALU = mybir.AluOpType
NPBF = ml_dtypes.bfloat16

NCORE = 8
S = 8192
D = 2048
TPC = 1024
NT = 8
KC = 16
DEPTH = 4
NEG = -30000.0
LN_EPS = 1e-5
RMS_EPS = 1e-6
DN_ALPHA = (2 * DEPTH) ** 0.25
DIL_RATES = (1, 4, 16)


def ss(start, n, step=1):
    return slice(start, start + step * (n - 1) + 1, step)


class Res:
    __slots__ = ("name", "ws", "rs", "dsem")

    def __init__(self, name):
        self.name = name
        self.ws = []
        self.rs = []
        self.dsem = None


class Prog:
    ENGS = ("pe", "act", "dve", "pool", "sp")

    def __init__(self, nc, es):
        self.nc = nc
        self.es = es
        self.eng = {"pe": nc.tensor, "act": nc.scalar, "dve": nc.vector, "pool": nc.gpsimd, "sp": nc.sync}
        self.streams = {e: [] for e in self.ENGS}
        self.cnt = {e: 0 for e in self.ENGS}
        self.sem = {}
        self.semtot = {}
        for e in self.ENGS:
            self.sem[e] = es.enter_context(nc.semaphore("s_" + e))
        self.waited = {e: {} for e in self.ENGS}
        self.ndsem = 0
        self.pending_noinc = {e: False for e in self.ENGS}
        self.nres = 0

    def res(self, name="r"):
        self.nres += 1
        return Res(name + str(self.nres))

    def sb(self, es, name, shape, dt):
        self.nres += 1
        t = es.enter_context(self.nc.sbuf_tensor("sb_%s_%d" % (name, self.nres), shape, dt))
        return t, self.res(name)

    def dsem_of(self, r):
        if r.dsem is None:
            key = "d%d" % self.ndsem
            self.ndsem += 1
            self.sem[key] = self.es.enter_context(self.nc.semaphore("sd%d" % self.ndsem))
            self.semtot[key] = 0
            r.dsem = key
        return r.dsem

    def _wait(self, e, deps):
        need = {}
        for (k, v) in deps:
            if k == e and e == "pe":
                continue
            if self.waited[e].get(k, 0) >= v:
                continue
            if need.get(k, 0) < v:
                need[k] = v
        for k, v in need.items():
            self.waited[e][k] = v
            sem = self.sem[k]
            eng = self.eng[e]
            self.streams[e].append(lambda eng=eng, sem=sem, v=v: eng.wait_ge(sem, v))

    @staticmethod
    def _deps(reads, writes):
        deps = []
        for r in reads:
            deps.extend(r.ws)
        for r in writes:
            deps.extend(r.ws)
            deps.extend(r.rs)
        return deps

    def op(self, e, meth, reads=(), writes=(), inc=True, **kw):
        self._wait(e, self._deps(reads, writes))
        sem = self.sem[e]
        if inc:
            self.streams[e].append(lambda meth=meth, kw=kw, sem=sem: meth(**kw).then_inc(sem, 1))
            self.cnt[e] += 1
            tok = (e, self.cnt[e])
            self.pending_noinc[e] = False
        else:
            assert e == "pe"
            self.streams[e].append(lambda meth=meth, kw=kw: meth(**kw))
            tok = (e, self.cnt[e] + 1)
            self.pending_noinc[e] = True
        for r in reads:
            r.rs.append(tok)
            if len(r.rs) > 64:
                r.rs = self._compact(r.rs)
        for r in writes:
            r.ws = self._compact(r.ws + [tok])
            r.rs = []
        return tok

    @staticmethod
    def _compact(toks):
        best = {}
        for k, v in toks:
            if best.get(k, 0) < v:
                best[k] = v
        return list(best.items())

    def dma(self, q, out, in_, reads=(), writes=(), semres=None, multi=False, **kw):
        self._wait(q, self._deps(reads, writes))
        if semres is None:
            semres = writes[0] if writes else reads[0]
        key = self.dsem_of(semres)
        self.semtot[key] += 16
        tok = (key, self.semtot[key])
        sem = self.sem[key]
        eng = self.eng[q]
        self.streams[q].append(lambda eng=eng, out=out, in_=in_, kw=kw, sem=sem: eng.dma_start(
            out=(out() if callable(out) else out), in_=(in_() if callable(in_) else in_), **kw).then_inc(sem, 16))
        for r in reads:
            r.rs.append(tok)
            if len(r.rs) > 64:
                r.rs = self._compact(r.rs)
        for r in writes:
            r.ws = self._compact(r.ws + [tok])
            r.rs = []
        return tok

    def raw(self, e, fn):
        self.streams[e].append(fn)

    def coll(self, kind, ins, outs, reads=(), writes=()):
        self._wait("pool", self._deps(reads, writes))
        key = self.dsem_of(writes[0])
        self.semtot[key] += 16
        tok = (key, self.semtot[key])
        sem = self.sem[key]
        nc = self.nc
        self.streams["pool"].append(lambda: nc.gpsimd.collective_compute(kind, ALU.bypass, replica_groups=[list(range(NCORE))], ins=ins, outs=outs).then_inc(sem, 16))
        for r in reads:
            r.rs.append(tok)
        for r in writes:
            r.ws = self._compact(r.ws + [tok])
            r.rs = []
        return tok

    def barrier(self):
        assert not any(self.pending_noinc.values())
        deps = [(e, self.cnt[e]) for e in ("pe", "act", "dve", "pool") if self.cnt[e] > 0]
        deps += [(k, v) for k, v in self.semtot.items() if v > 0]
        for e in self.ENGS:
            self._wait(e, [d for d in deps if d[0] != e])

    def flush(self):
        assert not any(self.pending_noinc.values())
        nc = self.nc
        st = self.streams
        with nc.Block() as block:
            @block.tensor
            def _(e):
                for f in st["pe"]:
                    f()

            @block.scalar
            def _(e):
                for f in st["act"]:
                    f()

            @block.vector
            def _(e):
                for f in st["dve"]:
                    f()

            @block.gpsimd
            def _(e):
                for f in st["pool"]:
                    f()

            @block.sync
            def _(e):
                for f in st["sp"]:
                    f()
        self.streams = {e: [] for e in self.ENGS}


class Ctx:
    pass


def setup_common(P, es):
    nc = P.nc
    C = Ctx()
    C.ps = []
    C.rps = []
    for i in range(8):
        t = es.enter_context(nc.psum_tensor("psb%d" % i, [128, 512], F32))
        C.ps.append(t)
        C.rps.append(P.res("ps"))
    C.ident, C.r_ident = P.sb(es, "ident", [128, 128], BF16)
    C.onesb, C.r_onesb = P.sb(es, "onesb", [128, 128], BF16)
    C.onesf, C.r_onesf = P.sb(es, "onesf", [128, 128], F32)
    C.eps_ln, C.r_eps_ln = P.sb(es, "epsln", [128, 1], F32)
    C.eps_rms, C.r_eps_rms = P.sb(es, "epsrms", [128, 1], F32)
    P.op("pool", nc.gpsimd.memset, [], [C.r_ident], ap=C.ident[:], constant=0.0)
    P.op("pool", nc.gpsimd.affine_select, [C.r_ident], [C.r_ident], out=C.ident[:], in_=C.ident[:], compare_op=ALU.not_equal, fill=1.0,
         base=0, pattern=[[-1, 128]], channel_multiplier=1)
    P.op("pool", nc.gpsimd.memset, [], [C.r_onesb], ap=C.onesb[:], constant=1.0)
    P.op("pool", nc.gpsimd.memset, [], [C.r_onesf], ap=C.onesf[:], constant=1.0)
    P.op("pool", nc.gpsimd.memset, [], [C.r_eps_ln], ap=C.eps_ln[:], constant=LN_EPS)
    P.op("pool", nc.gpsimd.memset, [], [C.r_eps_rms], ap=C.eps_rms[:], constant=RMS_EPS)
    return C


def build_xT(P, C, es, x_dram, r_x, xT, r_xT):
    nc = P.nc
    xb = [P.sb(es, "xb%d" % i, [128, D], BF16) for i in range(2)]
    for tt in range(NT):
        t, r = xb[tt % 2]
        P.dma("pool", t[:], x_dram[tt * 128:(tt + 1) * 128, :], reads=[r_x], writes=[r])
        for half in range(2):
            bank = 6 + half
            pst = C.ps[bank][:].bitcast(BF16)
            for k8 in range(8):
                kc = half * 8 + k8
                P.op("pe", nc.tensor.transpose, [r, C.r_ident], [C.rps[bank]], inc=(k8 == 7),
                     out=pst[:, k8 * 128:(k8 + 1) * 128], in_=t[:, kc * 128:(kc + 1) * 128], identity=C.ident[:])
            dst = xT[:, half * 8:(half + 1) * 8, tt * 128:(tt + 1) * 128]
            src = pst.rearrange("p (k t) -> p k t", k=8)
            if half == 0:
                P.op("dve", nc.vector.tensor_copy, [C.rps[bank]], [r_xT], out=dst, in_=src)
            else:
                P.op("act", nc.scalar.copy, [C.rps[bank]], [r_xT], out=dst, in_=src)


def load_wtile(P, wt, r_wt, w_dram, r_w, idx):
    P.dma("pool", wt[:, 0:8, :], w_dram[idx, :, 0:8, :], reads=[r_w], writes=[r_wt])
    P.dma("pool", wt[:, 8:16, :], w_dram[idx, :, 8:16, :], reads=[r_w], writes=[r_wt], multi=True)


def proj_tok(P, C, xT, r_xT, wt, r_wt, tt, bank, ncols=512):
    nc = P.nc
    for kc in range(KC):
        P.op("pe", nc.tensor.matmul, [r_xT, r_wt], [C.rps[bank]], inc=(kc == KC - 1),
             out=C.ps[bank][:, 0:ncols], lhsT=xT[:, kc, tt * 128:(tt + 1) * 128], rhs=wt[:, kc, 0:ncols], start=(kc == 0), stop=(kc == KC - 1))


def proj_feat(P, C, xT, r_xT, wt, r_wt, j, half, bank):
    nc = P.nc
    for kc in range(KC):
        P.op("pe", nc.tensor.matmul, [r_xT, r_wt], [C.rps[bank]], inc=(kc == KC - 1),
             out=C.ps[bank][:], lhsT=wt[:, kc, j * 128:(j + 1) * 128], rhs=xT[:, kc, half * 512:(half + 1) * 512], start=(kc == 0), stop=(kc == KC - 1))


def rope_tok(P, src, r_src, dst, r_dst, H, a, m, cos, sin, r_tab, t1, t2, r_t1, r_t2):
    nc = P.nc
    s3 = src.rearrange("p (h d) -> p h d", h=H)
    d3 = dst.rearrange("p (h d) -> p h d", h=H)
    x1 = s3[:, :, a:a + m]
    x2 = s3[:, :, a + m:a + 2 * m]
    cb = cos.unsqueeze(1).broadcast_to([128, H, m])
    sb_ = sin.unsqueeze(1).broadcast_to([128, H, m])
    u1 = t1[:, 0:H * m].rearrange("p (h d) -> p h d", h=H)
    u2 = t2[:, 0:H * m].rearrange("p (h d) -> p h d", h=H)
    TT = nc.vector.tensor_tensor
    P.op("dve", TT, [r_src, r_tab], [r_t1], out=u1, in0=x1, in1=cb, op=ALU.mult)
    P.op("dve", TT, [r_src, r_tab], [r_t2], out=u2, in0=x2, in1=sb_, op=ALU.mult)
    P.op("dve", TT, [r_t1, r_t2], [r_dst], out=d3[:, :, a:a + m], in0=u1, in1=u2, op=ALU.subtract)
    P.op("dve", TT, [r_src, r_tab], [r_t1], out=u1, in0=x2, in1=cb, op=ALU.mult)
    P.op("dve", TT, [r_src, r_tab], [r_t2], out=u2, in0=x1, in1=sb_, op=ALU.mult)
    P.op("dve", TT, [r_t1, r_t2], [r_dst], out=d3[:, :, a + m:a + 2 * m], in0=u1, in1=u2, op=ALU.add)


T_CA, T_SA, T_CD, T_SD, T_CR, T_SR, T_CC, T_SC = 0, 8, 16, 32, 48, 80, 112, 144
NTAB = 176


class TokQK:
    def __init__(self, P, C, es, tab, r_tab, need_norm=True):
        self.P, self.C, self.tab, self.r_tab = P, C, tab, r_tab
        self.hsb = [P.sb(es, "hsb%d" % i, [128, 512], F32) for i in range(2)]
        self.hn = [P.sb(es, "hn%d" % i, [128, 512], F32) for i in range(2)] if need_norm else [(None, None)] * 2
        self.qb = [P.sb(es, "qb%d" % i, [128, 512], BF16) for i in range(2)]
        self.t1 = [P.sb(es, "rt1%d" % i, [128, 256], F32) for i in range(2)]
        self.t2 = [P.sb(es, "rt2%d" % i, [128, 256], F32) for i in range(2)]
        self.ssq, self.r_ssq = P.sb(es, "ssq", [128, 8], F32)
        self.junk, self.r_junk = P.sb(es, "junk", [128, 128], F32)
        self.n = 0
        self.pending = None

    def run(self, xT, r_xT, wt, r_wt, tt, kind, nheads, dst_fn, g_bc=None, r_g=None, extra_v=None):
        P, C, nc = self.P, self.C, self.P.nc
        i = self.n % 2
        self.n += 1
        bank = 4 + i
        proj_tok(P, C, xT, r_xT, wt, r_wt, tt, bank)
        self.flush()
        hsb, r_hsb = self.hsb[i]
        hn, r_hn = self.hn[i]
        qb, r_qb = self.qb[i]
        t1, r_t1 = self.t1[i]
        t2, r_t2 = self.t2[i]
        tab, r_tab = self.tab, self.r_tab
        ps = C.ps[bank]
        nq = nheads * 128 if kind == "C" else 512
        P.op("act", nc.scalar.copy, [C.rps[bank]], [r_hsb], out=hsb[:], in_=ps[:])
        if extra_v is not None:
            extra_v(hsb, r_hsb)
        if kind in ("A", "D"):
            P.op("act", nc.scalar.copy, [r_hsb], [r_qb], out=qb[:], in_=hsb[:])
            if kind == "A":
                rope_tok(P, hsb[:], r_hsb, qb[:], r_qb, 8, 0, 8, tab[:, tt, T_CA:T_CA + 8], tab[:, tt, T_SA:T_SA + 8], r_tab, t1, t2, r_t1, r_t2)
            else:
                rope_tok(P, hsb[:], r_hsb, qb[:], r_qb, 4, 0, 16, tab[:, tt, T_CD:T_CD + 16], tab[:, tt, T_SD:T_SD + 16], r_tab, t1, t2, r_t1, r_t2)
        else:
            ssq, r_ssq = self.ssq, self.r_ssq
            P.op("dve", nc.vector.tensor_tensor, [r_hsb], [r_hn], out=hn[:, 0:nq], in0=hsb[:, 0:nq], in1=hsb[:, 0:nq], op=ALU.mult)
            P.op("dve", nc.vector.tensor_reduce, [r_hn], [r_ssq], out=ssq[:, 0:nheads], in_=hn[:, 0:nq].rearrange("p (h d) -> p h d", h=nheads),
                 axis=mybir.AxisListType.X, op=ALU.add)
            P.op("act", nc.scalar.activation, [r_ssq, C.r_eps_rms], [r_ssq], out=ssq[:, 0:nheads], in_=ssq[:, 0:nheads], func=AF.Sqrt,
                 scale=1.0 / 128, bias=C.eps_rms[:])
            P.op("dve", nc.vector.reciprocal, [r_ssq], [r_ssq], out=ssq[:, 0:nheads], in_=ssq[:, 0:nheads])
            for h in range(nheads):
                P.op("dve", nc.vector.scalar_tensor_tensor, [r_hsb, r_ssq, r_g], [r_hn], out=hn[:, h * 128:(h + 1) * 128],
                     in0=hsb[:, h * 128:(h + 1) * 128], scalar=ssq[:, h:h + 1], in1=g_bc[:], op0=ALU.mult, op1=ALU.mult)
            rope_tok(P, hn[:, 0:nq], r_hn, qb[:, 0:nq], r_qb, nheads, 0, 32, tab[:, tt, T_CR:T_CR + 32], tab[:, tt, T_SR:T_SR + 32], r_tab, t1, t2, r_t1, r_t2)
            rope_tok(P, hn[:, 0:nq], r_hn, qb[:, 0:nq], r_qb, nheads, 64, 32, tab[:, tt, T_CC:T_CC + 32], tab[:, tt, T_SC:T_SC + 32], r_tab, t1, t2, r_t1, r_t2)
        nj = nq // 128
        tb = 6 + i

        def tail(qb=qb, r_qb=r_qb, nj=nj, tb=tb, dst_fn=dst_fn):
            pst = C.ps[tb][:].bitcast(BF16)
            for j in range(nj):
                P.op("pe", nc.tensor.transpose, [r_qb, C.r_ident], [C.rps[tb]], inc=(j == nj - 1),
                     out=pst[:, j * 128:(j + 1) * 128], in_=qb[:, j * 128:(j + 1) * 128], identity=C.ident[:])
            ncp = 0
            for j in range(nj):
                dl = dst_fn(j)
                if isinstance(dl, tuple):
                    dl = [(dl[0], dl[1], 0, 128)]
                for (dst, r_dst, p0, p1) in dl:
                    if ncp % 2 == 0:
                        P.op("dve", nc.vector.tensor_copy, [C.rps[tb]], [r_dst], out=dst, in_=pst[p0:p1, j * 128:(j + 1) * 128])
                    else:
                        P.op("act", nc.scalar.copy, [C.rps[tb]], [r_dst], out=dst, in_=pst[p0:p1, j * 128:(j + 1) * 128])
                    ncp += 1
        self.pending = tail

    def flush(self):
        if self.pending is not None:
            self.pending()
            self.pending = None


def layer_norm_tile(P, C, xt, r_xt, g_bc, b_bc, r_gb, stats, r_stats, mv, r_mv):
    nc = P.nc
    for c in range(4):
        P.op("dve", nc.vector.bn_stats, [r_xt], [r_stats], out=stats[:, c * 6:(c + 1) * 6], in_=xt[:, c * 512:(c + 1) * 512])
    P.op("dve", nc.vector.bn_aggr, [r_stats], [r_mv], out=mv[:, 0:2], in_=stats[:, 0:24])
    P.op("act", nc.scalar.activation, [r_mv, C.r_eps_ln], [r_mv], out=mv[:, 2:3], in_=mv[:, 1:2], func=AF.Sqrt, scale=1.0, bias=C.eps_ln[:])
    P.op("dve", nc.vector.reciprocal, [r_mv], [r_mv], out=mv[:, 2:3], in_=mv[:, 2:3])
    P.op("dve", nc.vector.tensor_scalar, [r_xt, r_mv], [r_xt], out=xt[:], in0=xt[:], scalar1=mv[:, 0:1], scalar2=mv[:, 2:3],
         op0=ALU.subtract, op1=ALU.mult)
    P.op("pool", nc.gpsimd.tensor_tensor, [r_xt, r_gb], [r_xt], out=xt[:], in0=xt[:], in1=g_bc[:], op=ALU.mult)
    P.op("dve", nc.vector.tensor_tensor, [r_xt, r_gb], [r_xt], out=xt[:], in0=xt[:], in1=b_bc[:], op=ALU.add)


def emit_E(P, C, es, x, r_in, g, b, xo, r_out):
    gb, r_gb = P.sb(es, "gb", [128, D], F32)
    bb, _ = P.sb(es, "bb", [128, D], F32)
    P.dma("sp", gb[:], g, writes=[r_gb])
    P.dma("sp", bb[:], b, writes=[r_gb], multi=True)
    xts = [P.sb(es, "xt%d" % i, [128, D], F32) for i in range(2)]
    stats, r_stats = P.sb(es, "stats", [128, 24], F32)
    mv, r_mv = P.sb(es, "mv", [128, 4], F32)
    for tt in range(NT):
        xt, r_xt = xts[tt % 2]
        P.dma("sp", xt[:], x[tt * 128:(tt + 1) * 128, :], reads=[r_in], writes=[r_xt])
        layer_norm_tile(P, C, xt, r_xt, gb, bb, r_gb, stats, r_stats, mv, r_mv)
        P.dma("sp", xo[tt * 128:(tt + 1) * 128, :], xt[:], reads=[r_xt], writes=[r_out], semres=r_xt, multi=True)


def build_E():
    nc = bass.Bass("TRN2", target_bir_lowering=False)
    x = nc.dram_tensor("x", [TPC, D], F32, kind="ExternalInput").ap()
    g = nc.dram_tensor("g", [128, D], F32, kind="ExternalInput").ap()
    b = nc.dram_tensor("b", [128, D], F32, kind="ExternalInput").ap()
    xo = nc.dram_tensor("xo", [TPC, D], F32, kind="ExternalOutput").ap()
    with ExitStack() as es:
        P = Prog(nc, es)
        C = setup_common(P, es)
        emit_E(P, C, es, x, P.res("in"), g, b, xo, P.res("out"))
        P.barrier()
        P.flush()
    return nc


NKT = 22
NV = 2816


def emit_P(P, C, es, x, r_x, wP, r_wP, tabd, gkd, kT, r_kT, v, r_v):
    nc = P.nc
    xT, r_xT = P.sb(es, "xT", [128, KC, TPC], BF16)
    tab, r_tab = P.sb(es, "tab", [128, NT, NTAB], F32)
    gk, r_gk = P.sb(es, "gk", [128, 128], F32)
    P.dma("sp", tab[:], tabd, writes=[r_tab])
    P.dma("sp", gk[:], gkd, writes=[r_gk])
    build_xT(P, C, es, x, r_x, xT, r_xT)
    wts = [P.sb(es, "wt%d" % i, [128, KC, 512], BF16) for i in range(2)]
    kst = [P.sb(es, "kst%d" % i, [128, 4, TPC], BF16) for i in range(2)]
    vst = [P.sb(es, "vst%d" % i, [128, 512], BF16) for i in range(3)]
    tq = TokQK(P, C, es, tab, r_tab)
    order = [0, 1, 2, 3, 5, 4, 6, 7, 8, 9, 10]
    load_wtile(P, wts[0][0], wts[0][1], wP, r_wP, order[0])
    nks = 0
    nvs = 0
    for oi, ti in enumerate(order):
        wt, r_wt = wts[oi % 2]
        if oi + 1 < len(order):
            load_wtile(P, wts[(oi + 1) % 2][0], wts[(oi + 1) % 2][1], wP, r_wP, order[oi + 1])
        if ti in (0, 1, 2, 3, 5):
            ks, r_ks = kst[nks % 2]
            nks += 1
            if ti == 5:
                nch, ch0 = 2, 4
            else:
                nch, ch0 = 4, (0 if ti == 0 else 6 + (ti - 1) * 4)
            for tt in range(NT):
                dst_fn = (lambda j, ks=ks, r_ks=r_ks, tt=tt: (ks[:, j, tt * 128:(tt + 1) * 128], r_ks))
                if ti == 5:
                    vs, r_vs = vst[nvs % 3]
                    nvs += 1

                    def extra(hsb, r_hsb, vs=vs, r_vs=r_vs, tt=tt):
                        P.op("act", nc.scalar.copy, [r_hsb], [r_vs], out=vs[:, 0:256], in_=hsb[:, 256:512])
                        P.dma("sp", v[tt * 128:(tt + 1) * 128, 1024:1280], vs[:, 0:256], reads=[r_vs], writes=[r_v], semres=r_vs, multi=True)
                    tq.run(xT, r_xT, wt, r_wt, tt, "C", 2, dst_fn, g_bc=gk, r_g=r_gk, extra_v=extra)
                else:
                    tq.run(xT, r_xT, wt, r_wt, tt, "A" if ti == 0 else "D", 4, dst_fn)
            tq.flush()
            for j in range(nch):
                P.dma("sp", kT[ch0 + j], ks[:, j, :], reads=[r_ks], writes=[r_kT], semres=r_ks, multi=True)
        elif ti == 4:
            ks, r_ks = kst[nks % 2]
            nks += 1
            n = 0
            for j in range(4):
                for half in range(2):
                    bank = 4 + n % 2
                    n += 1
                    proj_feat(P, C, xT, r_xT, wt, r_wt, j, half, bank)
                    if n % 2 == 0:
                        P.op("act", nc.scalar.copy, [C.rps[bank]], [r_ks], out=ks[:, j, half * 512:(half + 1) * 512], in_=C.ps[bank][:])
                    else:
                        P.op("dve", nc.vector.tensor_copy, [C.rps[bank]], [r_ks], out=ks[:, j, half * 512:(half + 1) * 512], in_=C.ps[bank][:])
            for j in range(4):
                P.dma("sp", kT[18 + j], ks[:, j, :], reads=[r_ks], writes=[r_kT], semres=r_ks, multi=True)
        else:
            voff = {6: 0, 7: 512, 8: 1280, 9: 1792, 10: 2304}[ti]
            for tt in range(NT):
                bank = 4 + tt % 2
                proj_tok(P, C, xT, r_xT, wt, r_wt, tt, bank)
                vs, r_vs = vst[nvs % 3]
                nvs += 1
                if tt % 2 == 0:
                    P.op("act", nc.scalar.copy, [C.rps[bank]], [r_vs], out=vs[:], in_=C.ps[bank][:])
                else:
                    P.op("dve", nc.vector.tensor_copy, [C.rps[bank]], [r_vs], out=vs[:], in_=C.ps[bank][:])
                P.dma("sp", v[tt * 128:(tt + 1) * 128, voff:voff + 512], vs[:], reads=[r_vs], writes=[r_v], semres=r_vs, multi=True)


def build_P():
    nc = bass.Bass("TRN2", target_bir_lowering=False)
    x = nc.dram_tensor("x", [TPC, D], F32, kind="ExternalInput").ap()
    wP = nc.dram_tensor("wP", [11, 128, KC, 512], F32, kind="ExternalInput").ap()
    tabd = nc.dram_tensor("tab", [128, NT, NTAB], F32, kind="ExternalInput").ap()
    gkd = nc.dram_tensor("gk", [128, 128], F32, kind="ExternalInput").ap()
    kT = nc.dram_tensor("kT", [NKT, 128, TPC], BF16, kind="ExternalOutput").ap()
    v = nc.dram_tensor("v", [TPC, NV], BF16, kind="ExternalOutput").ap()
    with ExitStack() as es:
        P = Prog(nc, es)
        C = setup_common(P, es)
        emit_P(P, C, es, x, P.res("x"), wP, P.res("wP"), tabd, gkd, kT, P.res("kT"), v, P.res("v"))
        P.barrier()
        P.flush()
    return nc


BW = 1792
DW = 3072
NMASK = 22


class WStream:
    def __init__(self, P, wts, w_dram, r_w, order):
        self.P, self.wts, self.w, self.r_w, self.order = P, wts, w_dram, r_w, order
        self.i = 0
        load_wtile(P, wts[0][0], wts[0][1], w_dram, r_w, order[0])

    def get(self, expect):
        assert self.order[self.i] == expect, (self.order[self.i], expect)
        cur = self.wts[self.i % 2]
        self.i += 1
        if self.i < len(self.order):
            nxt = self.wts[self.i % 2]
            load_wtile(self.P, nxt[0], nxt[1], self.w, self.r_w, self.order[self.i])
        return cur


def emit_silu_z(P, C, xT, r_xT, wt, r_wt, siluz, r_siluz):
    nc = P.nc
    n = 0
    for j in range(4):
        for half in range(2):
            bank = 4 + n % 2
            n += 1
            proj_feat(P, C, xT, r_xT, wt, r_wt, j, half, bank)
            P.op("act", nc.scalar.activation, [C.rps[bank]], [r_siluz], out=siluz[:, j, half * 512:(half + 1) * 512], in_=C.ps[bank][:], func=AF.Silu)


class Grp:
    pass


def dense_attention(P, C, groups, pT, scale, accs):
    nc = P.nc
    SB = (0, 1, 2, 7)
    items = [(gi, kc) for gi in range(len(groups)) for kc in range(groups[gi].nk)]
    n = len(items)

    def qk(idx):
        gi, kc = items[idx]
        g = groups[gi]
        sbk = SB[idx % 4]
        P.op("pe", nc.tensor.matmul, [g.r_q, g.r_kfn(kc)], [C.rps[sbk]], out=C.ps[sbk][:], lhsT=g.kfn(kc), rhs=g.q, start=True, stop=True)

    for i0 in range(min(3, n)):
        qk(i0)
    for idx in range(n):
        gi, kc = items[idx]
        g = groups[gi]
        sbk = SB[idx % 4]
        pt, r_pt = pT[idx % len(pT)]
        P.op("act", nc.scalar.activation, [C.rps[sbk]], [r_pt], out=pt[:], in_=C.ps[sbk][:], func=AF.Exp, scale=scale)
        last = (kc == g.nk - 1)
        P.op("pe", nc.tensor.matmul, [r_pt, g.r_vfn(kc)], [C.rps[g.ob]], out=C.ps[g.ob][:], lhsT=g.vfn(kc), rhs=pt[:], start=(kc == 0), stop=last)
        ac, r_ac = accs[(gi % 2) * 2 + kc % 2]
        if kc < 2:
            P.op("dve", nc.vector.tensor_copy, [r_pt], [r_ac], out=ac[:], in_=pt[:])
        else:
            P.op("dve", nc.vector.tensor_tensor, [r_pt, r_ac], [r_ac], out=ac[:], in0=ac[:], in1=pt[:], op=ALU.add)
        if idx + 3 < n:
            qk(idx + 3)
        if last:
            for sl in range(2):
                a_, r_a_ = accs[(gi % 2) * 2 + sl]
                P.op("pe", nc.tensor.matmul, [r_a_, C.r_onesf], [C.rps[g.db]], inc=(sl == 1), out=C.ps[g.db][:], lhsT=C.onesf[:], rhs=a_[:],
                     start=(sl == 0), stop=(sl == 1))
            g.fin()


def emit_T(P, C, es, D_, lam_consts=None):
    nc = P.nc
    TT = nc.vector.tensor_tensor
    x, r_x = D_["x"], P.res("x")
    r_w = P.res("wT")
    r_kv = P.res("kvdram")
    r_xo = D_.get("r_xo") or P.res("xo")
    wts = [P.sb(es, "wt%d" % i, [128, KC, 512], BF16) for i in range(2)]
    sm, r_sm = P.sb(es, "sm", [128, 16], F32)
    bgT, r_bgT = P.sb(es, "bgT", [128, 64], F32)
    P.dma("sp", bgT[:], D_["bgT"], writes=[r_bgT])
    order = [6, 0, 8, 1, 7, 5, 9, 2, 3, 4] + list(range(10, 26))
    ws = WStream(P, wts, D_["wT"], r_w, order)

    with ExitStack() as em:
        xT, r_xT = P.sb(em, "xT", [128, KC, TPC], BF16)
        ysT, r_ysT = P.sb(em, "ysT", [128, KC, TPC], BF16)
        tab, r_tab = P.sb(em, "tab", [128, NT, NTAB], F32)
        P.dma("sp", tab[:], D_["tab"], writes=[r_tab])
        siluz, r_siluz = P.sb(em, "siluz", [128, 4, TPC], BF16)
        fin = []
        big, r_big = P.sb(em, "big", [128, 16384], BF16)
        mergedT = big[:].rearrange("p (k t) -> p k t", k=KC)
        r_mergedT = r_big

        with ExitStack() as ep:
            lp, r_lp = P.sb(ep, "lp", [128, 256], F32)
            cst, r_cst = P.sb(ep, "cst", [128, 4], F32)
            subg, r_subg = P.sb(ep, "subg", [128, 1], F32)
            pr, r_pr = P.sb(ep, "pr", [128, 128], F32)
            P.dma("sp", lp[:], D_["lp"], writes=[r_lp])
            P.dma("sp", cst[:], D_["cst"], writes=[r_cst])
            P.dma("sp", subg[:], D_["subg"], writes=[r_subg])
            P.op("dve", TT, [r_lp], [r_pr], out=pr[:].rearrange("p (a d) -> p a d", a=2), in0=lp[:].rearrange("p (a b d) -> p a b d", a=2, b=2)[:, :, 0, :],
                 in1=lp[:].rearrange("p (a b d) -> p a b d", a=2, b=2)[:, :, 1, :], op=ALU.mult)
            P.op("dve", nc.vector.tensor_reduce, [r_pr], [r_sm], out=sm[:, 3:5], in_=pr[:].rearrange("p (a d) -> p a d", a=2), axis=mybir.AxisListType.X, op=ALU.add)
            P.op("act", nc.scalar.activation, [r_sm], [r_sm], out=sm[:, 5:7], in_=sm[:, 3:5], func=AF.Exp)
            P.op("dve", TT, [r_sm], [r_sm], out=sm[:, 7:8], in0=sm[:, 5:6], in1=sm[:, 6:7], op=ALU.subtract)
            P.op("dve", TT, [r_sm, r_cst], [r_sm], out=sm[:, 0:1], in0=sm[:, 7:8], in1=cst[:, 0:1], op=ALU.add)
            P.op("dve", nc.vector.tensor_scalar, [r_sm], [r_sm], out=sm[:, 1:2], in0=sm[:, 0:1], scalar1=-1.0, scalar2=None, op0=ALU.mult)
            P.op("dve", TT, [r_subg, r_cst], [r_sm], out=sm[:, 2:3], in0=subg[:], in1=cst[:, 1:2], op=ALU.mult)
            build_xT(P, C, ep, x, r_x, xT, r_xT)
            P.barrier()
            P.flush()

        def fin_simple(ob, db, ci, j):
            rden, r_rden = fin[-2]
            on, r_on = fin[-1]
            P.op("dve", nc.vector.reciprocal, [C.rps[db]], [r_rden], out=rden[:], in_=C.ps[db][:])
            P.op("dve", TT, [C.rps[ob], r_rden], [r_on], out=on[:], in0=C.ps[ob][:], in1=rden[:], op=ALU.mult)
            P.op("dve", TT, [r_on, r_siluz], [r_ysT], out=ysT[:, ci, j * 512:(j + 1) * 512], in0=on[:], in1=siluz[:, ci % 4, j * 512:(j + 1) * 512], op=ALU.mult)

        with ExitStack() as ep:
            tq = TokQK(P, C, ep, tab, r_tab)
            pT = [P.sb(ep, "pT%d" % i, [128, 512], BF16) for i in range(4)]
            accs = [P.sb(ep, "dacc%d" % i, [128, 512], F32) for i in range(4)]
            fin = [P.sb(ep, "fin%d" % i, [128, 512], F32) for i in range(6)]
            gq, r_gq = P.sb(ep, "gq", [128, 128], F32)
            P.dma("sp", gq[:], D_["gq"], writes=[r_gq])
            qT, r_qT = P.sb(ep, "qT", [128, 4, TPC], BF16)
            qT2, r_qT2 = P.sb(ep, "qT2", [128, 4, TPC], BF16)
            P.op("pool", nc.gpsimd.memset, [], [r_qT], ap=qT[64:128, :, :], constant=0.0)
            P.op("pool", nc.gpsimd.memset, [], [r_qT2], ap=qT2[0:64, :, :], constant=0.0)
            ksb = big[:, 0:S]
            vsb = big[:, S:2 * S].rearrange("p (k d) -> p k d", k=64)
            r_kq = [P.res("kq") for _ in range(4)]
            r_vq = [P.res("vq") for _ in range(4)]

            def load_kv(kidx, vcol):
                for i in range(4):
                    P.dma("sp", ksb[:, i * 2048:(i + 1) * 2048], D_["kT_AC"][kidx, :, i * 2048:(i + 1) * 2048], reads=[r_kv], writes=[r_kq[i]])
                    P.dma("sp", vsb[:, i * 16:(i + 1) * 16, :],
                          D_["v_AC"][i * 2048:(i + 1) * 2048, vcol:vcol + 128].rearrange("(kc p) d -> p kc d", p=128), reads=[r_kv], writes=[r_vq[i]])

            wt, r_wt = ws.get(6)
            emit_silu_z(P, C, xT, r_xT, wt, r_wt, siluz, r_siluz)
            wt, r_wt = ws.get(0)
            for tt in range(NT):
                tq.run(xT, r_xT, wt, r_wt, tt, "A", 4, lambda j, tt=tt: [(qT[0:64, j, tt * 128:(tt + 1) * 128], r_qT, 0, 64),
                                                                        (qT2[64:128, j, tt * 128:(tt + 1) * 128], r_qT2, 64, 128)])
            tq.flush()
            ng = 0
            for h in range(4):
                load_kv(h, h * 128)
                groups = []
                for j in range(2):
                    for c in range(2):
                        g = Grp()
                        g.q = (qT if c == 0 else qT2)[:, h, j * 512:(j + 1) * 512]
                        g.r_q = (r_qT if c == 0 else r_qT2)
                        g.kfn = lambda kc: ksb[:, kc * 128:(kc + 1) * 128]
                        g.r_kfn = lambda kc: r_kq[kc // 16]
                        g.vfn = lambda kc: vsb[:, kc, :]
                        g.r_vfn = lambda kc: r_vq[kc // 16]
                        g.nk = 64
                        g.ob, g.db = (3, 4) if ng % 2 == 0 else (5, 6)
                        ng += 1

                        def fin_a(g=g, c=c, j=j, h=h):
                            rden, r_rden = fin[0]
                            P.op("dve", nc.vector.reciprocal, [C.rps[g.db]], [r_rden], out=rden[:], in_=C.ps[g.db][:])
                            if c == 0:
                                o1, r_o1 = fin[1]
                                P.op("dve", TT, [C.rps[g.ob], r_rden], [r_o1], out=o1[:], in0=C.ps[g.ob][:], in1=rden[:], op=ALU.mult)
                                return
                            o1, r_o1 = fin[1]
                            o2, r_o2 = fin[2]
                            dd, r_dd = fin[3]
                            sq, r_sq = fin[4]
                            rs, r_rs = fin[5]
                            P.op("dve", TT, [C.rps[g.ob], r_rden], [r_o2], out=o2[:], in0=C.ps[g.ob][:], in1=rden[:], op=ALU.mult)
                            P.op("dve", nc.vector.scalar_tensor_tensor, [r_o2, r_o1, r_sm], [r_dd], out=dd[:], in0=o2[:], scalar=sm[:, 1:2], in1=o1[:],
                                 op0=ALU.mult, op1=ALU.add)
                            P.op("dve", TT, [r_dd], [r_sq], out=sq[:], in0=dd[:], in1=dd[:], op=ALU.mult)
                            P.op("pe", nc.tensor.matmul, [r_sq, C.r_onesf], [C.rps[7]], out=C.ps[7][:], lhsT=C.onesf[:], rhs=sq[:], start=True, stop=True)
                            P.op("act", nc.scalar.activation, [C.rps[7], C.r_eps_rms], [r_rs], out=rs[:], in_=C.ps[7][:], func=AF.Sqrt, scale=1.0 / 128, bias=C.eps_rms[:])
                            P.op("dve", nc.vector.reciprocal, [r_rs], [r_rs], out=rs[:], in_=rs[:])
                            P.op("dve", nc.vector.scalar_tensor_tensor, [r_dd, r_rs, r_sm], [r_sq], out=sq[:], in0=dd[:], scalar=sm[:, 2:3], in1=rs[:],
                                 op0=ALU.mult, op1=ALU.mult)
                            P.op("dve", TT, [r_sq, r_siluz], [r_ysT], out=ysT[:, h, j * 512:(j + 1) * 512], in0=sq[:], in1=siluz[:, h, j * 512:(j + 1) * 512], op=ALU.mult)
                        g.fin = fin_a
                        groups.append(g)
                dense_attention(P, C, groups, pT, 0.125, accs)
            wt, r_wt = ws.get(8)
            emit_silu_z(P, C, xT, r_xT, wt, r_wt, siluz, r_siluz)
            wt, r_wt = ws.get(1)
            for tt in range(NT):
                tq.run(xT, r_xT, wt, r_wt, tt, "C", 4, lambda j, tt=tt: (qT[:, j, tt * 128:(tt + 1) * 128], r_qT), g_bc=gq, r_g=r_gq)
            tq.flush()
            for kv in range(2):
                load_kv(4 + kv, 512 + kv * 128)
                groups = []
                for gg in range(2):
                    hq = kv * 2 + gg
                    for j in range(2):
                        g = Grp()
                        g.q = qT[:, hq, j * 512:(j + 1) * 512]
                        g.r_q = r_qT
                        g.kfn = lambda kc: ksb[:, kc * 128:(kc + 1) * 128]
                        g.r_kfn = lambda kc: r_kq[kc // 16]
                        g.vfn = lambda kc: vsb[:, kc, :]
                        g.r_vfn = lambda kc: r_vq[kc // 16]
                        g.nk = 64
                        g.ob, g.db = (3, 4) if ng % 2 == 0 else (5, 6)
                        ng += 1
                        g.fin = (lambda g=g, hq=hq, j=j: fin_simple(g.ob, g.db, 8 + hq, j))
                        groups.append(g)
                dense_attention(P, C, groups, pT, 128 ** -0.5, accs)
            P.barrier()
            P.flush()

        sc128 = 128 ** -0.5
        with ExitStack() as ep:
            ksbs = [P.sb(ep, "ksbB%d" % i, [128, BW], BF16) for i in range(2)]
            vsbs = [P.sb(ep, "vsbB%d" % i, [128, 14, 128], BF16) for i in range(2)]
            bias = [P.sb(ep, "biasB%d" % i, [128, 7, 128], F32) for i in range(2)]
            tmp = [P.sb(ep, "tmpB%d" % i, [128, 7, 128], F32) for i in range(2)]
            pB = [P.sb(ep, "pB%d" % i, [128, 7, 128], BF16) for i in range(2)]
            fin = [P.sb(ep, "finB%d" % i, [128, 512], F32) for i in range(2)]
            qT, r_qT = P.sb(ep, "qTB", [128, 4, TPC], BF16)
            wt, r_wt = ws.get(7)
            emit_silu_z(P, C, xT, r_xT, wt, r_wt, siluz, r_siluz)
            wt, r_wt = ws.get(5)
            n = 0
            for j in range(4):
                for half in range(2):
                    bank = 4 + n % 2
                    n += 1
                    proj_feat(P, C, xT, r_xT, wt, r_wt, j, half, bank)
                    P.op("dve", nc.vector.tensor_copy, [C.rps[bank]], [r_qT], out=qT[:, j, half * 512:(half + 1) * 512], in_=C.ps[bank][:])
            itemsB = [(h, qt) for h in range(4) for qt in range(NT)]

            def b_stage1(i):
                h, qt = itemsB[i]
                ksb, r_ksb = ksbs[h % 2]
                vsb, r_vsb = vsbs[h % 2]
                if qt == 0:
                    P.dma("sp", ksb[:], D_["kT_Bw"][h], reads=[r_kv], writes=[r_ksb])
                    P.dma("sp", vsb[:], D_["v_Bw"][:, h * 128:(h + 1) * 128].rearrange("(ch p) d -> p ch d", p=128), reads=[r_kv], writes=[r_vsb])
                bt, r_bt = bias[i % 2]
                tp, r_tp = tmp[i % 2]
                pb, r_pb = pB[i % 2]
                P.dma("sp", bt[:], D_["biasB"][qt, h], reads=[r_kv], writes=[r_bt])
                sb0, sb1 = (0, 1) if i % 2 == 0 else (2, 7)
                qap = qT[:, h, qt * 128:(qt + 1) * 128]
                for kc in range(7):
                    bk = sb0 if kc < 4 else sb1
                    co = (kc % 4) * 128
                    P.op("pe", nc.tensor.matmul, [r_qT, r_ksb], [C.rps[bk]], inc=(kc in (3, 6)), out=C.ps[bk][:, co:co + 128],
                         lhsT=ksb[:, (qt + kc) * 128:(qt + kc + 1) * 128], rhs=qap, start=True, stop=True)
                P.op("dve", nc.vector.scalar_tensor_tensor, [C.rps[sb0], r_bt], [r_tp], out=tp[:, 0:4, :], in0=C.ps[sb0][:].rearrange("p (k q) -> p k q", k=4),
                     scalar=sc128, in1=bt[:, 0:4, :], op0=ALU.mult, op1=ALU.add)
                P.op("dve", nc.vector.scalar_tensor_tensor, [C.rps[sb1], r_bt], [r_tp], out=tp[:, 4:7, :], in0=C.ps[sb1][:, 0:384].rearrange("p (k q) -> p k q", k=3),
                     scalar=sc128, in1=bt[:, 4:7, :], op0=ALU.mult, op1=ALU.add)
                P.op("act", nc.scalar.activation, [r_tp], [r_pb], out=pb[:], in_=tp[:], func=AF.Exp)

            def b_stage2(i):
                h, qt = itemsB[i]
                vsb, r_vsb = vsbs[h % 2]
                pb, r_pb = pB[i % 2]
                jq = qt // 4
                ob, db = (3, 4) if (h * 2 + jq) % 2 == 0 else (5, 6)
                co = (qt % 4) * 128
                for kc in range(7):
                    P.op("pe", nc.tensor.matmul, [r_pb, r_vsb], [C.rps[ob]], inc=False, out=C.ps[ob][:, co:co + 128], lhsT=vsb[:, qt + kc, :], rhs=pb[:, kc, :],
                         start=(kc == 0), stop=(kc == 6))
                    P.op("pe", nc.tensor.matmul, [r_pb, C.r_onesb], [C.rps[db]], inc=(kc == 6), out=C.ps[db][:, co:co + 128], lhsT=C.onesb[:], rhs=pb[:, kc, :],
                         start=(kc == 0), stop=(kc == 6))
                if qt % 4 == 3:
                    fin_simple(ob, db, 4 + h, jq)

            b_stage1(0)
            for i in range(len(itemsB)):
                if i + 1 < len(itemsB):
                    b_stage1(i + 1)
                b_stage2(i)
            P.barrier()
            P.flush()

        with ExitStack() as ep:
            tq = TokQK(P, C, ep, tab, r_tab, need_norm=False)
            qTD = big[:, 0:12 * TPC].rearrange("p (h t) -> p h t", h=12)
            r_qTD = r_big
            ksbs = [P.sb(ep, "ksbD%d" % i, [128, DW], BF16) for i in range(2)]
            vsbs = [P.sb(ep, "vsbD%d" % i, [128, 32, 128], BF16) for i in range(2)]
            mk, r_mk = P.sb(ep, "maskD", [128, NMASK, 128], F32)
            oacc, r_oacc = P.sb(ep, "oacc", [128, TPC], F32)
            dacc, r_dacc = P.sb(ep, "dacc", [128, TPC], F32)
            tmp = [P.sb(ep, "tmpD%d" % i, [128, 2, 128], F32) for i in range(3)]
            pD = [P.sb(ep, "pD%d" % i, [128, 2, 128], BF16) for i in range(3)]
            P.dma("sp", mk[:], D_["maskD"], writes=[r_mk])
            wt, r_wt = ws.get(9)
            emit_silu_z(P, C, xT, r_xT, wt, r_wt, siluz, r_siluz)
            for g in range(3):
                wt, r_wt = ws.get(2 + g)
                for tt in range(NT):
                    tq.run(xT, r_xT, wt, r_wt, tt, "D", 4, lambda j, tt=tt, g=g: (qTD[:, g * 4 + j, tt * 128:(tt + 1) * 128], r_qTD))
            tq.flush()
            mbase = [0, 16, 20]
            blocks = [(hh, g) for hh in range(4) for g in range(3)]

            def d_load(bi):
                hh, g = blocks[bi]
                ksb, r_ksb = ksbs[bi % 2]
                vsb, r_vsb = vsbs[bi % 2]
                r = DIL_RATES[g]
                M = {1: 9, 4: 3, 16: 2}[r]
                W = TPC + 128 * r
                P.dma("sp", ksb[:, 0:W], D_["kT_Dw"][g * 4 + hh, :, 1024 - 64 * r:1024 - 64 * r + W], reads=[r_kv], writes=[r_ksb])
                c0 = g * 512 + hh * 128
                vsrc = D_["v_Dw"]
                if r == 1:
                    P.dma("sp", vsb[:, 0:9, :], vsrc[960:960 + 1152, c0:c0 + 128].rearrange("(m a) d -> a m d", a=128), reads=[r_kv], writes=[r_vsb])
                elif r == 4:
                    for m in range(3):
                        P.dma("sp", vsb[:, ss(m, 4, 3), :], vsrc[768 + 512 * m:768 + 512 * m + 512, c0:c0 + 128].rearrange("(a r) d -> a r d", r=4),
                              reads=[r_kv], writes=[r_vsb])
                else:
                    for r0 in (0, 8):
                        P.dma("sp", vsb[:, ss(2 * r0, 8, 2), :], vsrc[0:2048, c0:c0 + 128].rearrange("(a r) d -> a r d", r=16)[:, r0:r0 + 8, :],
                              reads=[r_kv], writes=[r_vsb])
                    P.dma("sp", vsb[0:64, ss(1, 16, 2), :], vsrc[2048:3072, c0:c0 + 128].rearrange("(a r) d -> a r d", r=16), reads=[r_kv], writes=[r_vsb])

            itc = [0]

            def d_compute(bi):
                hh, g = blocks[bi]
                ksb, r_ksb = ksbs[bi % 2]
                vsb, r_vsb = vsbs[bi % 2]
                r = DIL_RATES[g]
                n_ = TPC // r
                nq = min(128, n_)
                nqt = n_ // nq
                M = {1: 9, 4: 3, 16: 2}[r]
                ob0, db0 = 3, 5
                its = [(rho, qti) for rho in range(r) for qti in range(nqt)]

                def s1(k):
                    rho, qti = its[k]
                    U = qti * nq
                    idx = itc[0] + k
                    sbk = idx % 3
                    tp, r_tp = tmp[idx % 3]
                    pd, r_pd = pD[idx % 3]
                    qap = qTD[:, g * 4 + hh, ss(rho + r * U, nq, r)]
                    for ch in range(2):
                        u0 = U - 64 + 128 * ch
                        nk = 128 if (ch == 0 or nq == 128) else 64
                        kap = ksb[:, ss(r * (u0 + 64) + rho, nk, r)]
                        P.op("pe", nc.tensor.matmul, [r_qTD, r_ksb], [C.rps[sbk]], inc=(ch == 1), out=C.ps[sbk][0:nk, ch * 128:ch * 128 + nq],
                             lhsT=kap, rhs=qap, start=True, stop=True)
                    mi = mbase[g] + qti * 2
                    P.op("dve", nc.vector.scalar_tensor_tensor, [C.rps[sbk], r_mk], [r_tp], out=tp[:, :, 0:nq],
                         in0=C.ps[sbk][:, 0:256].rearrange("p (c q) -> p c q", c=2)[:, :, 0:nq],
                         scalar=sc128, in1=mk[:, mi:mi + 2, 0:nq], op0=ALU.mult, op1=ALU.add)
                    P.op("act", nc.scalar.activation, [r_tp], [r_pd], out=pd[:, :, 0:nq], in_=tp[:, :, 0:nq], func=AF.Exp)

                def s2(k):
                    rho, qti = its[k]
                    U = qti * nq
                    idx = itc[0] + k
                    pd, r_pd = pD[idx % 3]
                    col0 = rho * n_ + U
                    ob = ob0 + col0 // 512
                    db = db0 + col0 // 512
                    cofs = col0 % 512
                    for ch in range(2):
                        nk = 128 if (ch == 0 or nq == 128) else 64
                        vidx = rho * M + (qti + ch)
                        P.op("pe", nc.tensor.matmul, [r_pd, r_vsb], [C.rps[ob]], inc=False, out=C.ps[ob][:, cofs:cofs + nq], lhsT=vsb[0:nk, vidx, :],
                             rhs=pd[0:nk, ch, 0:nq], start=(ch == 0), stop=(ch == 1))
                        P.op("pe", nc.tensor.matmul, [r_pd, C.r_onesb], [C.rps[db]], inc=(ch == 1), out=C.ps[db][:, cofs:cofs + nq], lhsT=C.onesb[0:nk, :],
                             rhs=pd[0:nk, ch, 0:nq], start=(ch == 0), stop=(ch == 1))

                nI = len(its)
                s1(0)
                if nI > 1:
                    s1(1)
                for k in range(nI):
                    s2(k)
                    if k + 2 < nI:
                        s1(k + 2)
                itc[0] += nI
                for (acc, r_acc, b0) in ((oacc, r_oacc, ob0), (dacc, r_dacc, db0)):
                    for b in range(2):
                        if r == 1:
                            dst = acc[:, b * 512:(b + 1) * 512]
                            src = C.ps[b0 + b][:]
                        else:
                            dst = acc[:].rearrange("p (u r) -> p r u", r=r)[:, b * r // 2:(b + 1) * r // 2, :]
                            src = C.ps[b0 + b][:].rearrange("p (r u) -> p r u", u=n_)
                        if g == 0:
                            P.op("dve", nc.vector.tensor_copy, [C.rps[b0 + b]], [r_acc], out=dst, in_=src)
                        else:
                            P.op("dve", TT, [C.rps[b0 + b], r_acc], [r_acc], out=dst, in0=dst, in1=src, op=ALU.add)
                if g == 2:
                    P.op("dve", nc.vector.reciprocal, [r_dacc], [r_dacc], out=dacc[:], in_=dacc[:])
                    P.op("dve", TT, [r_oacc, r_dacc], [r_oacc], out=oacc[:], in0=oacc[:], in1=dacc[:], op=ALU.mult)
                    P.op("dve", TT, [r_oacc, r_siluz], [r_ysT], out=ysT[:, 12 + hh, :], in0=oacc[:], in1=siluz[:, hh, :], op=ALU.mult)

            d_load(0)
            for bi in range(len(blocks)):
                if bi + 1 < len(blocks):
                    d_load(bi + 1)
                d_compute(bi)
            P.barrier()
            P.flush()

        wo0, r_wo0 = P.sb(em, "wo0", [128, KC, 512], BF16)
        r_wod = P.res("wod")
        load_wtile(P, wo0, r_wo0, D_["wo"], r_wod, 0)
        with ExitStack() as ep:
            wbt = [P.sb(ep, "wbt%d" % i, [128, 16, 128], BF16) for i in range(2)]
            gs = [P.sb(ep, "gs%d" % i, [128, 512], F32) for i in range(3)]
            macc = [P.sb(ep, "macc%d" % i, [128, 512], F32) for i in range(2)]
            mt = [P.sb(ep, "mt%d" % i, [128, 512], F32) for i in range(2)]
            r_wbr = P.res("wbr")
            P.dma("pool", wbt[0][0][:], D_["wbr"][0], reads=[r_wbr], writes=[wbt[0][1]])
            it = 0
            for dc in range(KC):
                wt, r_wt = ws.get(10 + dc)
                wb, r_wb = wbt[dc % 2]
                if dc + 1 < KC:
                    P.dma("pool", wbt[(dc + 1) % 2][0][:], D_["wbr"][dc + 1], reads=[r_wbr], writes=[wbt[(dc + 1) % 2][1]])
                for n in range(4):
                    for half in range(2):
                        gb = it % 4
                        pb = 4 + it % 4
                        g_, r_g_ = gs[it % 3]
                        it += 1
                        for kc in range(KC):
                            P.op("pe", nc.tensor.matmul, [r_xT, r_wt], [C.rps[gb]], inc=(kc == KC - 1), out=C.ps[gb][:], lhsT=wt[:, kc, n * 128:(n + 1) * 128],
                                 rhs=xT[:, kc, half * 512:(half + 1) * 512], start=(kc == 0), stop=(kc == KC - 1))
                        P.op("act", nc.scalar.activation, [C.rps[gb], r_bgT], [r_g_], out=g_[:], in_=C.ps[gb][:], func=AF.Sigmoid, bias=bgT[:, n * 16 + dc:n * 16 + dc + 1], scale=1.0)
                        for wc in range(4):
                            P.op("pe", nc.tensor.matmul, [r_ysT, r_wb], [C.rps[pb]], inc=(wc == 3), out=C.ps[pb][:], lhsT=wb[:, n * 4 + wc, :],
                                 rhs=ysT[:, n * 4 + wc, half * 512:(half + 1) * 512], start=(wc == 0), stop=(wc == 3))
                        ma, r_ma = macc[half]
                        if n == 0:
                            P.op("dve", TT, [C.rps[pb], r_g_], [r_ma], out=ma[:], in0=C.ps[pb][:], in1=g_[:], op=ALU.mult)
                        else:
                            t_, r_t_ = mt[half]
                            P.op("dve", TT, [C.rps[pb], r_g_], [r_t_], out=t_[:], in0=C.ps[pb][:], in1=g_[:], op=ALU.mult)
                            if n < 3:
                                P.op("dve", TT, [r_ma, r_t_], [r_ma], out=ma[:], in0=ma[:], in1=t_[:], op=ALU.add)
                            else:
                                P.op("dve", TT, [r_ma, r_t_], [r_mergedT], out=mergedT[:, dc, half * 512:(half + 1) * 512], in0=ma[:], in1=t_[:], op=ALU.add)
            P.barrier()
            P.flush()

        with ExitStack() as ep:
            r_wx1, r_wy0, r_wy1 = P.res("wx1"), P.res("wy0"), P.res("wy1")
            wos = [(wo0, r_wo0), (xT[:, :, 512:1024], r_wx1), (ysT[:, :, 0:512], r_wy0), (ysT[:, :, 512:1024], r_wy1)]
            for cc in range(1, 4):
                load_wtile(P, wos[cc][0], wos[cc][1], D_["wo"], r_wod, cc)
            gb, r_gb = P.sb(ep, "lng", [128, D], F32)
            bb, _ = P.sb(ep, "lnb", [128, D], F32)
            P.dma("sp", gb[:], D_["lng"], writes=[r_gb])
            P.dma("sp", bb[:], D_["lnb"], writes=[r_gb])
            xts = [P.sb(ep, "xt%d" % i, [128, D], F32) for i in range(2)]
            stats, r_stats = P.sb(ep, "stats", [128, 24], F32)
            mv, r_mv = P.sb(ep, "mv", [128, 4], F32)
            for tt in range(NT):
                xt, r_xt = xts[tt % 2]
                P.dma("sp", xt[:], x[tt * 128:(tt + 1) * 128, :], reads=[r_x], writes=[r_xt])
                for cc in range(4):
                    bank = (tt % 2) * 4 + cc
                    wo_, r_wo_ = wos[cc]
                    for dc in range(KC):
                        P.op("pe", nc.tensor.matmul, [r_mergedT, r_wo_], [C.rps[bank]], inc=(dc == KC - 1), out=C.ps[bank][:], lhsT=mergedT[:, dc, tt * 128:(tt + 1) * 128],
                             rhs=wo_[:, dc, :], start=(dc == 0), stop=(dc == KC - 1))
                    P.op("dve", nc.vector.scalar_tensor_tensor, [r_xt, C.rps[bank]], [r_xt], out=xt[:, cc * 512:(cc + 1) * 512], in0=xt[:, cc * 512:(cc + 1) * 512],
                         scalar=float(DN_ALPHA), in1=C.ps[bank][:], op0=ALU.mult, op1=ALU.add)
                layer_norm_tile(P, C, xt, r_xt, gb, bb, r_gb, stats, r_stats, mv, r_mv)
                P.dma("sp", D_["xo"][tt * 128:(tt + 1) * 128, :], xt[:], reads=[r_xt], writes=[r_xo], semres=r_xt)
            P.barrier()
            P.flush()


def build_T():
    nc = bass.Bass("TRN2", target_bir_lowering=False)

    def din(name, shape, dt=F32):
        return nc.dram_tensor(name, shape, dt, kind="ExternalInput").ap()
    D_ = {
        "x": din("x", [TPC, D]), "wT": din("wT", [26, 128, KC, 512]), "tab": din("tab", [128, NT, NTAB]), "gq": din("gq", [128, 128]),
        "lp": din("lp", [128, 256]), "cst": din("cst", [128, 4]), "subg": din("subg", [128, 1]), "bgT": din("bgT", [128, 64]),
        "wbr": din("wbr", [KC, 128, 16, 128]), "wo": din("wo", [4, 128, KC, 512]), "lng": din("lng", [128, D]), "lnb": din("lnb", [128, D]),
        "kT_AC": din("kT_AC", [6, 128, S], BF16), "v_AC": din("v_AC", [S, 768], BF16),
        "kT_Bw": din("kT_Bw", [4, 128, BW], BF16), "v_Bw": din("v_Bw", [BW, 512], BF16), "biasB": din("biasB", [NT, 4, 128, 7, 128]),
        "kT_Dw": din("kT_Dw", [12, 128, DW], BF16), "v_Dw": din("v_Dw", [DW, 1536], BF16), "maskD": din("maskD", [128, NMASK, 128]),
    }
    D_["xo"] = nc.dram_tensor("xo", [TPC, D], F32, kind="ExternalOutput").ap()
    with ExitStack() as es:
        P = Prog(nc, es)
        C = setup_common(P, es)
        emit_T(P, C, es, D_)
    return nc


O_AQ, O_AK, O_AV, O_AZ = 0, 512, 1024, 1536
O_BQ, O_BK, O_BV, O_BZ = 2048, 2560, 3072, 3584
O_CQ, O_CK, O_CV, O_CZ = 4096, 4608, 4864, 5120
O_DQ, O_DK, O_DV, O_DZ = 5632, 7168, 8704, 10240
O_GL = 10752


def tileize(wcols):
    n = wcols.shape[1] // 512
    return np.ascontiguousarray(wcols.reshape(KC, 128, n, 512).transpose(2, 1, 0, 3))


def rep128(vec):
    return np.ascontiguousarray(np.broadcast_to(np.asarray(vec, np.float32).reshape(1, -1), (128, vec.size)))


def make_tabs():
    t = np.arange(S, dtype=np.int64)

    def cs(pos, half, dim, theta):
        inv = np.power(np.float32(theta), -np.arange(half, dtype=np.float32) * np.float32(2.0) / np.float32(dim)).astype(np.float32)
        ang = pos.astype(np.float32)[:, None] * inv[None, :]
        return np.cos(ang).astype(np.float32), np.sin(ang).astype(np.float32)
    cA, sA = cs(t, 8, 16, 500000.0)
    cD, sD = cs(t, 16, 32, 500000.0)
    cR, sR = cs(t // 64, 32, 64, 10000.0)
    cC, sC = cs(t % 64, 32, 64, 10000.0)
    full = np.concatenate([cA, sA, cD, sD, cR, sR, cC, sC], axis=1)
    out = []
    for c in range(NCORE):
        blk = full[c * TPC:(c + 1) * TPC].reshape(NT, 128, NTAB).transpose(1, 0, 2)
        out.append(np.ascontiguousarray(blk))
    return out


def wP_of(w):
    cols = np.concatenate([w[:, O_AK:O_AK + 512], w[:, O_DK:O_DK + 1536], w[:, O_BK:O_BK + 512],
                           w[:, O_CK:O_CK + 256], w[:, O_CV:O_CV + 256],
                           w[:, O_AV:O_AV + 512], w[:, O_BV:O_BV + 512], w[:, O_DV:O_DV + 1536]], axis=1)
    return tileize(cols)


_NC_CACHE = {}


def get_nc(name):
    if name not in _NC_CACHE:
        _NC_CACHE[name] = {"E": build_E, "P": build_P, "T": build_T}[name]()
    return _NC_CACHE[name]


def run_E(x2d, g, b):
    nc = get_nc("E")
    gb, bb = rep128(g), rep128(b)
    maps = [{"x": np.ascontiguousarray(x2d[c * TPC:(c + 1) * TPC]), "g": gb, "b": bb} for c in range(NCORE)]
    res = run_bass_kernel_spmd(nc, maps, core_ids=list(range(NCORE)))
    return np.concatenate([r["xo"] for r in res.results], axis=0)


def run_P(xcur, w, kn_g, tabs):
    nc = get_nc("P")
    wP = wP_of(w)
    gk = rep128(kn_g)
    maps = [{"x": np.ascontiguousarray(xcur[c * TPC:(c + 1) * TPC]), "wP": wP, "tab": tabs[c], "gk": gk} for c in range(NCORE)]
    res = run_bass_kernel_spmd(nc, maps, core_ids=list(range(NCORE)))
    kT = np.concatenate([r["kT"] for r in res.results], axis=2)
    v = np.concatenate([r["v"] for r in res.results], axis=0)
    return kT, v


def wT_of(w):
    gl = w[:, O_GL:O_GL + 8192].reshape(D, 4, KC, 128).transpose(0, 2, 1, 3).reshape(D, 8192)
    cols = np.concatenate([w[:, O_AQ:O_AQ + 512], w[:, O_CQ:O_CQ + 512], w[:, O_DQ:O_DQ + 1536], w[:, O_BQ:O_BQ + 512],
                           w[:, O_AZ:O_AZ + 512], w[:, O_BZ:O_BZ + 512], w[:, O_CZ:O_CZ + 512], w[:, O_DZ:O_DZ + 512], gl], axis=1)
    return tileize(cols)


def window(arr, axis, start, length):
    n = arr.shape[axis]
    lo, hi = max(start, 0), min(start + length, n)
    shp = list(arr.shape)
    shp[axis] = length
    out = np.zeros(shp, arr.dtype)
    sl_src = [slice(None)] * arr.ndim
    sl_dst = [slice(None)] * arr.ndim
    sl_src[axis] = slice(lo, hi)
    sl_dst[axis] = slice(lo - start, hi - start)
    out[tuple(sl_dst)] = arr[tuple(sl_src)]
    return out


def make_biasB_index():
    out = []
    for c in range(NCORE):
        qt = np.arange(NT)[:, None, None, None]
        a = np.arange(128)[None, :, None, None]
        kc = np.arange(7)[None, None, :, None]
        b = np.arange(128)[None, None, None, :]
        tk = TPC * c - 384 + 128 * (qt + kc) + a
        tq = TPC * c + 128 * qt + b
        tk, tq = np.broadcast_arrays(tk, tq)
        inr = (tk >= 0) & (tk < S)
        kr, kcol = tk // 64, tk % 64
        qr, qc = tq // 64, tq % 64
        r0 = np.clip(qr - 4, 0, 120)
        c0 = np.clip(qc - 8, 0, 48)
        valid = inr & (kr >= r0) & (kr < r0 + 8) & (kcol >= c0) & (kcol < c0 + 16)
        dr = np.clip(kr - qr + 7, 0, 14)
        dc = np.clip(kcol - qc + 15, 0, 30)
        out.append((np.where(valid, dr * 31 + dc, 0).astype(np.int64), valid))
    return out


def make_biasB(rpb_l, bidx):
    idx, valid = bidx
    flat = rpb_l.reshape(4, 15 * 31)
    g = flat[:, idx]
    g = np.where(valid[None], g, np.float32(NEG)).astype(np.float32)
    return np.ascontiguousarray(g.transpose(1, 0, 2, 3, 4))


def make_maskD():
    out = []
    for c in range(NCORE):
        m = np.full((128, NMASK, 128), NEG, np.float32)
        a = np.arange(128)[:, None]
        b = np.arange(128)[None, :]
        base = [0, 16, 20]
        for g, r in enumerate(DIL_RATES):
            n = TPC // r
            nq = min(128, n)
            for qti in range(n // nq):
                U = qti * nq
                for ch in range(2):
                    u = U - 64 + 128 * ch + a
                    uq = U + b
                    ug = n * c + u
                    valid = (np.abs(u - uq) <= 64) & (ug >= 0) & (ug < S // r)
                    m[:, base[g] + qti * 2 + ch, :] = np.where(valid, 0.0, NEG)
        out.append(m)
    return out


def run_T(xcur, l, inp, kT, v, tabs, bidx, maskD):
    nc = get_nc("T")
    w = inp["w_in"][l]
    lam_init = 0.8 - 0.6 * math.exp(-0.3 * l)
    common = {
        "wT": wT_of(w), "gq": rep128(inp["gqa_q_norm_g"][l]), "lp": rep128(inp["diff_lambda"][l].reshape(-1)),
        "cst": rep128(np.array([lam_init, 1.0 - lam_init, 0.0, 0.0], np.float32)),
        "subg": np.ascontiguousarray(inp["diff_subln_g"][l].reshape(128, 1)),
        "bgT": np.ascontiguousarray(inp["b_gate"][l].reshape(4, KC, 128).transpose(2, 0, 1).reshape(128, 64)),
        "wbr": np.ascontiguousarray(inp["w_branch"][l].reshape(4, 4, 128, KC, 128).transpose(3, 2, 0, 1, 4).reshape(KC, 128, 16, 128)),
        "wo": tileize(inp["w_out"][l]), "lng": rep128(inp["ln_g"][l]), "lnb": rep128(inp["ln_b"][l]),
        "kT_AC": np.ascontiguousarray(kT[0:6]), "v_AC": np.ascontiguousarray(np.concatenate([v[:, 0:512], v[:, 1024:1280]], axis=1)),
    }
    kT_B, v_B = kT[18:22], v[:, 512:1024]
    kT_D, v_D = kT[6:18], v[:, 1280:2816]
    maps = []
    for c in range(NCORE):
        m = dict(common)
        m["x"] = np.ascontiguousarray(xcur[c * TPC:(c + 1) * TPC])
        m["tab"] = tabs[c]
        m["kT_Bw"] = window(kT_B, 2, TPC * c - 384, BW)
        m["v_Bw"] = window(v_B, 0, TPC * c - 384, BW)
        m["biasB"] = make_biasB(inp["nat_rpb"][l], bidx[c])
        m["kT_Dw"] = window(kT_D, 2, TPC * c - 1024, DW)
        m["v_Dw"] = window(v_D, 0, TPC * c - 1024, DW)
        m["maskD"] = maskD[c]
        maps.append(m)
    res = run_bass_kernel_spmd(nc, maps, core_ids=list(range(NCORE)))
    return np.concatenate([r["xo"] for r in res.results], axis=0)


def kernel(**inputs):
    inp = {k: np.asarray(v) for k, v in inputs.items()}
    x = np.ascontiguousarray(inp["x"].reshape(S, D).astype(np.float32, copy=False))
    tabs = make_tabs()
    bidx = make_biasB_index()
    maskD = make_maskD()
    xcur = run_E(x, inp["emb_ln_g"], inp["emb_ln_b"])
    for l in range(DEPTH):
        kT, v = run_P(xcur, inp["w_in"][l], inp["gqa_k_norm_g"][l], tabs)
        xcur = run_T(xcur, l, inp, kT, v, tabs, bidx, maskD)
    return xcur.reshape(1, S, D).astype(np.float32)
```
